# Optimizing a Trainium2 kernel written in Bass

```python
import math
import jax, jax.numpy as jnp
from jax import lax
import numpy as np

D_MODEL = 2048
BATCH = 16
SEQ = 256
DEPTH = 2
DEC_BATCH = 2
DEC_SEQ = 2048
PAST_LEN = 512

GRID_W = 64
POOL_GROUPS = 4
POOL_GC = 128
POOL_W = POOL_GROUPS * POOL_GC
POOL_WINDOWS = (2, 4, 8, 16)
GDN_HEADS = 8
GDN_HEAD_DIM = 128
GDN_W = GDN_HEADS * GDN_HEAD_DIM
SHORT_CONV = 3
GDN_CHUNK = 64
NA_HEADS = 8
NA_HEAD_DIM = 64
NA_W = NA_HEADS * NA_HEAD_DIM
NA_WR = 8
NA_WC = 16
Q_BLOCK = 128
D_FF = 5632
FFN_CONV = 3
N_IN = POOL_W + 4 * GDN_W + 4 * GDN_HEADS + 3 * NA_W + 3 * D_MODEL
EPS = 1e-6

kernel_name = 'hybrid_pool_gdn_natten_prefix_dit_step'

f32 = jnp.float32


def rmsnorm(x, g):
    xf = x.astype(f32)
    y = xf * lax.rsqrt(jnp.mean(xf * xf, axis=-1, keepdims=True) + EPS)
    return (y * g.astype(f32)).astype(x.dtype)


def l2norm(x):
    return x * lax.rsqrt(jnp.sum(x * x, axis=-1, keepdims=True) + EPS)


def dwconv_centred(x, w):
    k, ch = w.shape
    return lax.conv_general_dilated(x, w[:, None, :].astype(x.dtype), (1,), [(k // 2, k // 2)],
                                    dimension_numbers=('NWC', 'WIO', 'NWC'), feature_group_count=ch)


def modulation(cvec, w, b):
    m = (jax.nn.silu(cvec) @ w + b).reshape(-1, 1, 6 * D_MODEL)
    return jnp.split(m, 6, axis=-1)


def multiscale_pool(u, pool_w, pool_scale):
    B, T, _ = u.shape
    ug = u.astype(f32).reshape(B, T, POOL_GROUPS, POOL_GC)
    csum = jnp.concatenate([jnp.zeros((B, 1, POOL_GROUPS, POOL_GC), f32), jnp.cumsum(ug, axis=1)], axis=1)
    t = np.arange(T)[:, None]
    win = np.array(POOL_WINDOWS)[None, :]
    lo = np.maximum(t - win // 2, 0)
    hi = np.minimum(t + win - 1 - win // 2, T - 1)
    gidx = np.arange(POOL_GROUPS)[None, :]
    wsum = csum[:, hi + 1, gidx] - csum[:, lo, gidx]
    mean = wsum / (hi - lo + 1).astype(np.float32)[None, :, :, None]
    y = jnp.einsum('btgc,gcd->btgd', mean - ug, pool_w.astype(f32)) * pool_scale.astype(f32).reshape(POOL_GROUPS, POOL_GC)
    return y.reshape(B, T, POOL_W).astype(u.dtype)


def chunk_gated_delta(q, k, v, g, beta, s0):
    B, H, T, Dh = q.shape
    C = GDN_CHUNK
    N = T // C
    q, k, v = (a.reshape(B, H, N, C, Dh) for a in (q, k, v))
    g = jnp.cumsum(g.reshape(B, H, N, C), axis=-1)
    beta = beta.reshape(B, H, N, C)
    kb = k * beta[..., None]
    incl = np.tril(np.ones((C, C), bool))
    strict = np.tril(np.ones((C, C), bool), -1)
    decay = jnp.exp(jnp.where(incl, g[..., :, None] - g[..., None, :], -jnp.inf))
    lmat = jnp.where(strict, jnp.einsum('bhnid,bhnjd->bhnij', kb, k) * decay, 0.0)
    eye = jnp.broadcast_to(jnp.eye(C, dtype=f32), lmat.shape)
    tinv = lax.linalg.triangular_solve(lmat + eye, eye, left_side=True, lower=True, unit_diagonal=True)
    u = tinv @ (v * beta[..., None])
    w = tinv @ (kb * jnp.exp(g)[..., None])
    qk = jnp.einsum('bhnid,bhnjd->bhnij', q, k) * decay
    qg = q * jnp.exp(g)[..., None]
    kd = k * jnp.exp(g[..., -1:] - g)[..., None]
    glast = jnp.exp(g[..., -1])
    xs = tuple(jnp.moveaxis(a, 2, 0) for a in (u, w, qk, qg, kd, glast))

    def step(S, inp):
        u_n, w_n, qk_n, qg_n, kd_n, gl_n = inp
        v_new = u_n - w_n @ S
        o = qg_n @ S + qk_n @ v_new
        S = S * gl_n[..., None, None] + jnp.swapaxes(kd_n, -1, -2) @ v_new
        return S, o

    S, o = lax.scan(step, s0, xs)
    return jnp.moveaxis(o, 0, 2).reshape(B, H, T, Dh), S


def gdn_branch(u_qkv, u_z, u_beta, u_a, conv_w, a_log, dt_bias, norm_g, s0):
    B, T, _ = u_qkv.shape
    H, Dh = GDN_HEADS, GDN_HEAD_DIM
    qkv = jax.nn.silu(dwconv_centred(u_qkv, conv_w)).astype(f32)
    to_bhtd = lambda a: a.reshape(B, T, H, Dh).transpose(0, 2, 1, 3)
    q, k, v = (to_bhtd(a) for a in jnp.split(qkv, 3, axis=-1))
    q = l2norm(q) * (Dh ** -0.5)
    k = l2norm(k)
    beta = jax.nn.sigmoid(u_beta.astype(f32)).reshape(B, T, 2, H).transpose(2, 0, 3, 1)
    a_in = u_a.astype(f32).reshape(B, T, 2, H).transpose(2, 0, 3, 1)
    g = -jnp.exp(a_log.astype(f32))[:, None, :, None] * jax.nn.softplus(a_in + dt_bias.astype(f32)[:, None, :, None])
    o_f, s_f = chunk_gated_delta(q, k, v, g[0], beta[0], s0[:, 0])
    flip = lambda a: jnp.flip(a, axis=2)
    o_b, s_b = chunk_gated_delta(flip(q), flip(k), flip(v), flip(g[1]), flip(beta[1]), s0[:, 1])
    o = (o_f + flip(o_b)).transpose(0, 2, 1, 3)
    o = rmsnorm(o, norm_g) * jax.nn.silu(u_z.astype(f32).reshape(B, T, H, Dh))
    return o.reshape(B, T, GDN_W).astype(u_qkv.dtype), jnp.stack([s_f, s_b], axis=1)


def attend_context(q, k, v):
    B, S, H, Dh = q.shape
    qb = q.reshape(B, S // Q_BLOCK, Q_BLOCK, H, Dh).swapaxes(0, 1)

    def block(qi):
        s = jnp.einsum('bqhd,bkhd->bhqk', qi, k).astype(f32) * (Dh ** -0.5)
        p = jax.nn.softmax(s, axis=-1).astype(v.dtype)
        return jnp.einsum('bhqk,bkhd->bqhd', p, v)

    o = lax.map(block, qb)
    return o.swapaxes(0, 1).reshape(B, S, H * Dh)


def na_latent(q, k, v, k_ctx, v_ctx, rpb):
    B, T, H, Dh = q.shape
    rows = T // GRID_W
    wr = min(NA_WR, rows)
    r = np.arange(rows)
    key_rows = np.clip(r - wr // 2, 0, rows - wr)[:, None] + np.arange(wr)[None, :]
    col = np.arange(GRID_W)
    col_start = np.clip(col - NA_WC // 2, 0, GRID_W - NA_WC)
    col_mask = (col[None, :] >= col_start[:, None]) & (col[None, :] < col_start[:, None] + NA_WC)
    d_row = key_rows - r[:, None] + NA_WR - 1
    d_col = np.clip(col[None, :] - col[:, None], -(NA_WC - 1), NA_WC - 1) + NA_WC - 1
    bias = jnp.take(rpb[:, d_row], d_col, axis=-1)
    bias = bias.transpose(0, 1, 3, 2, 4).astype(f32)
    qg = q.reshape(B, rows, GRID_W, H, Dh)
    kb = k.reshape(B, rows, GRID_W, H, Dh)[:, key_rows]
    vb = v.reshape(B, rows, GRID_W, H, Dh)[:, key_rows]
    scale = Dh ** -0.5
    s_nb = jnp.einsum('brqhd,brikhd->bhrqik', qg, kb).astype(f32) * scale + bias[None]
    s_nb = jnp.where(col_mask[None, None, None, :, None, :], s_nb, -jnp.inf)
    s_ctx = jnp.einsum('brqhd,bshd->bhrqs', qg, k_ctx).astype(f32) * scale
    n_nb = wr * GRID_W
    p = jax.nn.softmax(jnp.concatenate([s_nb.reshape(B, H, rows, GRID_W, n_nb), s_ctx], axis=-1), axis=-1).astype(v.dtype)
    p_nb = p[..., :n_nb].reshape(B, H, rows, GRID_W, wr, GRID_W)
    o = jnp.einsum('bhrqik,brikhd->brqhd', p_nb, vb) + jnp.einsum('bhrqs,bshd->brqhd', p[..., n_nb:], v_ctx)
    return o.reshape(B, T, H * Dh)


def token_mixer(h, lp, ctx):
    B, T, _ = h.shape
    sizes = (POOL_W, 3 * GDN_W, GDN_W, 2 * GDN_HEADS, 2 * GDN_HEADS, 3 * NA_W)
    u_pool, u_qkv, u_z, u_beta, u_a, u_na, u_gate = jnp.split(h @ lp['w_in'], np.cumsum(sizes).tolist(), axis=-1)
    y_pool = multiscale_pool(u_pool, lp['pool_w'], lp['pool_scale'])
    if ctx is None:
        s0 = jnp.zeros((B, 2, GDN_HEADS, GDN_HEAD_DIM, GDN_HEAD_DIM), f32)
    else:
        s0 = ctx[2].astype(f32)
    y_gdn, s_fin = gdn_branch(u_qkv, u_z, u_beta, u_a, lp['gdn_conv'], lp['gdn_a_log'], lp['gdn_dt_bias'], lp['gdn_norm_g'], s0)
    q, k, v = (a.reshape(B, T, NA_HEADS, NA_HEAD_DIM) for a in jnp.split(u_na, 3, axis=-1))
    if ctx is None:
        y_na = attend_context(q, k, v)
    else:
        y_na = na_latent(q, k, v, ctx[0], ctx[1], lp['na_rpb'])
    gate = jax.nn.sigmoid(u_gate).reshape(B, T, 3, D_MODEL)
    merged = (gate[:, :, 0] * (y_pool @ lp['w_branch_pool'])
              + gate[:, :, 1] * (y_gdn @ lp['w_branch_gdn'])
              + gate[:, :, 2] * (y_na @ lp['w_branch_na']))
    return merged @ lp['w_out'], (k, v, s_fin)


def conv_ffn(h, w_up, conv_w, w_down):
    a, b = jnp.split(dwconv_centred(h @ w_up, conv_w), 2, axis=-1)
    return (jax.nn.silu(a) * b) @ w_down


def trunk_layer(x, mods, lp, ctx):
    sh1, sc1, g1, sh2, sc2, g2 = mods
    h = rmsnorm(x, lp['g_norm1']) * (1 + sc1) + sh1
    mix, ctx_out = token_mixer(h, lp, ctx)
    x = x + g1 * mix
    h = rmsnorm(x, lp['g_norm2']) * (1 + sc2) + sh2
    x = x + g2 * conv_ffn(h, lp['w_up'], lp['ffn_conv'], lp['w_down'])
    return x, ctx_out


def setup_inputs(seed: int = 0) -> dict:
    key = jax.random.key(seed)
    ks = jax.random.split(key, 32)
    nrm = lambda kk, shape, s: jax.random.normal(kk, shape, f32) * s
    L, D = DEPTH, D_MODEL
    dt = jnp.exp(jax.random.uniform(ks[15], (L, 2, GDN_HEADS), f32, math.log(1e-3), math.log(1e-1)))
    return {
        'x_prompt': nrm(ks[0], (BATCH, SEQ, D), 1.0),
        'x_sample': nrm(ks[1], (DEC_BATCH, DEC_SEQ, D), 1.0),
        'cache_na_k': nrm(ks[2], (DEC_BATCH, L, PAST_LEN, NA_HEADS, NA_HEAD_DIM), 1.0),
        'cache_na_v': nrm(ks[3], (DEC_BATCH, L, PAST_LEN, NA_HEADS, NA_HEAD_DIM), 1.0),
        'state_gdn': nrm(ks[4], (DEC_BATCH, L, 2, GDN_HEADS, GDN_HEAD_DIM, GDN_HEAD_DIM), 0.1),
        'c': nrm(ks[5], (DEC_BATCH, D), 1.0),
        'c_ctx': nrm(ks[6], (D,), 1.0),
        'w_ada': nrm(ks[7], (L, D, 6 * D), 0.5 * D ** -0.5),
        'b_ada': nrm(ks[8], (L, 6 * D), 0.01),
        'g_norm1': 1.0 + nrm(ks[9], (L, D), 0.02),
        'w_in': nrm(ks[10], (L, D, N_IN), D ** -0.5),
        'pool_w': nrm(ks[11], (L, POOL_GROUPS, POOL_GC, POOL_GC), POOL_GC ** -0.5),
        'pool_scale': 1.0 + nrm(ks[12], (L, POOL_W), 0.1),
        'gdn_conv': nrm(ks[13], (L, SHORT_CONV, 3 * GDN_W), SHORT_CONV ** -0.5),
        'gdn_a_log': jnp.log(jax.random.uniform(ks[14], (L, 2, GDN_HEADS), f32, 1.0, 16.0)),
        'gdn_dt_bias': dt + jnp.log(-jnp.expm1(-dt)),
        'gdn_norm_g': 1.0 + nrm(ks[16], (L, GDN_HEAD_DIM), 0.02),
        'na_rpb': nrm(ks[17], (L, NA_HEADS, 2 * NA_WR - 1, 2 * NA_WC - 1), 0.1),
        'w_branch_pool': nrm(ks[18], (L, POOL_W, D), POOL_W ** -0.5),
        'w_branch_gdn': nrm(ks[19], (L, GDN_W, D), GDN_W ** -0.5),
        'w_branch_na': nrm(ks[20], (L, NA_W, D), NA_W ** -0.5),
        'w_out': nrm(ks[21], (L, D, D), D ** -0.5),
        'g_norm2': 1.0 + nrm(ks[22], (L, D), 0.02),
        'w_up': nrm(ks[23], (L, D, 2 * D_FF), D ** -0.5),
        'ffn_conv': nrm(ks[24], (L, FFN_CONV, 2 * D_FF), FFN_CONV ** -0.5),
        'w_down': nrm(ks[25], (L, D_FF, D), D_FF ** -0.5),
        'g_final': 1.0 + nrm(ks[26], (D,), 0.02),
    }


def reference(x_prompt, x_sample, cache_na_k, cache_na_v, state_gdn, c, c_ctx, w_ada, b_ada, g_norm1, w_in,
              pool_w, pool_scale, gdn_conv, gdn_a_log, gdn_dt_bias, gdn_norm_g, na_rpb, w_branch_pool,
              w_branch_gdn, w_branch_na, w_out, g_norm2, w_up, ffn_conv, w_down, g_final):
    xp, xs = x_prompt, x_sample
    new_k, new_v, new_s = [], [], []
    for l in range(DEPTH):
        lp = {'g_norm1': g_norm1[l], 'w_in': w_in[l], 'pool_w': pool_w[l], 'pool_scale': pool_scale[l],
              'gdn_conv': gdn_conv[l], 'gdn_a_log': gdn_a_log[l], 'gdn_dt_bias': gdn_dt_bias[l],
              'gdn_norm_g': gdn_norm_g[l], 'na_rpb': na_rpb[l], 'w_branch_pool': w_branch_pool[l],
              'w_branch_gdn': w_branch_gdn[l], 'w_branch_na': w_branch_na[l], 'w_out': w_out[l],
              'g_norm2': g_norm2[l], 'w_up': w_up[l], 'ffn_conv': ffn_conv[l], 'w_down': w_down[l]}
        xp, (k_l, v_l, s_l) = trunk_layer(xp, modulation(c_ctx, w_ada[l], b_ada[l]), lp, None)
        new_k.append(k_l)
        new_v.append(v_l)
        new_s.append(s_l)
        ctx_l = (cache_na_k[:, l], cache_na_v[:, l], state_gdn[:, l])
        xs, _ = trunk_layer(xs, modulation(c, w_ada[l], b_ada[l]), lp, ctx_l)
    y_prompt = rmsnorm(xp, g_final)
    y_sample = rmsnorm(xs, g_final)
    new_cache_na_k = jnp.stack(new_k, axis=1)
    new_cache_na_v = jnp.stack(new_v, axis=1)
    new_state_gdn = jnp.stack(new_s, axis=1).astype(x_prompt.dtype)
    return (y_prompt, y_sample, new_cache_na_k, new_cache_na_v, new_state_gdn)
```

```python
import numpy as np
from contextlib import ExitStack
import concourse.bass as bass
import concourse.mybir as mybir
from concourse.alu_op_type import AluOpType as ALU
from concourse.bass_utils import run_bass_kernel_spmd

F32 = mybir.dt.float32
BF16 = mybir.dt.bfloat16
AF = mybir.ActivationFunctionType

D = 2048
KC = 16
L = 2
NT = 2560
TT = 512
NTT = 5
SEQS = [(0, 256, False), (256, 256, False), (512, 2048, True)]
DFF = 5632
N_IN = 12320
Q0, K0, V0, Z0, BETA0, A0 = 512, 1536, 2560, 3584, 4608, 4624
NAQ0, NAK0, NAV0, GATE0 = 4640, 5152, 5664, 6176
EPS = 1e-6
BIG = 30000.0
DBG = {}
AW = 45056


class Res:
    __slots__ = ("w", "rc", "rd", "excl")

    def __init__(self):
        self.excl = False
        self.w = None
        self.rc = {}
        self.rd = []


class Prog:
    CE = ("pe", "act", "dve")
    EPOCH = 16000
    KDMA = 20

    def __init__(self, nc):
        self.nc = nc
        self.ops = []
        self.res = {}
        self.last = {}
        self.lastdma = {"sp": [], "pool": []}
        self.bar = set()
        self.bar_id = 0
        self.crossed = {}

    def R(self, *key):
        r = self.res.get(key)
        if r is None:
            r = self.res[key] = Res()
            r.excl = key[0] == "ps"
        return r

    def barrier(self):
        deps = set()
        for e in self.CE:
            if e in self.last:
                deps.add(self.last[e])
        for q in ("sp", "pool"):
            deps.update(self.lastdma[q][-self.KDMA:])
        self.bar = deps
        self.bar_id += 1
        self.res = {}

    def add(self, eng, fn, reads=(), writes=(), dma=False):
        i = len(self.ops)
        deps = set()
        xs = [r for r in reads if r.excl]
        if xs:
            writes = list(writes) + xs
            reads = [r for r in reads if not r.excl]
        for r in reads:
            if r.w is not None:
                deps.add(r.w)
        for r in writes:
            if r.w is not None:
                deps.add(r.w)
            for e2, j in r.rc.items():
                deps.add(j)
            deps.update(r.rd)
        if self.crossed.get(eng) != self.bar_id:
            deps |= self.bar
            self.crossed[eng] = self.bar_id
        if eng == "pe":
            deps = {j for j in deps if self.ops[j][0] != "pe" or self.ops[j][3]}
        for r in reads:
            if dma:
                r.rd.append(i)
            else:
                r.rc[eng] = i
        for r in writes:
            r.w = i
            r.rc = {}
            r.rd = []
        if dma:
            self.lastdma[eng].append(i)
        else:
            self.last[eng] = i
        self.ops.append([eng, fn, deps, dma, False, None, None])

    def emit(self):
        nc = self.nc
        ops = self.ops
        for op in ops:
            for j in op[2]:
                ops[j][4] = True
        cnt = {e: 0 for e in self.CE}
        nd = {"sp": 0, "pool": 0}
        tot = {"sp": [0] * self.KDMA, "pool": [0] * self.KDMA}
        for op in ops:
            e = op[0]
            if op[3]:
                slot = nd[e] % self.KDMA
                nd[e] += 1
                prev = tot[e][slot]
                tot[e][slot] += 16
                op[5] = ("d", e, slot)
                op[6] = (prev, tot[e][slot])
            elif op[4]:
                ep, v = divmod(cnt[e], self.EPOCH)
                cnt[e] += 1
                op[5] = ("c", e, ep)
                op[6] = v + 1
        st = ExitStack()
        sems = {}
        for e in self.CE:
            for ep in range(cnt[e] // self.EPOCH + 1):
                sems[("c", e, ep)] = st.enter_context(nc.semaphore(f"s_{e}_{ep}"))
        for e in ("sp", "pool"):
            for s in range(self.KDMA):
                sems[("d", e, s)] = st.enter_context(nc.semaphore(f"d_{e}_{s}"))
        block = st.enter_context(nc.Block())
        per = {e: [] for e in ("pe", "act", "dve", "sp", "pool")}
        for op in ops:
            per[op[0]].append(op)
        KD = self.KDMA

        def run(e, eng):
            waited = {}
            for op in per[e]:
                waits = {}
                for j in op[2]:
                    oj = ops[j]
                    k = oj[5]
                    v = oj[6][1] if oj[3] else oj[6]
                    if waits.get(k, 0) < v:
                        waits[k] = v
                if op[3] and op[6][0] > 0:
                    k = op[5]
                    if waits.get(k, 0) < op[6][0]:
                        waits[k] = op[6][0]
                for k, v in waits.items():
                    if waited.get(k, 0) < v:
                        eng.wait_ge(sems[k], v)
                        waited[k] = v
                ins = op[1](eng)
                if op[3]:
                    ins.then_inc(sems[op[5]], 16)
                elif op[4]:
                    ins.then_inc(sems[op[5]], 1)
            if e in ("sp", "pool"):
                for s in range(KD):
                    if tot[e][s] > 0:
                        eng.wait_ge(sems[("d", e, s)], tot[e][s])

        @block.tensor
        def _(t):
            run("pe", t)

        @block.scalar
        def _(t):
            run("act", t)

        @block.vector
        def _(t):
            run("dve", t)

        @block.sync
        def _(t):
            run("sp", t)

        @block.gpsimd
        def _(t):
            run("pool", t)

        st.close()


def build(plan=None, dbg=()):
    nc = bass.Bass("TRN2", target_bir_lowering=False)
    P = Prog(nc)
    R = P.R
    st = ExitStack()

    def din(name, shape, dt=F32):
        return nc.dram_tensor(name, list(shape), dt, kind="ExternalInput").ap()

    def dout(name, shape, dt=F32):
        return nc.dram_tensor(name, list(shape), dt, kind="ExternalOutput").ap()

    def dscr(name, shape, dt=F32):
        kind = "ExternalOutput" if name in dbg else "Internal"
        return nc.dram_tensor(name, list(shape), dt, kind=kind).ap()

    def sb(name, shape, dt=F32):
        return st.enter_context(nc.sbuf_tensor(name, list(shape), dt))

    xin = din("xin", [NT, D])
    condT = din("condT", [128, KC, 2])
    ck = din("ck", [L, 8, 64, 512])
    cvv = din("cvv", [L, 512, 512])
    s0 = din("s0", [L, 2, 8, 128, 128])
    w_ada = din("w_ada", [L, D, 6 * D])
    b_adaT = din("b_adaT", [L, 128, 96])
    gn1T = din("gn1T", [L, 128, KC])
    gn2T = din("gn2T", [L, 128, KC])
    gfT = din("gfT", [128, KC])
    w_in = din("w_in", [L, D, N_IN])
    pool_w = din("pool_w", [L, 4, 128, 128])
    pool_scT = din("pool_scT", [L, 128, 4])
    gconvT = din("gconvT", [L, 128, 24, 3])
    alog = din("alog", [L, 16, 1])
    dtb = din("dtb", [L, 16, 1])
    gng = din("gng", [L, 128, 1])
    rpbp = din("rpbp", [L, 8, 15, 128])
    wb = din("wb", [L, D, D])
    w_out = din("w_out", [L, D, D])
    w_up = din("w_up", [L, D, 2 * DFF])
    fconvT = din("fconvT", [L, 128, 88, 3])
    w_down = din("w_down", [L, DFF, D])
    c_invcnt = din("c_invcnt", [4, 128, 2304])
    c_misc = din("c_misc", [128, 640])
    c_reset = din("c_reset", [16, 2048])

    y_out = dout("y_out", [NT, D])
    nk_out = dout("nk_out", [2, L, 256, 512])
    nv_out = dout("nv_out", [2, L, 256, 512])
    ns_out = dout("ns_out", [2, L, 2, 8, 128, 128])

    XT = dscr("XT", [D, NT])
    U = dscr("U", [N_IN, NT])
    YT = dscr("YT", [D, NT], BF16)
    MT = dscr("MT", [D, NT], BF16)
    QKVn = dscr("QKVn", [3072, NT])
    KVtok = dscr("KVtok", [NT, 2048])
    GS = dscr("GS", [2, 16, NT])
    OFB = dscr("OFB", [2, 1024, NT])
    XTv = XT.rearrange("(kc p) n -> p kc n", p=128)

    arena = sb("arena", [128, AW], F32)
    ident = sb("ident", [128, 128], F32)
    ones_m = sb("ones_m", [128, 128], F32)
    ones1 = sb("ones1", [128, 128], F32)
    onesb = sb("onesb", [128, 128], BF16)
    ones128 = sb("ones128", [128, 128], F32)
    masks = sb("masks", [64, 4, 64], F32)
    colmask = sb("colmask", [64, 64], F32)
    isb = sb("isb", [16, 1], F32)
    Jm = sb("Jm", [64, 64], F32)
    resetm = sb("resetm", [16, 2048], F32)
    silucT = sb("silucT", [128, KC, 2], BF16)
    condsb = sb("condsb", [128, KC, 2], F32)
    modsb = sb("modsb", [128, 96, 2], F32)
    badaT = sb("badaT", [128, 96], F32)
    gnT = sb("gnT", [128, 3, KC], F32)
    a_sc = sb("a_sc", [128, 2, KC, 2], F32)
    prm = sb("prm", [128, 512], F32)
    tokS = sb("tokS", [64, 40, 64], F32)
    zcol = sb("zcol", [128, 1], F32)
    ocol = sb("ocol", [128, 1], F32)
    epscol = sb("epscol", [128, 1], F32)
    psum = [st.enter_context(nc.psum_tensor(f"ps{i}", [128, 512], F32)) for i in range(8)]
    Rps = lambda i: R("ps", i)

    cnt = {"n": 0, "off": 0, "rr": 0}

    def uid():
        cnt["n"] += 1
        return cnt["n"]

    def phase():
        P.barrier()
        cnt["off"] = 0

    def carve(ncols_f32, shape=None, dt=F32, parts=128):
        o = cnt["off"]
        assert o + ncols_f32 <= AW, ("arena overflow", o, ncols_f32)
        cnt["off"] = o + ncols_f32
        a = arena[0:parts, o:o + ncols_f32]
        if dt == BF16:
            a = a.bitcast(BF16)
        if shape is not None and len(shape) == 3:
            a = a.rearrange("p (a b) -> p a b", a=shape[1])
        elif shape is not None and len(shape) == 4:
            a = a.rearrange("p (a b c) -> p a b c", a=shape[1], b=shape[2])
        return a

    def dma(out, in_, reads, writes, q="sp", slow=False):
        if slow:
            P.add(q, lambda e, o=out, i=in_: e.dma_start(out=o, in_=i, allow_slow_non_contiguous=True), reads, writes, dma=True)
        else:
            P.add(q, lambda e, o=out, i=in_: e.dma_start(out=o, in_=i), reads, writes, dma=True)

    def mm(out, lhsT, rhs, start, stop, reads, writes):
        P.add("pe", lambda e, o=out, a=lhsT, b=rhs, s0_=start, s1_=stop: e.matmul(o, a, b, start=s0_, stop=s1_), reads, writes)

    def tr(out, in_, idn, reads, writes):
        P.add("pe", lambda e, o=out, a=in_, b=idn: e.transpose(o, a, b), reads, writes)

    def act(out, in_, func, reads, writes, bias=None, scale=None):
        def f(e, o=out, i=in_, fu=func, b=bias, s=scale):
            kw = {}
            if b is not None:
                kw["bias"] = b
            if s is not None:
                kw["scale"] = s
            return e.activation(out=o, in_=i, func=fu, **kw)
        P.add("act", f, reads, writes)

    def ts(out, in0, s1, s2, op0, op1, reads, writes):
        if op1 is None:
            P.add("dve", lambda e, o=out, a=in0, x=s1, p0=op0: e.tensor_scalar(o, a, x, None, p0), reads, writes)
        else:
            P.add("dve", lambda e, o=out, a=in0, x=s1, y=s2, p0=op0, p1=op1: e.tensor_scalar(o, a, x, y, p0, p1), reads, writes)

    def tt(out, in0, in1, op, reads, writes):
        P.add("dve", lambda e, o=out, a=in0, b=in1, p=op: e.tensor_tensor(o, a, b, p), reads, writes)

    def stt(out, in0, s, in1, op0, op1, reads, writes):
        P.add("dve", lambda e, o=out, a=in0, x=s, b=in1, p0=op0, p1=op1: e.scalar_tensor_tensor(o, a, x, b, p0, p1), reads, writes)

    def cp(out, in_, reads, writes, eng="dve"):
        if eng == "dve":
            P.add("dve", lambda e, o=out, i=in_: e.tensor_copy(o, i), reads, writes)
        else:
            act(out, in_, AF.Copy, reads, writes)

    def memset(ap, val, writes):
        P.add("dve", lambda e, a=ap, v=val: e.memset(a, v), (), writes)

    def rsqrt(out, in_, reads, writes):
        act(out, in_, AF.Sqrt, reads, writes, bias=epscol[0:in_.shape[0], :])
        P.add("dve", lambda e, o=out: e.reciprocal(o, o), writes, writes)

    def evac_eng():
        cnt["rr"] += 1
        return "dve" if cnt["rr"] % 2 else "act"

    def phase_consts():
        Rc = R("c")
        dma(ident[:], c_misc[:, 0:128], (), [Rc])
        dma(masks[:], c_misc[0:64, 128:384].rearrange("p (a b) -> p a b", a=4), (), [Rc])
        dma(colmask[:], c_misc[0:64, 384:448], (), [Rc])
        dma(isb[:], c_misc[0:16, 448:449], (), [Rc], slow=True)
        dma(Jm[:], c_misc[0:64, 512:576], (), [Rc])
        dma(resetm[:], c_reset[:, :], (), [Rc])
        memset(ones_m[:], 1.0 / D, [Rc])
        memset(ones1[:], 1.0, [Rc])
        memset(ones128[:], 1.0 / 128, [Rc])
        memset(onesb[:], 1.0, [Rc])
        memset(zcol[:], 0.0, [Rc])
        memset(ocol[:], 1.0, [Rc])
        memset(epscol[:], EPS, [Rc])
        for c0 in range(0, AW, 4096):
            memset(arena[:, c0:min(AW, c0 + 4096)], 0.0, [Rc])
        memset(tokS[:], 0.0, [Rc])
        dma(condsb[:], condT[:, :, :], (), [R("condsb")])
        act(silucT[:], condsb[:], AF.Silu, [R("condsb")], [R("silucT")])
        dma(gnT[:, 2, :], gfT[:, :], (), [R("gnT")])

    def phase_in_transpose():
        phase()
        wk = [carve(2048) for _ in range(2)]
        xt = carve(8192, [128, KC, 512])
        for t in range(NTT):
            for b in range(4):
                tok0 = t * TT + b * 128
                w_, Rw_ = wk[b % 2], R("wk", b % 2)
                dma(w_, xin[tok0:tok0 + 128, :], (), [Rw_])
                for g in range(4):
                    pb = (b * 4 + g) % 8
                    for j in range(4):
                        kc = g * 4 + j
                        tr(psum[pb][:, j * 128:(j + 1) * 128], w_[:, kc * 128:(kc + 1) * 128], ident[:], [Rw_], [Rps(pb)])
                    o = xt[:, g * 4:(g + 1) * 4, b * 128:(b + 1) * 128]
                    i = psum[pb][:, :].rearrange("p (j n) -> p j n", j=4)
                    cp(o, i, [Rps(pb)], [R("xt")], evac_eng())
            dma(XTv[:, :, t * TT:(t + 1) * TT], xt, [R("xt")], [])

    def phase_mod(l):
        phase()
        wbuf = [carve(4096, [128, KC, 512], BF16) for _ in range(2)]
        dma(badaT[:], b_adaT[l], (), [R("badaT")])
        dma(gnT[:, 0, :], gn1T[l], (), [R("gnT")])
        dma(gnT[:, 1, :], gn2T[l], (), [R("gnT")])
        wv = w_ada[l].rearrange("(kc p) n -> p kc n", p=128)
        for g in range(24):
            s = g % 2
            dma(wbuf[s], wv[:, :, g * 512:(g + 1) * 512], (), [R("wbuf", s)], q="pool")
            for j in range(4):
                ch = g * 4 + j
                pb = ch % 8
                for kc in range(KC):
                    mm(psum[pb][:, 0:2], wbuf[s][:, kc, j * 128:(j + 1) * 128], silucT[:, kc, :], kc == 0, kc == KC - 1,
                       [R("wbuf", s)], [Rps(pb)])
                act(modsb[:, ch, :], psum[pb][:, 0:2], AF.Identity, [Rps(pb), R("badaT")], [R("modsb")], bias=badaT[:, ch:ch + 1])
        for ni, (sci, gi) in enumerate(((1, 0), (4, 1))):
            for c in range(2):
                ts(a_sc[:, ni, :, c], modsb[:, sci * 16:(sci + 1) * 16, c], 1.0, None, ALU.add, None, [R("modsb")], [R("a_sc")])
                tt(a_sc[:, ni, :, c], a_sc[:, ni, :, c], gnT[:, gi, :], ALU.mult, [R("a_sc"), R("gnT")], [R("a_sc")])

    def phase_norm(ni, final=False):
        phase()
        H = carve(20480, [128, KC, NT], BF16)
        xt = carve(8192, [128, KC, 512])
        sq = [carve(512) for _ in range(2)]
        tmp = [carve(512) for _ in range(3)]
        rstd = carve(512)
        wk = [carve(2048) for _ in range(2)]
        for t in range(NTT):
            c = 0 if t == 0 else 1
            dma(xt, XTv[:, :, t * TT:(t + 1) * TT], (), [R("xt")])
            pb = t % 2
            for kc in range(KC):
                s = kc % 2
                act(sq[s], xt[:, kc, :], AF.Square, [R("xt")], [R("sq", s)])
                mm(psum[pb][:], ones_m[:], sq[s], kc == 0, kc == KC - 1, [R("sq", s)], [Rps(pb)])
            rsqrt(rstd, psum[pb][:], [Rps(pb)], [R("rstd")])
            if not final:
                for kc in range(KC):
                    s = kc % 3
                    stt(tmp[s], xt[:, kc, :], a_sc[:, ni, kc, c:c + 1], rstd, ALU.mult, ALU.mult,
                        [R("xt"), R("rstd")], [R("tmp", s)])
                    act(H[:, kc, t * TT:(t + 1) * TT], tmp[s], AF.Identity, [R("tmp", s)], [R("H", kc)],
                        bias=modsb[:, ni * 48 + kc, c:c + 1])
            else:
                for kc in range(KC):
                    stt(xt[:, kc, :], xt[:, kc, :], gnT[:, 2, kc:kc + 1], rstd, ALU.mult, ALU.mult,
                        [R("xt"), R("rstd")], [R("xt")])
                for b in range(4):
                    w_, Rw_ = wk[b % 2], R("wk", b % 2)
                    for g in range(4):
                        pb2 = 2 + (b * 4 + g) % 6
                        for j in range(4):
                            kc = g * 4 + j
                            tr(psum[pb2][:, j * 128:(j + 1) * 128], xt[:, kc, b * 128:(b + 1) * 128], ident[:], [R("xt")], [Rps(pb2)])
                        cp(w_[:, g * 512:(g + 1) * 512], psum[pb2][:], [Rps(pb2)], [Rw_], evac_eng())
                    tok0 = t * TT + b * 128
                    dma(y_out[tok0:tok0 + 128, :], w_, [Rw_], [])
        return H

    def phase_proj(wmat, col0, ncols, dst, dst_row0, func_of_col=None, groups=None):
        P.barrier()
        cnt["off"] = 20480
        H = arena[:, 0:20480].bitcast(BF16).rearrange("p (a b) -> p a b", a=KC)
        wbuf = [carve(4096, [128, KC, 512], BF16) for _ in range(2)]
        stg = [carve(2560) for _ in range(2)]
        wv = wmat.rearrange("(kc p) n -> p kc n", p=128)
        if groups is None:
            groups = []
            g0 = col0
            while g0 < col0 + ncols:
                gw = min(512, col0 + ncols - g0)
                groups.append((g0, gw))
                g0 += gw
        for gi, (g0, gw) in enumerate(groups):
            s = gi % 2
            dma(wbuf[s][:, :, 0:gw], wv[:, :, g0:g0 + gw], (), [R("wbuf", s)], q="pool")
            for j in range(0, gw, 128):
                cw = min(128, gw - j)
                col = g0 + j
                wi = uid() % 2
                fu = func_of_col(col) if func_of_col else None
                for t in range(NTT):
                    pb = uid() % 8
                    for kc in range(KC):
                        mm(psum[pb][0:cw, :], wbuf[s][:, kc, j:j + cw], H[:, kc, t * TT:(t + 1) * TT], kc == 0, kc == KC - 1,
                           [R("wbuf", s), R("H", kc)], [Rps(pb)])
                    o = stg[wi][0:cw, t * TT:(t + 1) * TT]
                    if fu is not None:
                        act(o, psum[pb][0:cw, :], fu, [Rps(pb)], [R("stg", wi)])
                    else:
                        cp(o, psum[pb][0:cw, :], [Rps(pb)], [R("stg", wi)], evac_eng())
                r0 = dst_row0 + (col - col0)
                dma(dst[r0:r0 + cw, :], stg[wi][0:cw, :], [R("stg", wi)], [])

    def conv3(out, in_, w3, segs, reads, writes):
        ts(out, in_, w3[:, 1:2], None, ALU.mult, None, reads, writes)
        for (a, b) in segs:
            stt(out[:, a + 1:b], in_[:, a:b - 1], w3[:, 0:1], out[:, a + 1:b], ALU.mult, ALU.add, list(reads) + list(writes), writes)
            stt(out[:, a:b - 1], in_[:, a + 1:b], w3[:, 2:3], out[:, a:b - 1], ALU.mult, ALU.add, list(reads) + list(writes), writes)

    def phase_pool(l):
        phase()
        PAD = 16
        Wd = NT + 6 * PAD
        ub = carve(Wd)
        la = carve(Wd)
        lb = carve(Wd)
        ic = carve(2304)
        dT = [carve(1280, None, BF16) for _ in range(2)]
        ys = [carve(1280, None, BF16) for _ in range(2)]
        pw = carve(256, [128, 4, 128], BF16)
        dma(prm[:, 0:4], pool_scT[l], (), [R("prm")])
        dma(pw, pool_w[l].rearrange("g c d -> c g d"), (), [R("pw")], q="pool")
        memset(ub, 0.0, [R("ub")])
        memset(la, 0.0, [R("la")])
        memset(lb, 0.0, [R("lb")])
        offs = [a + PAD * (2 * si + 1) for si, (a, T, _) in enumerate(SEQS)]
        for g in range(4):
            for si, (a, T, _) in enumerate(SEQS):
                dma(ub[:, offs[si]:offs[si] + T], U[g * 128:(g + 1) * 128, a:a + T], (), [R("ub")])
            dma(ic, c_invcnt[g], (), [R("ic")])
            tt(la[:, 1:Wd], ub[:, 0:Wd - 1], ub[:, 1:Wd], ALU.add, [R("ub")], [R("la")])
            cur, Rcur, oth, Roth = la, R("la"), lb, R("lb")
            sh = 1
            for lev in range(g):
                tt(oth[:, sh:Wd - sh], cur[:, 0:Wd - 2 * sh], cur[:, 2 * sh:Wd], ALU.add, [Rcur], [Roth])
                cur, Rcur, oth, Roth = oth, Roth, cur, Rcur
                sh *= 2
            d_ = dT[g % 2]
            for si, (a, T, _) in enumerate(SEQS):
                o = offs[si]
                ioff = 0 if T == 256 else 256
                tt(cur[:, o:o + T], cur[:, o:o + T], ic[:, ioff:ioff + T], ALU.mult, [Rcur, R("ic")], [Rcur])
                tt(d_[:, a:a + T], cur[:, o:o + T], ub[:, o:o + T], ALU.subtract, [Rcur, R("ub")], [R("dT", g % 2)])
            y_ = ys[g % 2]
            for t in range(NTT):
                pb = uid() % 8
                mm(psum[pb][:], pw[:, g, :], d_[:, t * TT:(t + 1) * TT], True, True, [R("pw"), R("dT", g % 2)], [Rps(pb)])
                ts(y_[:, t * TT:(t + 1) * TT], psum[pb][:], prm[:, g:g + 1], None, ALU.mult, None, [Rps(pb), R("prm")], [R("ys", g % 2)])
            dma(YT[g * 128:(g + 1) * 128, :], y_, [R("ys", g % 2)], [])

    def phase_gdn_pre(l):
        phase()
        raw = [carve(2048) for _ in range(2)]
        cvb = [carve(2048) for _ in range(2)]
        sqb = [carve(512) for _ in range(2)]
        rn = [carve(512) for _ in range(2)]
        stg = [carve(512, [128, 4, 128]) for _ in range(2)]
        Bt = carve(2048, parts=16)
        At = carve(2048, parts=16)
        Gf = carve(2048, parts=16)
        Gp = carve(2048, parts=16)
        EG = carve(2048, parts=16)
        BE = carve(2048, parts=16)
        ED = carve(2048, parts=16)
        td = carve(2048, parts=16)
        cw_ = carve(72, [128, 24, 3])
        dma(cw_, gconvT[l], (), [R("cw")])
        dma(prm[0:16, 8:9], alog[l], (), [R("prm")])
        dma(prm[0:16, 9:10], dtb[l], (), [R("prm")])
        act(prm[0:16, 10:11], prm[0:16, 8:9], AF.Exp, [R("prm")], [R("prm2")])
        ts(prm[0:16, 10:11], prm[0:16, 10:11], -1.0, None, ALU.mult, None, [R("prm2")], [R("prm2")])
        it = 0
        for (a, T, _) in SEQS:
            for which, r00 in enumerate((Q0, K0, V0)):
                for h in range(8):
                    b_ = it % 2
                    it += 1
                    x_, Rx = raw[b_][:, 0:T], R("raw", b_)
                    c_, Rcv = cvb[b_][:, 0:T], R("cvb", b_)
                    r0 = r00 + h * 128
                    dma(x_, U[r0:r0 + 128, a:a + T], (), [Rx])
                    conv3(c_, x_, cw_[:, which * 8 + h, :], [(0, T)], [Rx, R("cw")], [Rcv])
                    act(c_, c_, AF.Silu, [Rcv], [Rcv])
                    if which < 2:
                        for sl in range(0, T, 512):
                            w_ = min(512, T - sl)
                            sb_ = uid() % 2
                            pb = uid() % 8
                            act(sqb[sb_][:, 0:w_], c_[:, sl:sl + w_], AF.Square, [Rcv], [R("sqb", sb_)])
                            mm(psum[pb][:, 0:w_], ones1[:], sqb[sb_][:, 0:w_], True, True, [R("sqb", sb_)], [Rps(pb)])
                            rsqrt(rn[sb_][:, 0:w_], psum[pb][:, 0:w_], [Rps(pb)], [R("rn", sb_)])
                            scl = (128.0 ** -0.5) if which == 0 else 1.0
                            stt(c_[:, sl:sl + w_], c_[:, sl:sl + w_], scl, rn[sb_][:, 0:w_], ALU.mult, ALU.mult, [Rcv, R("rn", sb_)], [Rcv])
                    dma(QKVn[which * 1024 + h * 128: which * 1024 + (h + 1) * 128, a:a + T], c_, [Rcv], [])
                    if which >= 1:
                        for g in range(T // 512 if T >= 512 else 1):
                            nb = min(4, T // 128)
                            pb = uid() % 8
                            si_ = uid() % 2
                            for j in range(nb):
                                blk = g * 4 + j
                                tr(psum[pb][:, j * 128:(j + 1) * 128], c_[:, blk * 128:(blk + 1) * 128], ident[:], [Rcv], [Rps(pb)])
                            cp(stg[si_][:, 0:nb, :], psum[pb][:, 0:nb * 128].rearrange("p (j n) -> p j n", j=nb), [Rps(pb)], [R("stg", si_)], evac_eng())
                            t0 = a + g * 512
                            cc = (which - 1) * 1024 + h * 128
                            dma(KVtok[t0:t0 + nb * 128, cc:cc + 128].rearrange("(j p) c -> p j c", p=128), stg[si_][:, 0:nb, :], [R("stg", si_)], [])
            Rg = R("gate")
            dma(Bt[:, 0:T], U[BETA0:BETA0 + 16, a:a + T], (), [Rg])
            dma(At[:, 0:T], U[A0:A0 + 16, a:a + T], (), [Rg])
            act(Bt[:, 0:T], Bt[:, 0:T], AF.Sigmoid, [Rg], [Rg])
            act(At[:, 0:T], At[:, 0:T], AF.Exp, [Rg, R("prm")], [Rg], bias=prm[0:16, 9:10])
            act(At[:, 0:T], At[:, 0:T], AF.Ln, [Rg], [Rg], bias=ocol[0:16, :])
            ts(At[:, 0:T], At[:, 0:T], prm[0:16, 10:11], None, ALU.mult, None, [Rg, R("prm2")], [Rg])
            P.add("dve", lambda e, o=Gf[:, 0:T], d0=resetm[:, 0:T], d1=At[:, 0:T]: e.tensor_tensor_scan(o, d0, d1, 0.0, ALU.mult, ALU.add), [Rg], [Rg])
            nch = T // 64
            tot = Gf[:, 63:T:64].unsqueeze(2).to_broadcast([16, nch, 64])
            v3 = lambda ap: ap[:, 0:T].rearrange("p (c j) -> p c j", j=64)
            stt(v3(td), v3(Gf), -2.0, tot, ALU.mult, ALU.add, [Rg], [Rg])
            tt(td[:, 0:T], td[:, 0:T], At[:, 0:T], ALU.add, [Rg], [Rg])
            stt(Gp[:, 0:T], td[:, 0:T], isb[:, 0:1], Gf[:, 0:T], ALU.mult, ALU.add, [Rg], [Rg])
            act(EG[:, 0:T], Gp[:, 0:T], AF.Exp, [Rg], [Rg])
            tt(BE[:, 0:T], Bt[:, 0:T], EG[:, 0:T], ALU.mult, [Rg], [Rg])
            tt(v3(td), tot, v3(Gp), ALU.subtract, [Rg], [Rg])
            act(ED[:, 0:T], td[:, 0:T], AF.Exp, [Rg], [Rg])
            dma(GS[0, :, a:a + T], Gp[:, 0:T], [Rg], [])
            dma(GS[1, :, a:a + T], EG[:, 0:T], [Rg], [])
            for c in range(nch):
                pb = uid() % 8
                for k_, src in enumerate((Gp, Bt, BE, ED)):
                    tr(psum[pb][0:64, k_ * 16:(k_ + 1) * 16], src[:, c * 64:(c + 1) * 64], ident[0:16, 0:16], [Rg], [Rps(pb)])
                cp(tokS[:, a // 64 + c, :], psum[pb][0:64, 0:64], [Rps(pb)], [R("tokS")], evac_eng())

    def phase_gdn_main(l):
        phase()
        S = carve(1024, [128, 8, 128])
        qT = [carve(512, [128, 8, 64]) for _ in range(2)]
        kT = [carve(512, [128, 8, 64]) for _ in range(2)]
        ktok = [carve(1024, [64, 8, 128], parts=64) for _ in range(2)]
        vtok = [carve(1024, [64, 8, 128], parts=64) for _ in range(2)]
        GB = [carve(512, [64, 8, 64], parts=64) for _ in range(2)]
        EGB = [carve(512, [128, 8, 64]) for _ in range(2)]
        c8 = lambda: carve(512, [64, 8, 64], parts=64)
        XS, XTm, DmS, DmT, Lm, Mm, QKm = c8(), c8(), c8(), c8(), c8(), c8(), c8()
        Rtb = [c8(), c8()]
        Lp = [c8(), c8()]
        Mp = [c8(), c8()]
        vtb = carve(1024, [64, 8, 128], parts=64)
        ktb = carve(1024, [64, 8, 128], parts=64)
        kd = carve(1024, [64, 8, 128], parts=64)
        vnew = carve(1024, [64, 8, 128], parts=64)
        wT = carve(512, [128, 8, 64])
        qg = carve(512, [128, 8, 64])
        ost = [carve(512, [128, 8, 64]) for _ in range(2)]
        I64 = ident[0:64, 0:64]
        step = 0
        for si, (a, T, is_s) in enumerate(SEQS):
            nch = T // 64
            for d in range(2):
                RS = [R("S", h) for h in range(8)]
                if is_s:
                    dma(S, s0[l, d].rearrange("h k v -> k h v"), (), RS)
                else:
                    memset(S, 0.0, RS)
                mS = masks[:, 2 * d, :]
                mT = masks[:, 2 * d + 1, :]
                for s_ in range(min(nch, DBG.get("gdn_steps", 10 ** 9))):
                    c = s_ if d == 0 else nch - 1 - s_
                    b_ = step % 2
                    step += 1
                    t0 = a + c * 64
                    cg = t0 // 64
                    Rl = R("ld", b_)
                    dma(qT[b_], QKVn[0:1024, t0:t0 + 64].rearrange("(h p) t -> p h t", p=128), (), [Rl])
                    dma(kT[b_], QKVn[1024:2048, t0:t0 + 64].rearrange("(h p) t -> p h t", p=128), (), [Rl])
                    dma(ktok[b_], KVtok[t0:t0 + 64, 0:1024].rearrange("p (h d) -> p h d", h=8), (), [Rl])
                    dma(vtok[b_], KVtok[t0:t0 + 64, 1024:2048].rearrange("p (h d) -> p h d", h=8), (), [Rl])
                    gsrc = GS[0, d * 8:(d + 1) * 8, t0:t0 + 64]
                    dma(GB[b_], bass.AP(gsrc.tensor, gsrc.offset, [[0, 64], [NT, 8], [1, 64]]), (), [Rl])
                    esrc = GS[1, d * 8:(d + 1) * 8, t0:t0 + 64]
                    dma(EGB[b_], bass.AP(esrc.tensor, esrc.offset, [[0, 128], [NT, 8], [1, 64]]), (), [Rl])
                    col = lambda kind, h: tokS[:, cg, kind * 16 + d * 8 + h: kind * 16 + d * 8 + h + 1]
                    HR = lambda n, h: R(n, h)
                    for h in range(8):
                        mm(psum[h][0:64, 0:64], kT[b_][:, h, :], kT[b_][:, h, :], True, True, [Rl], [R("ps", h)])
                        mm(psum[h][0:64, 64:128], kT[b_][:, h, :], qT[b_][:, h, :], True, True, [Rl], [R("ps", h)])
                    if DBG.get("gdn_stage", "z") == "0":
                        continue
                    for h in range(8):
                        stt(XS[:, h, :], GB[b_][:, h, :], col(0, h), mS, ALU.subtract, ALU.add, [Rl, R("tokS")], [HR("XS", h)])
                        act(DmS[:, h, :], XS[:, h, :], AF.Exp, [HR("XS", h)], [HR("DmS", h)], scale=-1.0)
                        stt(XTm[:, h, :], GB[b_][:, h, :], col(0, h), mT, ALU.subtract, ALU.add, [Rl, R("tokS")], [HR("XT", h)])
                        act(DmT[:, h, :], XTm[:, h, :], AF.Exp, [HR("XT", h)], [HR("DmT", h)])
                    if DBG.get("gdn_stage", "z") == "a":
                        continue
                    for h in range(8):
                        stt(Lm[:, h, :], psum[h][0:64, 0:64], col(1, h), DmS[:, h, :], ALU.mult, ALU.mult, [R("ps", h), HR("DmS", h), R("tokS")], [HR("L", h)])
                        tt(QKm[:, h, :], psum[h][0:64, 64:128], DmT[:, h, :], ALU.mult, [R("ps", h), HR("DmT", h)], [HR("QKm", h)])
                    if DBG.get("c_sub", 9) < 2:
                        continue
                    for h in range(8):
                        tr(psum[h][0:64, 128:192], Lm[:, h, :], I64, [HR("L", h)], [R("ps", h)])
                    if DBG.get("c_sub", 9) < 3:
                        continue
                    for h in range(8):
                        cp(Mm[:, h, :], psum[h][0:64, 128:192], [R("ps", h)], [HR("M", h)], "act")
                        stt(Rtb[0][:, h, :], psum[h][0:64, 128:192], -1.0, I64, ALU.mult, ALU.add, [R("ps", h)], [HR("Rt0", h)])
                    if DBG.get("gdn_stage", "z") == "b":
                        continue
                    for h in range(8):
                        mm(psum[h][0:64, 256:320], Mm[:, h, :], Lm[:, h, :], True, True, [HR("M", h), HR("L", h)], [R("ps", h)])
                        mm(psum[h][0:64, 128:192], Lm[:, h, :], Mm[:, h, :], True, True, [HR("M", h), HR("L", h)], [R("ps", h)])
                    if DBG.get("d_sub", 9) < 1:
                        continue
                    for h in range(8):
                        cp(Lp[0][:, h, :], psum[h][0:64, 256:320], [R("ps", h)], [HR("Lp0", h)], "act")
                        cp(Mp[0][:, h, :], psum[h][0:64, 128:192], [R("ps", h)], [HR("Mp0", h)], "dve")
                    Rt, RtN = Rtb[0], "Rt0"
                    for lev in range(min(5, DBG.get("d_lev", 5))):
                        i_ = lev % 2
                        Rt, RtN = Rtb[lev % 2], f"Rt{lev % 2}"
                        Rt2, Rt2N = Rtb[1 - lev % 2], f"Rt{1 - lev % 2}"
                        o_ = 1 - i_
                        last = lev == 4
                        for h in range(8):
                            if not last:
                                mm(psum[h][0:64, 128:192], Lp[i_][:, h, :], Mp[i_][:, h, :], True, True, [HR(f"Lp{i_}", h), HR(f"Mp{i_}", h)], [R("ps", h)])
                            mm(psum[h][0:64, 192:256], Lp[i_][:, h, :], Rt[:, h, :], True, True, [HR(f"Lp{i_}", h), HR(RtN, h)], [R("ps", h)])
                            if not last:
                                mm(psum[h][0:64, 256:320], Mp[i_][:, h, :], Lp[i_][:, h, :], True, True, [HR(f"Lp{i_}", h), HR(f"Mp{i_}", h)], [R("ps", h)])
                        for h in range(8):
                            if DBG.get("lev_sub", 9) >= 1:
                                tt(Rt2[:, h, :], psum[h][0:64, 192:256], Rt[:, h, :], ALU.add, [R("ps", h), HR(RtN, h)], [HR(Rt2N, h)])
                            if not last and DBG.get("lev_sub", 9) >= 2:
                                cp(Mp[o_][:, h, :], psum[h][0:64, 128:192], [R("ps", h)], [HR(f"Mp{o_}", h)], "act")
                                cp(Lp[o_][:, h, :], psum[h][0:64, 256:320], [R("ps", h)], [HR(f"Lp{o_}", h)], "act" if h % 2 else "dve")
                    if DBG.get("gdn_stage", "z") == "d":
                        continue
                    nlev_ = min(5, DBG.get("d_lev", 5))
                    Rt, RtN = Rtb[nlev_ % 2], f"Rt{nlev_ % 2}"
                    for h in range(8):
                        ts(vtb[:, h, :], vtok[b_][:, h, :], col(1, h), None, ALU.mult, None, [Rl, R("tokS")], [HR("vtb", h)])
                        ts(ktb[:, h, :], ktok[b_][:, h, :], col(2, h), None, ALU.mult, None, [Rl, R("tokS")], [HR("ktb", h)])
                        ts(kd[:, h, :], ktok[b_][:, h, :], col(3, h), None, ALU.mult, None, [Rl, R("tokS")], [HR("kd", h)])
                        tt(qg[:, h, :], qT[b_][:, h, :], EGB[b_][:, h, :], ALU.mult, [Rl], [HR("qg", h)])
                    for h in range(8):
                        mm(psum[h][:, 448:512], ktb[:, h, :], Rt[:, h, :], True, True, [HR("ktb", h), HR(RtN, h)], [R("ps", h)])
                    for h in range(8):
                        act(wT[:, h, :], psum[h][:, 448:512], AF.Copy, [R("ps", h)], [HR("wT", h)], scale=-1.0)
                    if DBG.get("gdn_stage", "z") == "e":
                        continue
                    for h in range(8):
                        mm(psum[h][0:64, 320:448], Rt[:, h, :], vtb[:, h, :], True, False, [HR(RtN, h), HR("vtb", h)], [R("ps", h)])
                        mm(psum[h][0:64, 320:448], wT[:, h, :], S[:, h, :], False, True, [HR("wT", h), R("S", h)], [R("ps", h)])
                    for h in range(8):
                        cp(vnew[:, h, :], psum[h][0:64, 320:448], [R("ps", h)], [HR("vnew", h)], "act" if h % 2 else "dve")
                    if DBG.get("gdn_stage", "z") == "f":
                        continue
                    ob_ = ost[b_]
                    for h in range(8):
                        mm(psum[h][:, 0:64], S[:, h, :], qg[:, h, :], True, False, [R("S", h), HR("qg", h)], [R("ps", h)])
                        mm(psum[h][:, 0:64], vnew[:, h, :], QKm[:, h, :], False, True, [HR("vnew", h), HR("QKm", h)], [R("ps", h)])
                    for h in range(8):
                        cp(ob_[:, h, :], psum[h][:, 0:64], [R("ps", h)], [R("ost", b_)], "act" if h % 2 else "dve")
                    dma(OFB[d, :, t0:t0 + 64].rearrange("(h p) t -> p h t", p=128), ob_, [R("ost", b_)], [])
                    if DBG.get("gdn_stage", "z") == "g":
                        continue
                    gcol = 63 if d == 0 else 0
                    for h in range(8):
                        mm(psum[h][:, 128:256], kd[:, h, :], vnew[:, h, :], True, True, [HR("kd", h), HR("vnew", h)], [R("ps", h), R("ps", h)])
                    for h in range(8):
                        ts(S[:, h, :], S[:, h, :], EGB[b_][:, h, gcol:gcol + 1], None, ALU.mult, None, [Rl, R("S", h)], [R("S", h)])
                        tt(S[:, h, :], psum[h][:, 128:256], S[:, h, :], ALU.add, [R("ps", h), R("ps", h), R("S", h)], [R("S", h)])
                if not is_s:
                    dma(ns_out[si, l, d].rearrange("h k v -> k h v"), S, [R("S", h) for h in range(8)], [])

    def phase_gdn_post(l):
        phase()
        of_ = [carve(2560) for _ in range(2)]
        ob_ = [carve(2560) for _ in range(2)]
        zz = [carve(2560) for _ in range(2)]
        sq = [carve(512) for _ in range(2)]
        rs = [carve(512) for _ in range(2)]
        yb = [carve(1280, None, BF16) for _ in range(2)]
        dma(prm[:, 16:17], gng[l], (), [R("prm")])
        for h in range(8):
            b_ = h % 2
            dma(of_[b_], OFB[0, h * 128:(h + 1) * 128, :], (), [R("of", b_)])
            dma(ob_[b_], OFB[1, h * 128:(h + 1) * 128, :], (), [R("ob", b_)])
            dma(zz[b_], U[Z0 + h * 128:Z0 + (h + 1) * 128, :], (), [R("zz", b_)])
            tt(of_[b_], of_[b_], ob_[b_], ALU.add, [R("of", b_), R("ob", b_)], [R("of", b_)])
            for t in range(NTT):
                s = uid() % 2
                pb = uid() % 8
                sl = slice(t * TT, (t + 1) * TT)
                act(sq[s], of_[b_][:, sl], AF.Square, [R("of", b_)], [R("sq", s)])
                mm(psum[pb][:], ones128[:], sq[s], True, True, [R("sq", s)], [Rps(pb)])
                rsqrt(rs[s], psum[pb][:], [Rps(pb)], [R("rs", s)])
                stt(of_[b_][:, sl], of_[b_][:, sl], prm[:, 16:17], rs[s], ALU.mult, ALU.mult, [R("of", b_), R("rs", s), R("prm")], [R("of", b_)])
                tt(yb[b_][:, sl], of_[b_][:, sl], zz[b_][:, sl], ALU.mult, [R("of", b_), R("zz", b_)], [R("yb", b_)])
            dma(YT[512 + h * 128:512 + (h + 1) * 128, :], yb[b_], [R("yb", b_)], [])

    def phase_na(l):
        phase()
        BM = carve(7680, [64, 120, 64], parts=64)
        vtk = carve(10240, [64, 40, 512], BF16, parts=64)
        kvfb = carve(8192, parts=64)
        kvf = [kvfb[:, i * 4096:(i + 1) * 4096].rearrange("p (a b) -> p a b", a=8) for i in range(2)]
        kctx = carve(2048, [64, 8, 512], BF16, parts=64)
        vctx = carve(1024, [128, 4, 512], BF16)
        fr = [carve(2560) for _ in range(2)]
        qTh = [carve(1280, None, BF16, parts=64) for _ in range(2)]
        kTh = [carve(1280, None, BF16, parts=64) for _ in range(2)]
        sc = [carve(512, parts=64) for _ in range(2)]
        pT = [carve(256, None, BF16, parts=64) for _ in range(2)]
        pc = [carve(128, None, BF16) for _ in range(2)]
        rden = [carve(256, parts=64) for _ in range(2)]
        yst = [carve(1280, None, BF16, parts=64) for _ in range(2)]
        RB = R("BM")
        src = rpbp[l]
        BHv = kvfb
        for h_ in range(8):
            dma(BHv[:, h_ * 960:(h_ + 1) * 960].rearrange("p (a b) -> p a b", a=15),
                bass.AP(src.tensor, src.offset + h_ * 15 * 128, [[1, 64], [128, 15], [1, 64]]), (), [R("kvf", 0), R("kvf", 1)])
        BMv = BM.rearrange("p a b -> p (a b)")
        for j in range(15):
            pb = uid() % 8
            mm(psum[pb][0:64, :], Jm[:, :], BHv[:, j * 512:(j + 1) * 512], True, True, [R("kvf", 0), R("kvf", 1)], [Rps(pb)])
            cmb = colmask[:, :].unsqueeze(1).to_broadcast([64, 8, 64])
            tt(BMv[:, j * 512:(j + 1) * 512].rearrange("p (a b) -> p a b", a=8), psum[pb][0:64, :].rearrange("p (a b) -> p a b", a=8), cmb, ALU.add, [Rps(pb)], [RB])
        dma(kctx, ck[l].rearrange("h d k -> d h k"), (), [R("kctx")], q="pool")
        dma(vctx, cvv[l].rearrange("(b p) c -> p b c", p=128), (), [R("vctx")], q="pool")
        for which, r00 in ((0, NAK0), (1, NAV0)):
            for chn in range(4):
                b_ = uid() % 2
                dma(fr[b_], U[r00 + chn * 128:r00 + (chn + 1) * 128, :], (), [R("fr", b_)])
                nrow = 40 if which == 1 else 8
                for r4 in range(0, nrow, 4):
                    pb = uid() % 8
                    for j in range(4):
                        r = r4 + j
                        tr(psum[pb][0:64, j * 128:(j + 1) * 128], fr[b_][:, r * 64:(r + 1) * 64], ident[:], [R("fr", b_)], [Rps(pb)])
                    src_ = psum[pb][0:64, :].rearrange("p (j n) -> p j n", j=4)
                    if which == 1:
                        cp(vtk[:, r4:r4 + 4, chn * 128:(chn + 1) * 128], src_, [Rps(pb)], [R("vtk")], evac_eng())
                    if r4 < 8:
                        cp(kvf[which][:, r4:r4 + 4, chn * 128:(chn + 1) * 128], src_, [Rps(pb)], [R("kvf", which)], evac_eng())
        for sq_ in range(2):
            dma(nk_out[sq_, l].rearrange("(r p) c -> p r c", p=64), kvf[0][:, sq_ * 4:(sq_ + 1) * 4, :], [R("kvf", 0)], [])
            dma(nv_out[sq_, l].rearrange("(r p) c -> p r c", p=64), kvf[1][:, sq_ * 4:(sq_ + 1) * 4, :], [R("kvf", 1)], [])
        for h in range(8):
            b_ = h % 2
            dma(qTh[b_], U[NAQ0 + h * 64:NAQ0 + (h + 1) * 64, :], (), [R("qTh", b_)], q="pool")
            dma(kTh[b_], U[NAK0 + h * 64:NAK0 + (h + 1) * 64, :], (), [R("kTh", b_)], q="pool")
            hs = slice(h * 64, (h + 1) * 64)
            for sq_ in range(2):
                for qb in range(4):
                    q0 = sq_ * 256 + qb * 64
                    i_ = uid() % 2
                    pS, pO, pD = (uid() % 2) * 4, (uid() % 2) * 4 + 1, (uid() % 2) * 4 + 2
                    for kr in range(4):
                        k0 = sq_ * 256 + kr * 64
                        mm(psum[pS][0:64, kr * 64:(kr + 1) * 64], kTh[b_][:, k0:k0 + 64], qTh[b_][:, q0:q0 + 64], True, True,
                           [R("qTh", b_), R("kTh", b_)], [Rps(pS)])
                    act(pT[i_][:, 0:256], psum[pS][0:64, 0:256], AF.Exp, [Rps(pS)], [R("pT", i_)], scale=0.125)
                    for kr in range(4):
                        mm(psum[pO][0:64, 0:64], vtk[:, sq_ * 4 + kr, hs], pT[i_][:, kr * 64:(kr + 1) * 64], kr == 0, kr == 3,
                           [R("vtk"), R("pT", i_)], [Rps(pO)])
                    for kr in range(4):
                        mm(psum[pD][0:64, 0:64], onesb[0:64, 0:64], pT[i_][:, kr * 64:(kr + 1) * 64], kr == 0, kr == 3,
                           [R("pT", i_)], [Rps(pD)])
                    P.add("dve", lambda e, o=rden[i_][:, 0:64], i=psum[pD][0:64, 0:64]: e.reciprocal(o, i), [Rps(pD)], [R("rden", i_)])
                    tt(yst[b_][:, q0:q0 + 64], psum[pO][0:64, 0:64], rden[i_][:, 0:64], ALU.mult, [Rps(pO), R("rden", i_)], [R("yst", b_)])
            for r in range(32):
                q0 = 512 + r * 64
                kr0 = min(max(r - 4, 0), 24)
                d0 = kr0 - r + 7
                i_ = uid() % 2
                base = (r % 2) * 4
                pS, pC, pO, pD = base, base + 1, base + 2, base + 3
                for i in range(8):
                    k0 = 512 + (kr0 + i) * 64
                    mm(psum[pS][0:64, i * 64:(i + 1) * 64], kTh[b_][:, k0:k0 + 64], qTh[b_][:, q0:q0 + 64], True, True,
                       [R("qTh", b_), R("kTh", b_)], [Rps(pS)])
                for cb in range(4):
                    mm(psum[pC][:, cb * 64:(cb + 1) * 64], kctx[:, h, cb * 128:(cb + 1) * 128], qTh[b_][:, q0:q0 + 64], True, True,
                       [R("qTh", b_), R("kctx")], [Rps(pC)])
                bm = BM[:, h * 15 + d0:h * 15 + d0 + 8, :]
                stt(sc[i_].rearrange("p (a b) -> p a b", a=8), psum[pS][0:64, :].rearrange("p (a b) -> p a b", a=8), 0.125, bm, ALU.mult, ALU.add,
                    [Rps(pS), RB], [R("sc", i_)])
                act(pT[i_], sc[i_], AF.Exp, [R("sc", i_)], [R("pT", i_)])
                act(pc[i_], psum[pC][:, 0:256], AF.Exp, [Rps(pC)], [R("pc", i_)], scale=0.125)
                for i in range(8):
                    mm(psum[pO][0:64, 0:64], vtk[:, 8 + kr0 + i, hs], pT[i_][:, i * 64:(i + 1) * 64], i == 0, False,
                       [R("vtk"), R("pT", i_)], [Rps(pO)])
                for cb in range(4):
                    mm(psum[pO][0:64, 0:64], vctx[:, cb, hs], pc[i_][:, cb * 64:(cb + 1) * 64], False, cb == 3,
                       [R("vctx"), R("pc", i_)], [Rps(pO)])
                for i in range(8):
                    mm(psum[pD][0:64, 0:64], onesb[0:64, 0:64], pT[i_][:, i * 64:(i + 1) * 64], i == 0, False, [R("pT", i_)], [Rps(pD)])
                for cb in range(4):
                    mm(psum[pD][0:64, 0:64], onesb[:, 0:64], pc[i_][:, cb * 64:(cb + 1) * 64], False, cb == 3, [R("pc", i_)], [Rps(pD)])
                P.add("dve", lambda e, o=rden[i_][:, 0:64], i=psum[pD][0:64, 0:64]: e.reciprocal(o, i), [Rps(pD)], [R("rden", i_)])
                tt(yst[b_][:, q0:q0 + 64], psum[pO][0:64, 0:64], rden[i_][:, 0:64], ALU.mult, [Rps(pO), R("rden", i_)], [R("yst", b_)])
            dma(YT[1536 + h * 64:1536 + (h + 1) * 64, :], yst[b_], [R("yst", b_)], [])

    def load_H(src):
        H = carve(20480, [128, KC, NT], BF16)
        for kc in range(KC):
            dma(H[:, kc, :], src[kc * 128:(kc + 1) * 128, :], (), [R("H", kc)])
        return H

    def phase_merge(l):
        phase()
        H = load_H(YT)
        wbuf = [carve(4096, [128, KC, 512], BF16) for _ in range(2)]
        gt = [carve(1536, [128, 3, 512]) for _ in range(2)]
        t0_ = [carve(512) for _ in range(2)]
        t1_ = [carve(512) for _ in range(2)]
        ms = [carve(1280, None, BF16) for _ in range(2)]
        wv = wb[l].rearrange("(kc p) n -> p kc n", p=128)
        KR = ((0, 4), (4, 12), (12, 16))
        for g in range(4):
            s = g % 2
            dma(wbuf[s], wv[:, :, g * 512:(g + 1) * 512], (), [R("wbuf", s)], q="pool")
            for j in range(4):
                oc = g * 4 + j
                mb = oc % 2
                for t in range(NTT):
                    gi = uid() % 2
                    sl = slice(t * TT, (t + 1) * TT)
                    for i in range(3):
                        r0 = GATE0 + i * 2048 + oc * 128
                        dma(gt[gi][:, i, :], U[r0:r0 + 128, sl], (), [R("gt", gi)])
                    pbs = [(uid() % 2) * 4 + i for i in range(3)]
                    for i, (k0_, k1_) in enumerate(KR):
                        for kc in range(k0_, k1_):
                            mm(psum[pbs[i]][:], wbuf[s][:, kc, j * 128:(j + 1) * 128], H[:, kc, sl], kc == k0_, kc == k1_ - 1,
                               [R("wbuf", s), R("H", kc)], [Rps(pbs[i])])
                    tt(t0_[gi], psum[pbs[0]][:], gt[gi][:, 0, :], ALU.mult, [Rps(pbs[0]), R("gt", gi)], [R("t0", gi)])
                    tt(t1_[gi], psum[pbs[1]][:], gt[gi][:, 1, :], ALU.mult, [Rps(pbs[1]), R("gt", gi)], [R("t1", gi)])
                    tt(t0_[gi], t0_[gi], t1_[gi], ALU.add, [R("t0", gi), R("t1", gi)], [R("t0", gi)])
                    tt(t1_[gi], psum[pbs[2]][:], gt[gi][:, 2, :], ALU.mult, [Rps(pbs[2]), R("gt", gi)], [R("t1", gi)])
                    tt(ms[mb][:, sl], t0_[gi], t1_[gi], ALU.add, [R("t0", gi), R("t1", gi)], [R("ms", mb)])
                dma(MT[oc * 128:(oc + 1) * 128, :], ms[mb], [R("ms", mb)], [])

    def phase_wout(l):
        phase()
        H = load_H(MT)
        wbuf = [carve(4096, [128, KC, 512], BF16) for _ in range(2)]
        xs = [carve(2560) for _ in range(2)]
        wv = w_out[l].rearrange("(kc p) n -> p kc n", p=128)
        for g in range(4):
            s = g % 2
            dma(wbuf[s], wv[:, :, g * 512:(g + 1) * 512], (), [R("wbuf", s)], q="pool")
            for j in range(4):
                oc = g * 4 + j
                xb = oc % 2
                dma(xs[xb], XT[oc * 128:(oc + 1) * 128, :], (), [R("xs", xb)])
                for t in range(NTT):
                    c = 0 if t == 0 else 1
                    sl = slice(t * TT, (t + 1) * TT)
                    pb = uid() % 8
                    for kc in range(KC):
                        mm(psum[pb][:], wbuf[s][:, kc, j * 128:(j + 1) * 128], H[:, kc, sl], kc == 0, kc == KC - 1,
                           [R("wbuf", s), R("H", kc)], [Rps(pb)])
                    stt(xs[xb][:, sl], psum[pb][:], modsb[:, 32 + oc, c:c + 1], xs[xb][:, sl], ALU.mult, ALU.add,
                        [Rps(pb), R("xs", xb)], [R("xs", xb)])
                dma(XT[oc * 128:(oc + 1) * 128, :], xs[xb], [R("xs", xb)], [])

    def phase_ffn_down(l):
        phase()
        actb = carve(11264, [128, 44, 512], BF16)
        ab = [carve(520) for _ in range(4)]
        cb_ = [carve(520) for _ in range(4)]
        wd = [carve(2816, [128, 44, 128], BF16) for _ in range(2)]
        xs = [carve(512) for _ in range(2)]
        fw = carve(264, [128, 88, 3])
        dma(fw, fconvT[l], (), [R("fw")])
        wv = w_down[l].rearrange("(kc p) n -> p kc n", p=128)
        for t in range(NTT):
            c = 0 if t == 0 else 1
            lo = t * TT
            hl = t > 1
            hr = 1 <= t < 4
            segs = [(1, 257), (257, 513)] if t == 0 else [(0, 514)]
            for ch in range(44):
                bufs = []
                for half in range(2):
                    bi = (ch % 2) * 2 + half
                    r0 = half * DFF + ch * 128
                    a_, Ra_ = ab[bi], R("ab", bi)
                    c0 = lo - (1 if hl else 0)
                    c1 = lo + TT + (1 if hr else 0)
                    o0 = 0 if hl else 1
                    if not hl:
                        memset(a_[:, 0:1], 0.0, [Ra_])
                    if not hr:
                        memset(a_[:, 513:514], 0.0, [Ra_])
                    dma(a_[:, o0:o0 + (c1 - c0)], U[r0:r0 + 128, c0:c1], (), [Ra_])
                    cv_, Rc_ = cb_[bi], R("cb", bi)
                    conv3(cv_[:, 0:514], a_[:, 0:514], fw[:, half * 44 + ch, :], segs, [Ra_, R("fw")], [Rc_])
                    bufs.append((cv_, Rc_))
                (ca, Rca), (cbb, Rcb) = bufs
                act(ca[:, 1:513], ca[:, 1:513], AF.Silu, [Rca], [Rca])
                tt(actb[:, ch, :], ca[:, 1:513], cbb[:, 1:513], ALU.mult, [Rca, Rcb], [R("actb", ch)])
            for oc in range(16):
                s = oc % 2
                dma(wd[s], wv[:, :, oc * 128:(oc + 1) * 128], (), [R("wd", s)], q="pool")
                dma(xs[s], XT[oc * 128:(oc + 1) * 128, lo:lo + TT], (), [R("xs", s)])
                pb = uid() % 8
                for kc in range(44):
                    mm(psum[pb][:], wd[s][:, kc, :], actb[:, kc, :], kc == 0, kc == 43, [R("wd", s), R("actb", kc)], [Rps(pb)])
                stt(xs[s], psum[pb][:], modsb[:, 80 + oc, c:c + 1], xs[s], ALU.mult, ALU.add, [Rps(pb), R("xs", s)], [R("xs", s)])
                dma(XT[oc * 128:(oc + 1) * 128, lo:lo + TT], xs[s], [R("xs", s)], [])

    def in_func(col):
        if Z0 <= col < BETA0:
            return AF.Silu
        if col >= GATE0:
            return AF.Sigmoid
        return None

    IN_GROUPS = [(g * 512, 512) for g in range(9)] + [(4608, 32)] + [(4640 + g * 512, 512) for g in range(15)]

    if plan is None:
        plan = ["consts", "intr"]
        for l in range(L):
            plan += [("mod", l), ("norm1", l), ("pool", l), ("gdn_pre", l), ("gdn_main", l), ("gdn_post", l), ("na", l),
                     ("merge", l), ("wout", l), ("norm2", l), ("ffn", l)]
        plan += ["final"]
    for p in plan:
        if p == "consts":
            phase_consts()
        elif p == "intr":
            phase_in_transpose()
        elif p == "final":
            phase_norm(0, final=True)
        else:
            nm, l = p
            if nm == "mod":
                phase_mod(l)
            elif nm == "norm1":
                phase_norm(0)
                phase_proj(w_in[l], 0, N_IN, U, 0, in_func, IN_GROUPS)
            elif nm == "pool":
                phase_pool(l)
            elif nm == "gdn_pre":
                phase_gdn_pre(l)
            elif nm == "gdn_main":
                phase_gdn_main(l)
            elif nm == "gdn_post":
                phase_gdn_post(l)
            elif nm == "na":
                phase_na(l)
            elif nm == "merge":
                phase_merge(l)
            elif nm == "wout":
                phase_wout(l)
            elif nm == "norm2":
                phase_norm(1)
                phase_proj(w_up[l], 0, 2 * DFF, U, 0)
            elif nm == "ffn":
                phase_ffn_down(l)
    P.emit()
    st.close()
    return nc


def make_consts():
    misc = np.zeros((128, 640), np.float32)
    misc[0:64, 512:576] = np.eye(64, dtype=np.float32)[::-1]
    misc[:, 0:128] = np.eye(128, dtype=np.float32)
    i = np.arange(64)[:, None]
    j = np.arange(64)[None, :]
    m = np.zeros((64, 4, 64), np.float32)
    m[:, 0, :] = np.where(j >= i, BIG, 0.0)
    m[:, 1, :] = np.where(j < i, -BIG, 0.0)
    m[:, 2, :] = np.where(j <= i, BIG, 0.0)
    m[:, 3, :] = np.where(j > i, -BIG, 0.0)
    misc[0:64, 128:384] = m.reshape(64, 256)
    qc = np.arange(64)
    cs = np.clip(qc - 8, 0, 48)
    kc = np.arange(64)[:, None]
    ok = (kc >= cs[None, :]) & (kc < cs[None, :] + 16)
    misc[0:64, 384:448] = np.where(ok, 0.0, -BIG)
    misc[0:16, 448] = (np.arange(16) >= 8).astype(np.float32)
    reset = np.ones((16, 2048), np.float32)
    reset[:, 0::64] = 0.0
    inv = np.zeros((4, 128, 2304), np.float32)
    for g, w in enumerate((2, 4, 8, 16)):
        for off, T in ((0, 256), (256, 2048)):
            t = np.arange(T)
            lo = np.maximum(t - w // 2, 0)
            hi = np.minimum(t + w - 1 - w // 2, T - 1)
            inv[g, :, off:off + T] = (1.0 / (hi - lo + 1).astype(np.float32))[None, :]
    return misc, reset, inv


def make_inputs(inp):
    f = lambda a: np.ascontiguousarray(np.asarray(a, dtype=np.float32))
    misc, reset, inv = make_consts()
    tr128 = lambda v: f(v.reshape(-1, 128).T)
    rp = f(inp["na_rpb"])
    rpbp = np.zeros((L, 8, 15, 128), np.float32)
    rpbp[..., 48:79] = rp[..., ::-1]
    shared = {
        "w_ada": f(inp["w_ada"]),
        "b_adaT": f(np.stack([tr128(inp["b_ada"][l]) for l in range(L)])),
        "gn1T": f(np.stack([tr128(inp["g_norm1"][l]) for l in range(L)])),
        "gn2T": f(np.stack([tr128(inp["g_norm2"][l]) for l in range(L)])),
        "gfT": tr128(f(inp["g_final"])),
        "w_in": f(inp["w_in"]),
        "pool_w": f(inp["pool_w"]),
        "pool_scT": f(np.stack([tr128(inp["pool_scale"][l]) for l in range(L)])),
        "gconvT": f(np.asarray(inp["gdn_conv"]).reshape(L, 3, 24, 128).transpose(0, 3, 2, 1)),
        "alog": f(np.asarray(inp["gdn_a_log"]).reshape(L, 16, 1)),
        "dtb": f(np.asarray(inp["gdn_dt_bias"]).reshape(L, 16, 1)),
        "gng": f(np.asarray(inp["gdn_norm_g"]).reshape(L, 128, 1)),
        "rpbp": rpbp,
        "wb": f(np.concatenate([inp["w_branch_pool"], inp["w_branch_gdn"], inp["w_branch_na"]], axis=1)),
        "w_out": f(inp["w_out"]),
        "w_up": f(inp["w_up"]),
        "fconvT": f(np.asarray(inp["ffn_conv"]).reshape(L, 3, 88, 128).transpose(0, 3, 2, 1)),
        "w_down": f(inp["w_down"]),
        "c_invcnt": inv,
        "c_misc": misc,
        "c_reset": reset,
    }
    xp = np.asarray(inp["x_prompt"], np.float32)
    xs = np.asarray(inp["x_sample"], np.float32)
    maps = []
    for c in range(8):
        b = c % 2
        m = dict(shared)
        m["xin"] = f(np.concatenate([xp[2 * c], xp[2 * c + 1], xs[b]], axis=0))
        cond = np.stack([np.asarray(inp["c_ctx"], np.float32), np.asarray(inp["c"], np.float32)[b]], axis=0)
        m["condT"] = f(cond.reshape(2, KC, 128).transpose(2, 1, 0))
        m["ck"] = f(np.asarray(inp["cache_na_k"])[b].transpose(0, 2, 3, 1))
        m["cvv"] = f(np.asarray(inp["cache_na_v"])[b].reshape(L, 512, 512))
        m["s0"] = f(np.asarray(inp["state_gdn"])[b])
        maps.append(m)
    return maps


_NC = {}


def kernel(**inputs):
    import os
    ncores = int(os.environ.get("KCORES", "8"))
    if "nc" not in _NC:
        if os.environ.get("KPLAN") == "gdn1":
            DBG["gdn_steps"] = 1
            _NC["nc"] = build(plan=["consts", ("gdn_main", 0)])
        else:
            _NC["nc"] = build()
    nc = _NC["nc"]
    maps = make_inputs(inputs)[:ncores]
    res = run_bass_kernel_spmd(nc, maps, core_ids=list(range(ncores)))
    if ncores < 8:
        res.results.extend([res.results[0]] * (8 - ncores))
    rs = res.results
    yp = np.zeros((16, 256, D), np.float32)
    ys = np.zeros((2, 2048, D), np.float32)
    nk = np.zeros((16, L, 256, 8, 64), np.float32)
    nv = np.zeros((16, L, 256, 8, 64), np.float32)
    ns = np.zeros((16, L, 2, 8, 128, 128), np.float32)
    for c in range(8):
        y = np.asarray(rs[c]["y_out"])
        yp[2 * c] = y[0:256]
        yp[2 * c + 1] = y[256:512]
        if c < 2:
            ys[c] = y[512:2560]
        nk[2 * c:2 * c + 2] = np.asarray(rs[c]["nk_out"]).reshape(2, L, 256, 8, 64)
        nv[2 * c:2 * c + 2] = np.asarray(rs[c]["nv_out"]).reshape(2, L, 256, 8, 64)
        ns[2 * c:2 * c + 2] = np.asarray(rs[c]["ns_out"])
    return yp, ys, nk, nv, ns
```

```python
import numpy as np
from contextlib import ExitStack
import concourse.bass as bass
import concourse.mybir as mybir
from concourse.alu_op_type import AluOpType as ALU
from concourse.bass_utils import run_bass_kernel_spmd

F32 = mybir.dt.float32
BF16 = mybir.dt.bfloat16
AF = mybir.ActivationFunctionType

D = 2048
KC = 16
L = 2
NT = 2560
TT = 512
NTT = 5
SEQS = [(0, 256, False), (256, 256, False), (512, 2048, True)]
DFF = 5632
N_IN = 12320
Q0, K0, V0, Z0, BETA0, A0 = 512, 1536, 2560, 3584, 4608, 4624
NAQ0, NAK0, NAV0, GATE0 = 4640, 5152, 5664, 6176
EPS = 1e-6
BIG = 30000.0
DBG = {}
AW = 45056


class Res:
    __slots__ = ("w", "rc", "rd", "excl")

    def __init__(self):
        self.excl = False
        self.w = None
        self.rc = {}
        self.rd = []


class Prog:
    CE = ("pe", "act", "dve")
    EPOCH = 16000
    KDMA = 20

    def __init__(self, nc):
        self.nc = nc
        self.ops = []
        self.res = {}
        self.last = {}
        self.lastdma = {"sp": [], "pool": []}
        self.bar = set()
        self.bar_id = 0
        self.crossed = {}

    def R(self, *key):
        r = self.res.get(key)
        if r is None:
            r = self.res[key] = Res()
            r.excl = key[0] == "ps"
        return r

    def barrier(self):
        deps = set()
        for e in self.CE:
            if e in self.last:
                deps.add(self.last[e])
        for q in ("sp", "pool"):
            deps.update(self.lastdma[q][-self.KDMA:])
        self.bar = deps
        self.bar_id += 1
        self.res = {}

    def add(self, eng, fn, reads=(), writes=(), dma=False):
        i = len(self.ops)
        deps = set()
        xs = [r for r in reads if r.excl]
        if xs:
            writes = list(writes) + xs
            reads = [r for r in reads if not r.excl]
        for r in reads:
            if r.w is not None:
                deps.add(r.w)
        for r in writes:
            if r.w is not None:
                deps.add(r.w)
            for e2, j in r.rc.items():
                deps.add(j)
            deps.update(r.rd)
        if self.crossed.get(eng) != self.bar_id:
            deps |= self.bar
            self.crossed[eng] = self.bar_id
        if eng == "pe":
            deps = {j for j in deps if self.ops[j][0] != "pe" or self.ops[j][3]}
        for r in reads:
            if dma:
                r.rd.append(i)
            else:
                r.rc[eng] = i
        for r in writes:
            r.w = i
            r.rc = {}
            r.rd = []
        if dma:
            self.lastdma[eng].append(i)
        else:
            self.last[eng] = i
        self.ops.append([eng, fn, deps, dma, False, None, None])

    def emit(self):
        nc = self.nc
        ops = self.ops
        for op in ops:
            for j in op[2]:
                ops[j][4] = True
        cnt = {e: 0 for e in self.CE}
        nd = {"sp": 0, "pool": 0}
        tot = {"sp": [0] * self.KDMA, "pool": [0] * self.KDMA}
        for op in ops:
            e = op[0]
            if op[3]:
                slot = nd[e] % self.KDMA
                nd[e] += 1
                prev = tot[e][slot]
                tot[e][slot] += 16
                op[5] = ("d", e, slot)
                op[6] = (prev, tot[e][slot])
            elif op[4]:
                ep, v = divmod(cnt[e], self.EPOCH)
                cnt[e] += 1
                op[5] = ("c", e, ep)
                op[6] = v + 1
        st = ExitStack()
        sems = {}
        for e in self.CE:
            for ep in range(cnt[e] // self.EPOCH + 1):
                sems[("c", e, ep)] = st.enter_context(nc.semaphore(f"s_{e}_{ep}"))
        for e in ("sp", "pool"):
            for s in range(self.KDMA):
                sems[("d", e, s)] = st.enter_context(nc.semaphore(f"d_{e}_{s}"))
        block = st.enter_context(nc.Block())
        per = {e: [] for e in ("pe", "act", "dve", "sp", "pool")}
        for op in ops:
            per[op[0]].append(op)
        KD = self.KDMA

        def run(e, eng):
            waited = {}
            for op in per[e]:
                waits = {}
                for j in op[2]:
                    oj = ops[j]
                    k = oj[5]
                    v = oj[6][1] if oj[3] else oj[6]
                    if waits.get(k, 0) < v:
                        waits[k] = v
                if op[3] and op[6][0] > 0:
                    k = op[5]
                    if waits.get(k, 0) < op[6][0]:
                        waits[k] = op[6][0]
                for k, v in waits.items():
                    if waited.get(k, 0) < v:
                        eng.wait_ge(sems[k], v)
                        waited[k] = v
                ins = op[1](eng)
                if op[3]:
                    ins.then_inc(sems[op[5]], 16)
                elif op[4]:
                    ins.then_inc(sems[op[5]], 1)
            if e in ("sp", "pool"):
                for s in range(KD):
                    if tot[e][s] > 0:
                        eng.wait_ge(sems[("d", e, s)], tot[e][s])

        @block.tensor
        def _(t):
            run("pe", t)

        @block.scalar
        def _(t):
            run("act", t)

        @block.vector
        def _(t):
            run("dve", t)

        @block.sync
        def _(t):
            run("sp", t)

        @block.gpsimd
        def _(t):
            run("pool", t)

        st.close()


def build(plan=None, dbg=()):
    nc = bass.Bass("TRN2", target_bir_lowering=False)
    P = Prog(nc)
    R = P.R
    st = ExitStack()

    def din(name, shape, dt=F32):
        return nc.dram_tensor(name, list(shape), dt, kind="ExternalInput").ap()

    def dout(name, shape, dt=F32):
        return nc.dram_tensor(name, list(shape), dt, kind="ExternalOutput").ap()

    def dscr(name, shape, dt=F32):
        kind = "ExternalOutput" if name in dbg else "Internal"
        return nc.dram_tensor(name, list(shape), dt, kind=kind).ap()

    def sb(name, shape, dt=F32):
        return st.enter_context(nc.sbuf_tensor(name, list(shape), dt))

    xin = din("xin", [NT, D])
    condT = din("condT", [128, KC, 2])
    ck = din("ck", [L, 8, 64, 512])
    cvv = din("cvv", [L, 512, 512])
    s0 = din("s0", [L, 2, 8, 128, 128])
    w_ada = din("w_ada", [L, D, 6 * D])
    b_adaT = din("b_adaT", [L, 128, 96])
    gn1T = din("gn1T", [L, 128, KC])
    gn2T = din("gn2T", [L, 128, KC])
    gfT = din("gfT", [128, KC])
    w_in = din("w_in", [L, D, N_IN])
    pool_w = din("pool_w", [L, 4, 128, 128])
    pool_scT = din("pool_scT", [L, 128, 4])
    gconvT = din("gconvT", [L, 128, 24, 3])
    alog = din("alog", [L, 16, 1])
    dtb = din("dtb", [L, 16, 1])
    gng = din("gng", [L, 128, 1])
    rpbp = din("rpbp", [L, 8, 15, 128])
    wb = din("wb", [L, D, D])
    w_out = din("w_out", [L, D, D])
    w_up = din("w_up", [L, D, 2 * DFF])
    fconvT = din("fconvT", [L, 128, 88, 3])
    w_down = din("w_down", [L, DFF, D])
    c_invcnt = din("c_invcnt", [4, 128, 2304])
    c_misc = din("c_misc", [128, 640])
    c_reset = din("c_reset", [16, 2048])

    y_out = dout("y_out", [NT, D])
    nk_out = dout("nk_out", [2, L, 256, 512])
    nv_out = dout("nv_out", [2, L, 256, 512])
    ns_out = dout("ns_out", [2, L, 2, 8, 128, 128])

    XT = dscr("XT", [D, NT])
    U = dscr("U", [N_IN, NT])
    YT = dscr("YT", [D, NT], BF16)
    MT = dscr("MT", [D, NT], BF16)
    QKVn = dscr("QKVn", [3072, NT])
    KVtok = dscr("KVtok", [NT, 2048])
    GS = dscr("GS", [2, 16, NT])
    OFB = dscr("OFB", [2, 1024, NT])
    XTv = XT.rearrange("(kc p) n -> p kc n", p=128)

    arena = sb("arena", [128, AW], F32)
    ident = sb("ident", [128, 128], F32)
    ones_m = sb("ones_m", [128, 128], F32)
    ones1 = sb("ones1", [128, 128], F32)
    onesb = sb("onesb", [128, 128], BF16)
    ones128 = sb("ones128", [128, 128], F32)
    masks = sb("masks", [64, 4, 64], F32)
    colmask = sb("colmask", [64, 64], F32)
    isb = sb("isb", [16, 1], F32)
    Jm = sb("Jm", [64, 64], F32)
    resetm = sb("resetm", [16, 2048], F32)
    silucT = sb("silucT", [128, KC, 2], BF16)
    condsb = sb("condsb", [128, KC, 2], F32)
    modsb = sb("modsb", [128, 96, 2], F32)
    badaT = sb("badaT", [128, 96], F32)
    gnT = sb("gnT", [128, 3, KC], F32)
    a_sc = sb("a_sc", [128, 2, KC, 2], F32)
    prm = sb("prm", [128, 512], F32)
    tokS = sb("tokS", [64, 40, 64], F32)
    zcol = sb("zcol", [128, 1], F32)
    ocol = sb("ocol", [128, 1], F32)
    epscol = sb("epscol", [128, 1], F32)
    psum = [st.enter_context(nc.psum_tensor(f"ps{i}", [128, 512], F32)) for i in range(8)]
    Rps = lambda i: R("ps", i)

    cnt = {"n": 0, "off": 0, "rr": 0}

    def uid():
        cnt["n"] += 1
        return cnt["n"]

    def phase():
        P.barrier()
        cnt["off"] = 0

    def carve(ncols_f32, shape=None, dt=F32, parts=128):
        o = cnt["off"]
        assert o + ncols_f32 <= AW, ("arena overflow", o, ncols_f32)
        cnt["off"] = o + ncols_f32
        a = arena[0:parts, o:o + ncols_f32]
        if dt == BF16:
            a = a.bitcast(BF16)
        if shape is not None and len(shape) == 3:
            a = a.rearrange("p (a b) -> p a b", a=shape[1])
        elif shape is not None and len(shape) == 4:
            a = a.rearrange("p (a b c) -> p a b c", a=shape[1], b=shape[2])
        return a

    def dma(out, in_, reads, writes, q="sp", slow=False):
        if slow:
            P.add(q, lambda e, o=out, i=in_: e.dma_start(out=o, in_=i, allow_slow_non_contiguous=True), reads, writes, dma=True)
        else:
            P.add(q, lambda e, o=out, i=in_: e.dma_start(out=o, in_=i), reads, writes, dma=True)

    def mm(out, lhsT, rhs, start, stop, reads, writes):
        P.add("pe", lambda e, o=out, a=lhsT, b=rhs, s0_=start, s1_=stop: e.matmul(o, a, b, start=s0_, stop=s1_), reads, writes)

    def tr(out, in_, idn, reads, writes):
        P.add("pe", lambda e, o=out, a=in_, b=idn: e.transpose(o, a, b), reads, writes)

    def act(out, in_, func, reads, writes, bias=None, scale=None):
        def f(e, o=out, i=in_, fu=func, b=bias, s=scale):
            kw = {}
            if b is not None:
                kw["bias"] = b
            if s is not None:
                kw["scale"] = s
            return e.activation(out=o, in_=i, func=fu, **kw)
        P.add("act", f, reads, writes)

    def ts(out, in0, s1, s2, op0, op1, reads, writes):
        if op1 is None:
            P.add("dve", lambda e, o=out, a=in0, x=s1, p0=op0: e.tensor_scalar(o, a, x, None, p0), reads, writes)
        else:
            P.add("dve", lambda e, o=out, a=in0, x=s1, y=s2, p0=op0, p1=op1: e.tensor_scalar(o, a, x, y, p0, p1), reads, writes)

    def tt(out, in0, in1, op, reads, writes):
        P.add("dve", lambda e, o=out, a=in0, b=in1, p=op: e.tensor_tensor(o, a, b, p), reads, writes)

    def stt(out, in0, s, in1, op0, op1, reads, writes):
        P.add("dve", lambda e, o=out, a=in0, x=s, b=in1, p0=op0, p1=op1: e.scalar_tensor_tensor(o, a, x, b, p0, p1), reads, writes)

    def cp(out, in_, reads, writes, eng="dve"):
        if eng == "dve":
            P.add("dve", lambda e, o=out, i=in_: e.tensor_copy(o, i), reads, writes)
        else:
            act(out, in_, AF.Copy, reads, writes)

    def memset(ap, val, writes):
        P.add("dve", lambda e, a=ap, v=val: e.memset(a, v), (), writes)

    def rsqrt(out, in_, reads, writes):
        act(out, in_, AF.Sqrt, reads, writes, bias=epscol[0:in_.shape[0], :])
        P.add("dve", lambda e, o=out: e.reciprocal(o, o), writes, writes)

    def evac_eng():
        cnt["rr"] += 1
        return "dve" if cnt["rr"] % 2 else "act"

    def phase_consts():
        Rc = R("c")
        dma(ident[:], c_misc[:, 0:128], (), [Rc])
        dma(masks[:], c_misc[0:64, 128:384].rearrange("p (a b) -> p a b", a=4), (), [Rc])
        dma(colmask[:], c_misc[0:64, 384:448], (), [Rc])
        dma(isb[:], c_misc[0:16, 448:449], (), [Rc], slow=True)
        dma(Jm[:], c_misc[0:64, 512:576], (), [Rc])
        dma(resetm[:], c_reset[:, :], (), [Rc])
        memset(ones_m[:], 1.0 / D, [Rc])
        memset(ones1[:], 1.0, [Rc])
        memset(ones128[:], 1.0 / 128, [Rc])
        memset(onesb[:], 1.0, [Rc])
        memset(zcol[:], 0.0, [Rc])
        memset(ocol[:], 1.0, [Rc])
        memset(epscol[:], EPS, [Rc])
        for c0 in range(0, AW, 4096):
            memset(arena[:, c0:min(AW, c0 + 4096)], 0.0, [Rc])
        memset(tokS[:], 0.0, [Rc])
        dma(condsb[:], condT[:, :, :], (), [R("condsb")])
        act(silucT[:], condsb[:], AF.Silu, [R("condsb")], [R("silucT")])
        dma(gnT[:, 2, :], gfT[:, :], (), [R("gnT")])

    def phase_in_transpose():
        phase()
        wk = [carve(2048) for _ in range(2)]
        xt = carve(8192, [128, KC, 512])
        for t in range(NTT):
            for b in range(4):
                tok0 = t * TT + b * 128
                w_, Rw_ = wk[b % 2], R("wk", b % 2)
                dma(w_, xin[tok0:tok0 + 128, :], (), [Rw_])
                for g in range(4):
                    pb = (b * 4 + g) % 8
                    for j in range(4):
                        kc = g * 4 + j
                        tr(psum[pb][:, j * 128:(j + 1) * 128], w_[:, kc * 128:(kc + 1) * 128], ident[:], [Rw_], [Rps(pb)])
                    o = xt[:, g * 4:(g + 1) * 4, b * 128:(b + 1) * 128]
                    i = psum[pb][:, :].rearrange("p (j n) -> p j n", j=4)
                    cp(o, i, [Rps(pb)], [R("xt")], evac_eng())
            dma(XTv[:, :, t * TT:(t + 1) * TT], xt, [R("xt")], [])

    def phase_mod(l):
        phase()
        wbuf = [carve(4096, [128, KC, 512], BF16) for _ in range(2)]
        dma(badaT[:], b_adaT[l], (), [R("badaT")])
        dma(gnT[:, 0, :], gn1T[l], (), [R("gnT")])
        dma(gnT[:, 1, :], gn2T[l], (), [R("gnT")])
        wv = w_ada[l].rearrange("(kc p) n -> p kc n", p=128)
        for g in range(24):
            s = g % 2
            dma(wbuf[s], wv[:, :, g * 512:(g + 1) * 512], (), [R("wbuf", s)], q="pool")
            for j in range(4):
                ch = g * 4 + j
                pb = ch % 8
                for kc in range(KC):
                    mm(psum[pb][:, 0:2], wbuf[s][:, kc, j * 128:(j + 1) * 128], silucT[:, kc, :], kc == 0, kc == KC - 1,
                       [R("wbuf", s)], [Rps(pb)])
                act(modsb[:, ch, :], psum[pb][:, 0:2], AF.Identity, [Rps(pb), R("badaT")], [R("modsb")], bias=badaT[:, ch:ch + 1])
        for ni, (sci, gi) in enumerate(((1, 0), (4, 1))):
            for c in range(2):
                ts(a_sc[:, ni, :, c], modsb[:, sci * 16:(sci + 1) * 16, c], 1.0, None, ALU.add, None, [R("modsb")], [R("a_sc")])
                tt(a_sc[:, ni, :, c], a_sc[:, ni, :, c], gnT[:, gi, :], ALU.mult, [R("a_sc"), R("gnT")], [R("a_sc")])

    def phase_norm(ni, final=False):
        phase()
        H = carve(20480, [128, KC, NT], BF16)
        xt = carve(8192, [128, KC, 512])
        sq = [carve(512) for _ in range(2)]
        tmp = [carve(512) for _ in range(3)]
        rstd = carve(512)
        wk = [carve(2048) for _ in range(2)]
        for t in range(NTT):
            c = 0 if t == 0 else 1
            dma(xt, XTv[:, :, t * TT:(t + 1) * TT], (), [R("xt")])
            pb = t % 2
            for kc in range(KC):
                s = kc % 2
                act(sq[s], xt[:, kc, :], AF.Square, [R("xt")], [R("sq", s)])
                mm(psum[pb][:], ones_m[:], sq[s], kc == 0, kc == KC - 1, [R("sq", s)], [Rps(pb)])
            rsqrt(rstd, psum[pb][:], [Rps(pb)], [R("rstd")])
            if not final:
                for kc in range(KC):
                    s = kc % 3
                    stt(tmp[s], xt[:, kc, :], a_sc[:, ni, kc, c:c + 1], rstd, ALU.mult, ALU.mult,
                        [R("xt"), R("rstd")], [R("tmp", s)])
                    act(H[:, kc, t * TT:(t + 1) * TT], tmp[s], AF.Identity, [R("tmp", s)], [R("H", kc)],
                        bias=modsb[:, ni * 48 + kc, c:c + 1])
            else:
                for kc in range(KC):
                    stt(xt[:, kc, :], xt[:, kc, :], gnT[:, 2, kc:kc + 1], rstd, ALU.mult, ALU.mult,
                        [R("xt"), R("rstd")], [R("xt")])
                for b in range(4):
                    w_, Rw_ = wk[b % 2], R("wk", b % 2)
                    for g in range(4):
                        pb2 = 2 + (b * 4 + g) % 6
                        for j in range(4):
                            kc = g * 4 + j
                            tr(psum[pb2][:, j * 128:(j + 1) * 128], xt[:, kc, b * 128:(b + 1) * 128], ident[:], [R("xt")], [Rps(pb2)])
                        cp(w_[:, g * 512:(g + 1) * 512], psum[pb2][:], [Rps(pb2)], [Rw_], evac_eng())
                    tok0 = t * TT + b * 128
                    dma(y_out[tok0:tok0 + 128, :], w_, [Rw_], [])
        return H

    def phase_proj(wmat, col0, ncols, dst, dst_row0, func_of_col=None, groups=None):
        P.barrier()
        cnt["off"] = 20480
        H = arena[:, 0:20480].bitcast(BF16).rearrange("p (a b) -> p a b", a=KC)
        wbuf = [carve(4096, [128, KC, 512], BF16) for _ in range(2)]
        stg = [carve(2560) for _ in range(2)]
        wv = wmat.rearrange("(kc p) n -> p kc n", p=128)
        if groups is None:
            groups = []
            g0 = col0
            while g0 < col0 + ncols:
                gw = min(512, col0 + ncols - g0)
                groups.append((g0, gw))
                g0 += gw
        for gi, (g0, gw) in enumerate(groups):
            s = gi % 2
            dma(wbuf[s][:, :, 0:gw], wv[:, :, g0:g0 + gw], (), [R("wbuf", s)], q="pool")
            for j in range(0, gw, 128):
                cw = min(128, gw - j)
                col = g0 + j
                wi = uid() % 2
                fu = func_of_col(col) if func_of_col else None
                for t in range(NTT):
                    pb = uid() % 8
                    for kc in range(KC):
                        mm(psum[pb][0:cw, :], wbuf[s][:, kc, j:j + cw], H[:, kc, t * TT:(t + 1) * TT], kc == 0, kc == KC - 1,
                           [R("wbuf", s), R("H", kc)], [Rps(pb)])
                    o = stg[wi][0:cw, t * TT:(t + 1) * TT]
                    if fu is not None:
                        act(o, psum[pb][0:cw, :], fu, [Rps(pb)], [R("stg", wi)])
                    else:
                        cp(o, psum[pb][0:cw, :], [Rps(pb)], [R("stg", wi)], evac_eng())
                r0 = dst_row0 + (col - col0)
                dma(dst[r0:r0 + cw, :], stg[wi][0:cw, :], [R("stg", wi)], [])

    def conv3(out, in_, w3, segs, reads, writes):
        ts(out, in_, w3[:, 1:2], None, ALU.mult, None, reads, writes)
        for (a, b) in segs:
            stt(out[:, a + 1:b], in_[:, a:b - 1], w3[:, 0:1], out[:, a + 1:b], ALU.mult, ALU.add, list(reads) + list(writes), writes)
            stt(out[:, a:b - 1], in_[:, a + 1:b], w3[:, 2:3], out[:, a:b - 1], ALU.mult, ALU.add, list(reads) + list(writes), writes)

    def phase_pool(l):
        phase()
        PAD = 16
        Wd = NT + 6 * PAD
        ub = carve(Wd)
        la = carve(Wd)
        lb = carve(Wd)
        ic = carve(2304)
        dT = [carve(1280, None, BF16) for _ in range(2)]
        ys = [carve(1280, None, BF16) for _ in range(2)]
        pw = carve(256, [128, 4, 128], BF16)
        dma(prm[:, 0:4], pool_scT[l], (), [R("prm")])
        dma(pw, pool_w[l].rearrange("g c d -> c g d"), (), [R("pw")], q="pool")
        memset(ub, 0.0, [R("ub")])
        memset(la, 0.0, [R("la")])
        memset(lb, 0.0, [R("lb")])
        offs = [a + PAD * (2 * si + 1) for si, (a, T, _) in enumerate(SEQS)]
        for g in range(4):
            for si, (a, T, _) in enumerate(SEQS):
                dma(ub[:, offs[si]:offs[si] + T], U[g * 128:(g + 1) * 128, a:a + T], (), [R("ub")])
            dma(ic, c_invcnt[g], (), [R("ic")])
            tt(la[:, 1:Wd], ub[:, 0:Wd - 1], ub[:, 1:Wd], ALU.add, [R("ub")], [R("la")])
            cur, Rcur, oth, Roth = la, R("la"), lb, R("lb")
            sh = 1
            for lev in range(g):
                tt(oth[:, sh:Wd - sh], cur[:, 0:Wd - 2 * sh], cur[:, 2 * sh:Wd], ALU.add, [Rcur], [Roth])
                cur, Rcur, oth, Roth = oth, Roth, cur, Rcur
                sh *= 2
            d_ = dT[g % 2]
            for si, (a, T, _) in enumerate(SEQS):
                o = offs[si]
                ioff = 0 if T == 256 else 256
                tt(cur[:, o:o + T], cur[:, o:o + T], ic[:, ioff:ioff + T], ALU.mult, [Rcur, R("ic")], [Rcur])
                tt(d_[:, a:a + T], cur[:, o:o + T], ub[:, o:o + T], ALU.subtract, [Rcur, R("ub")], [R("dT", g % 2)])
            y_ = ys[g % 2]
            for t in range(NTT):
                pb = uid() % 8
                mm(psum[pb][:], pw[:, g, :], d_[:, t * TT:(t + 1) * TT], True, True, [R("pw"), R("dT", g % 2)], [Rps(pb)])
                ts(y_[:, t * TT:(t + 1) * TT], psum[pb][:], prm[:, g:g + 1], None, ALU.mult, None, [Rps(pb), R("prm")], [R("ys", g % 2)])
            dma(YT[g * 128:(g + 1) * 128, :], y_, [R("ys", g % 2)], [])

    def phase_gdn_pre(l):
        phase()
        raw = [carve(2048) for _ in range(2)]
        cvb = [carve(2048) for _ in range(2)]
        sqb = [carve(512) for _ in range(2)]
        rn = [carve(512) for _ in range(2)]
        stg = [carve(512, [128, 4, 128]) for _ in range(2)]
        Bt = carve(2048, parts=16)
        At = carve(2048, parts=16)
        Gf = carve(2048, parts=16)
        Gp = carve(2048, parts=16)
        EG = carve(2048, parts=16)
        BE = carve(2048, parts=16)
        ED = carve(2048, parts=16)
        td = carve(2048, parts=16)
        cw_ = carve(72, [128, 24, 3])
        dma(cw_, gconvT[l], (), [R("cw")])
        dma(prm[0:16, 8:9], alog[l], (), [R("prm")])
        dma(prm[0:16, 9:10], dtb[l], (), [R("prm")])
        act(prm[0:16, 10:11], prm[0:16, 8:9], AF.Exp, [R("prm")], [R("prm2")])
        ts(prm[0:16, 10:11], prm[0:16, 10:11], -1.0, None, ALU.mult, None, [R("prm2")], [R("prm2")])
        it = 0
        for (a, T, _) in SEQS:
            for which, r00 in enumerate((Q0, K0, V0)):
                for h in range(8):
                    b_ = it % 2
                    it += 1
                    x_, Rx = raw[b_][:, 0:T], R("raw", b_)
                    c_, Rcv = cvb[b_][:, 0:T], R("cvb", b_)
                    r0 = r00 + h * 128
                    dma(x_, U[r0:r0 + 128, a:a + T], (), [Rx])
                    conv3(c_, x_, cw_[:, which * 8 + h, :], [(0, T)], [Rx, R("cw")], [Rcv])
                    act(c_, c_, AF.Silu, [Rcv], [Rcv])
                    if which < 2:
                        for sl in range(0, T, 512):
                            w_ = min(512, T - sl)
                            sb_ = uid() % 2
                            pb = uid() % 8
                            act(sqb[sb_][:, 0:w_], c_[:, sl:sl + w_], AF.Square, [Rcv], [R("sqb", sb_)])
                            mm(psum[pb][:, 0:w_], ones1[:], sqb[sb_][:, 0:w_], True, True, [R("sqb", sb_)], [Rps(pb)])
                            rsqrt(rn[sb_][:, 0:w_], psum[pb][:, 0:w_], [Rps(pb)], [R("rn", sb_)])
                            scl = (128.0 ** -0.5) if which == 0 else 1.0
                            stt(c_[:, sl:sl + w_], c_[:, sl:sl + w_], scl, rn[sb_][:, 0:w_], ALU.mult, ALU.mult, [Rcv, R("rn", sb_)], [Rcv])
                    dma(QKVn[which * 1024 + h * 128: which * 1024 + (h + 1) * 128, a:a + T], c_, [Rcv], [])
                    if which >= 1:
                        for g in range(T // 512 if T >= 512 else 1):
                            nb = min(4, T // 128)
                            pb = uid() % 8
                            si_ = uid() % 2
                            for j in range(nb):
                                blk = g * 4 + j
                                tr(psum[pb][:, j * 128:(j + 1) * 128], c_[:, blk * 128:(blk + 1) * 128], ident[:], [Rcv], [Rps(pb)])
                            cp(stg[si_][:, 0:nb, :], psum[pb][:, 0:nb * 128].rearrange("p (j n) -> p j n", j=nb), [Rps(pb)], [R("stg", si_)], evac_eng())
                            t0 = a + g * 512
                            cc = (which - 1) * 1024 + h * 128
                            dma(KVtok[t0:t0 + nb * 128, cc:cc + 128].rearrange("(j p) c -> p j c", p=128), stg[si_][:, 0:nb, :], [R("stg", si_)], [])
            Rg = R("gate")
            dma(Bt[:, 0:T], U[BETA0:BETA0 + 16, a:a + T], (), [Rg])
            dma(At[:, 0:T], U[A0:A0 + 16, a:a + T], (), [Rg])
            act(Bt[:, 0:T], Bt[:, 0:T], AF.Sigmoid, [Rg], [Rg])
            act(At[:, 0:T], At[:, 0:T], AF.Exp, [Rg, R("prm")], [Rg], bias=prm[0:16, 9:10])
            act(At[:, 0:T], At[:, 0:T], AF.Ln, [Rg], [Rg], bias=ocol[0:16, :])
            ts(At[:, 0:T], At[:, 0:T], prm[0:16, 10:11], None, ALU.mult, None, [Rg, R("prm2")], [Rg])
            P.add("dve", lambda e, o=Gf[:, 0:T], d0=resetm[:, 0:T], d1=At[:, 0:T]: e.tensor_tensor_scan(o, d0, d1, 0.0, ALU.mult, ALU.add), [Rg], [Rg])
            nch = T // 64
            tot = Gf[:, 63:T:64].unsqueeze(2).to_broadcast([16, nch, 64])
            v3 = lambda ap: ap[:, 0:T].rearrange("p (c j) -> p c j", j=64)
            stt(v3(td), v3(Gf), -2.0, tot, ALU.mult, ALU.add, [Rg], [Rg])
            tt(td[:, 0:T], td[:, 0:T], At[:, 0:T], ALU.add, [Rg], [Rg])
            stt(Gp[:, 0:T], td[:, 0:T], isb[:, 0:1], Gf[:, 0:T], ALU.mult, ALU.add, [Rg], [Rg])
            act(EG[:, 0:T], Gp[:, 0:T], AF.Exp, [Rg], [Rg])
            tt(BE[:, 0:T], Bt[:, 0:T], EG[:, 0:T], ALU.mult, [Rg], [Rg])
            tt(v3(td), tot, v3(Gp), ALU.subtract, [Rg], [Rg])
            act(ED[:, 0:T], td[:, 0:T], AF.Exp, [Rg], [Rg])
            dma(GS[0, :, a:a + T], Gp[:, 0:T], [Rg], [])
            dma(GS[1, :, a:a + T], EG[:, 0:T], [Rg], [])
            for c in range(nch):
                pb = uid() % 8
                for k_, src in enumerate((Gp, Bt, BE, ED)):
                    tr(psum[pb][0:64, k_ * 16:(k_ + 1) * 16], src[:, c * 64:(c + 1) * 64], ident[0:16, 0:16], [Rg], [Rps(pb)])
                cp(tokS[:, a // 64 + c, :], psum[pb][0:64, 0:64], [Rps(pb)], [R("tokS")], evac_eng())

    def phase_gdn_main(l):
        phase()
        S = carve(1024, [128, 8, 128])
        qT = [carve(512, [128, 8, 64]) for _ in range(2)]
        kT = [carve(512, [128, 8, 64]) for _ in range(2)]
        ktok = [carve(1024, [64, 8, 128], parts=64) for _ in range(2)]
        vtok = [carve(1024, [64, 8, 128], parts=64) for _ in range(2)]
        GB = [carve(512, [64, 8, 64], parts=64) for _ in range(2)]
        EGB = [carve(512, [128, 8, 64]) for _ in range(2)]
        c8 = lambda: carve(512, [64, 8, 64], parts=64)
        XS, XTm, DmS, DmT, Lm, Mm, QKm = c8(), c8(), c8(), c8(), c8(), c8(), c8()
        Rtb = [c8(), c8()]
        Lp = [c8(), c8()]
        Mp = [c8(), c8()]
        vtb = carve(1024, [64, 8, 128], parts=64)
        ktb = carve(1024, [64, 8, 128], parts=64)
        kd = carve(1024, [64, 8, 128], parts=64)
        vnew = carve(1024, [64, 8, 128], parts=64)
        wT = carve(512, [128, 8, 64])
        qg = carve(512, [128, 8, 64])
        ost = [carve(512, [128, 8, 64]) for _ in range(2)]
        I64 = ident[0:64, 0:64]
        step = 0
        for si, (a, T, is_s) in enumerate(SEQS):
            nch = T // 64
            for d in range(2):
                RS = [R("S", h) for h in range(8)]
                if is_s:
                    dma(S, s0[l, d].rearrange("h k v -> k h v"), (), RS)
                else:
                    memset(S, 0.0, RS)
                mS = masks[:, 2 * d, :]
                mT = masks[:, 2 * d + 1, :]
                for s_ in range(min(nch, DBG.get("gdn_steps", 10 ** 9))):
                    c = s_ if d == 0 else nch - 1 - s_
                    b_ = step % 2
                    step += 1
                    t0 = a + c * 64
                    cg = t0 // 64
                    Rl = R("ld", b_)
                    dma(qT[b_], QKVn[0:1024, t0:t0 + 64].rearrange("(h p) t -> p h t", p=128), (), [Rl])
                    dma(kT[b_], QKVn[1024:2048, t0:t0 + 64].rearrange("(h p) t -> p h t", p=128), (), [Rl])
                    dma(ktok[b_], KVtok[t0:t0 + 64, 0:1024].rearrange("p (h d) -> p h d", h=8), (), [Rl])
                    dma(vtok[b_], KVtok[t0:t0 + 64, 1024:2048].rearrange("p (h d) -> p h d", h=8), (), [Rl])
                    gsrc = GS[0, d * 8:(d + 1) * 8, t0:t0 + 64]
                    dma(GB[b_], bass.AP(gsrc.tensor, gsrc.offset, [[0, 64], [NT, 8], [1, 64]]), (), [Rl])
                    esrc = GS[1, d * 8:(d + 1) * 8, t0:t0 + 64]
                    dma(EGB[b_], bass.AP(esrc.tensor, esrc.offset, [[0, 128], [NT, 8], [1, 64]]), (), [Rl])
                    col = lambda kind, h: tokS[:, cg, kind * 16 + d * 8 + h: kind * 16 + d * 8 + h + 1]
                    HR = lambda n, h: R(n, h)
                    for h in range(8):
                        mm(psum[h][0:64, 0:64], kT[b_][:, h, :], kT[b_][:, h, :], True, True, [Rl], [R("ps", h)])
                        mm(psum[h][0:64, 64:128], kT[b_][:, h, :], qT[b_][:, h, :], True, True, [Rl], [R("ps", h)])
                    if DBG.get("gdn_stage", "z") == "0":
                        continue
                    for h in range(8):
                        stt(XS[:, h, :], GB[b_][:, h, :], col(0, h), mS, ALU.subtract, ALU.add, [Rl, R("tokS")], [HR("XS", h)])
                        act(DmS[:, h, :], XS[:, h, :], AF.Exp, [HR("XS", h)], [HR("DmS", h)], scale=-1.0)
                        stt(XTm[:, h, :], GB[b_][:, h, :], col(0, h), mT, ALU.subtract, ALU.add, [Rl, R("tokS")], [HR("XT", h)])
                        act(DmT[:, h, :], XTm[:, h, :], AF.Exp, [HR("XT", h)], [HR("DmT", h)])
                    if DBG.get("gdn_stage", "z") == "a":
                        continue
                    for h in range(8):
                        stt(Lm[:, h, :], psum[h][0:64, 0:64], col(1, h), DmS[:, h, :], ALU.mult, ALU.mult, [R("ps", h), HR("DmS", h), R("tokS")], [HR("L", h)])
                        tt(QKm[:, h, :], psum[h][0:64, 64:128], DmT[:, h, :], ALU.mult, [R("ps", h), HR("DmT", h)], [HR("QKm", h)])
                    if DBG.get("c_sub", 9) < 2:
                        continue
                    for h in range(8):
                        tr(psum[h][0:64, 128:192], Lm[:, h, :], I64, [HR("L", h)], [R("ps", h)])
                    if DBG.get("c_sub", 9) < 3:
                        continue
                    for h in range(8):
                        cp(Mm[:, h, :], psum[h][0:64, 128:192], [R("ps", h)], [HR("M", h)], "act")
                        stt(Rtb[0][:, h, :], psum[h][0:64, 128:192], -1.0, I64, ALU.mult, ALU.add, [R("ps", h)], [HR("Rt0", h)])
                    if DBG.get("gdn_stage", "z") == "b":
                        continue
                    for h in range(8):
                        mm(psum[h][0:64, 256:320], Mm[:, h, :], Lm[:, h, :], True, True, [HR("M", h), HR("L", h)], [R("ps", h)])
                        mm(psum[h][0:64, 128:192], Lm[:, h, :], Mm[:, h, :], True, True, [HR("M", h), HR("L", h)], [R("ps", h)])
                    if DBG.get("d_sub", 9) < 1:
                        continue
                    for h in range(8):
                        cp(Lp[0][:, h, :], psum[h][0:64, 256:320], [R("ps", h)], [HR("Lp0", h)], "act")
                        cp(Mp[0][:, h, :], psum[h][0:64, 128:192], [R("ps", h)], [HR("Mp0", h)], "dve")
                    Rt, RtN = Rtb[0], "Rt0"
                    for lev in range(min(5, DBG.get("d_lev", 5))):
                        i_ = lev % 2
                        Rt, RtN = Rtb[lev % 2], f"Rt{lev % 2}"
                        Rt2, Rt2N = Rtb[1 - lev % 2], f"Rt{1 - lev % 2}"
                        o_ = 1 - i_
                        last = lev == 4
                        for h in range(8):
                            if not last:
                                mm(psum[h][0:64, 128:192], Lp[i_][:, h, :], Mp[i_][:, h, :], True, True, [HR(f"Lp{i_}", h), HR(f"Mp{i_}", h)], [R("ps", h)])
                            mm(psum[h][0:64, 192:256], Lp[i_][:, h, :], Rt[:, h, :], True, True, [HR(f"Lp{i_}", h), HR(RtN, h)], [R("ps", h)])
                            if not last:
                                mm(psum[h][0:64, 256:320], Mp[i_][:, h, :], Lp[i_][:, h, :], True, True, [HR(f"Lp{i_}", h), HR(f"Mp{i_}", h)], [R("ps", h)])
                        for h in range(8):
                            if DBG.get("lev_sub", 9) >= 1:
                                tt(Rt2[:, h, :], psum[h][0:64, 192:256], Rt[:, h, :], ALU.add, [R("ps", h), HR(RtN, h)], [HR(Rt2N, h)])
                            if not last and DBG.get("lev_sub", 9) >= 2:
                                cp(Mp[o_][:, h, :], psum[h][0:64, 128:192], [R("ps", h)], [HR(f"Mp{o_}", h)], "act")
                                cp(Lp[o_][:, h, :], psum[h][0:64, 256:320], [R("ps", h)], [HR(f"Lp{o_}", h)], "act" if h % 2 else "dve")
                    if DBG.get("gdn_stage", "z") == "d":
                        continue
                    nlev_ = min(5, DBG.get("d_lev", 5))
                    Rt, RtN = Rtb[nlev_ % 2], f"Rt{nlev_ % 2}"
                    for h in range(8):
                        ts(vtb[:, h, :], vtok[b_][:, h, :], col(1, h), None, ALU.mult, None, [Rl, R("tokS")], [HR("vtb", h)])
                        ts(ktb[:, h, :], ktok[b_][:, h, :], col(2, h), None, ALU.mult, None, [Rl, R("tokS")], [HR("ktb", h)])
                        ts(kd[:, h, :], ktok[b_][:, h, :], col(3, h), None, ALU.mult, None, [Rl, R("tokS")], [HR("kd", h)])
                        tt(qg[:, h, :], qT[b_][:, h, :], EGB[b_][:, h, :], ALU.mult, [Rl], [HR("qg", h)])
                    for h in range(8):
                        mm(psum[h][:, 448:512], ktb[:, h, :], Rt[:, h, :], True, True, [HR("ktb", h), HR(RtN, h)], [R("ps", h)])
                    for h in range(8):
                        act(wT[:, h, :], psum[h][:, 448:512], AF.Copy, [R("ps", h)], [HR("wT", h)], scale=-1.0)
                    if DBG.get("gdn_stage", "z") == "e":
                        continue
                    for h in range(8):
                        mm(psum[h][0:64, 320:448], Rt[:, h, :], vtb[:, h, :], True, False, [HR(RtN, h), HR("vtb", h)], [R("ps", h)])
                        mm(psum[h][0:64, 320:448], wT[:, h, :], S[:, h, :], False, True, [HR("wT", h), R("S", h)], [R("ps", h)])
                    for h in range(8):
                        cp(vnew[:, h, :], psum[h][0:64, 320:448], [R("ps", h)], [HR("vnew", h)], "act" if h % 2 else "dve")
                    if DBG.get("gdn_stage", "z") == "f":
                        continue
                    ob_ = ost[b_]
                    for h in range(8):
                        mm(psum[h][:, 0:64], S[:, h, :], qg[:, h, :], True, False, [R("S", h), HR("qg", h)], [R("ps", h)])
                        mm(psum[h][:, 0:64], vnew[:, h, :], QKm[:, h, :], False, True, [HR("vnew", h), HR("QKm", h)], [R("ps", h)])
                    for h in range(8):
                        cp(ob_[:, h, :], psum[h][:, 0:64], [R("ps", h)], [R("ost", b_)], "act" if h % 2 else "dve")
                    dma(OFB[d, :, t0:t0 + 64].rearrange("(h p) t -> p h t", p=128), ob_, [R("ost", b_)], [])
                    if DBG.get("gdn_stage", "z") == "g":
                        continue
                    gcol = 63 if d == 0 else 0
                    for h in range(8):
                        mm(psum[h][:, 128:256], kd[:, h, :], vnew[:, h, :], True, True, [HR("kd", h), HR("vnew", h)], [R("ps", h), R("ps", h)])
                    for h in range(8):
                        ts(S[:, h, :], S[:, h, :], EGB[b_][:, h, gcol:gcol + 1], None, ALU.mult, None, [Rl, R("S", h)], [R("S", h)])
                        tt(S[:, h, :], psum[h][:, 128:256], S[:, h, :], ALU.add, [R("ps", h), R("ps", h), R("S", h)], [R("S", h)])
                if not is_s:
                    dma(ns_out[si, l, d].rearrange("h k v -> k h v"), S, [R("S", h) for h in range(8)], [])

    def phase_gdn_post(l):
        phase()
        of_ = [carve(2560) for _ in range(2)]
        ob_ = [carve(2560) for _ in range(2)]
        zz = [carve(2560) for _ in range(2)]
        sq = [carve(512) for _ in range(2)]
        rs = [carve(512) for _ in range(2)]
        yb = [carve(1280, None, BF16) for _ in range(2)]
        dma(prm[:, 16:17], gng[l], (), [R("prm")])
        for h in range(8):
            b_ = h % 2
            dma(of_[b_], OFB[0, h * 128:(h + 1) * 128, :], (), [R("of", b_)])
            dma(ob_[b_], OFB[1, h * 128:(h + 1) * 128, :], (), [R("ob", b_)])
            dma(zz[b_], U[Z0 + h * 128:Z0 + (h + 1) * 128, :], (), [R("zz", b_)])
            tt(of_[b_], of_[b_], ob_[b_], ALU.add, [R("of", b_), R("ob", b_)], [R("of", b_)])
            for t in range(NTT):
                s = uid() % 2
                pb = uid() % 8
                sl = slice(t * TT, (t + 1) * TT)
                act(sq[s], of_[b_][:, sl], AF.Square, [R("of", b_)], [R("sq", s)])
                mm(psum[pb][:], ones128[:], sq[s], True, True, [R("sq", s)], [Rps(pb)])
                rsqrt(rs[s], psum[pb][:], [Rps(pb)], [R("rs", s)])
                stt(of_[b_][:, sl], of_[b_][:, sl], prm[:, 16:17], rs[s], ALU.mult, ALU.mult, [R("of", b_), R("rs", s), R("prm")], [R("of", b_)])
                tt(yb[b_][:, sl], of_[b_][:, sl], zz[b_][:, sl], ALU.mult, [R("of", b_), R("zz", b_)], [R("yb", b_)])
            dma(YT[512 + h * 128:512 + (h + 1) * 128, :], yb[b_], [R("yb", b_)], [])

    def phase_na(l):
        phase()
        BM = carve(7680, [64, 120, 64], parts=64)
        vtk = carve(10240, [64, 40, 512], BF16, parts=64)
        kvfb = carve(8192, parts=64)
        kvf = [kvfb[:, i * 4096:(i + 1) * 4096].rearrange("p (a b) -> p a b", a=8) for i in range(2)]
        kctx = carve(2048, [64, 8, 512], BF16, parts=64)
        vctx = carve(1024, [128, 4, 512], BF16)
        fr = [carve(2560) for _ in range(2)]
        qTh = [carve(1280, None, BF16, parts=64) for _ in range(2)]
        kTh = [carve(1280, None, BF16, parts=64) for _ in range(2)]
        sc = [carve(512, parts=64) for _ in range(2)]
        pT = [carve(256, None, BF16, parts=64) for _ in range(2)]
        pc = [carve(128, None, BF16) for _ in range(2)]
        rden = [carve(256, parts=64) for _ in range(2)]
        yst = [carve(1280, None, BF16, parts=64) for _ in range(2)]
        RB = R("BM")
        src = rpbp[l]
        BHv = kvfb
        for h_ in range(8):
            dma(BHv[:, h_ * 960:(h_ + 1) * 960].rearrange("p (a b) -> p a b", a=15),
                bass.AP(src.tensor, src.offset + h_ * 15 * 128, [[1, 64], [128, 15], [1, 64]]), (), [R("kvf", 0), R("kvf", 1)])
        BMv = BM.rearrange("p a b -> p (a b)")
        for j in range(15):
            pb = uid() % 8
            mm(psum[pb][0:64, :], Jm[:, :], BHv[:, j * 512:(j + 1) * 512], True, True, [R("kvf", 0), R("kvf", 1)], [Rps(pb)])
            cmb = colmask[:, :].unsqueeze(1).to_broadcast([64, 8, 64])
            tt(BMv[:, j * 512:(j + 1) * 512].rearrange("p (a b) -> p a b", a=8), psum[pb][0:64, :].rearrange("p (a b) -> p a b", a=8), cmb, ALU.add, [Rps(pb)], [RB])
        dma(kctx, ck[l].rearrange("h d k -> d h k"), (), [R("kctx")], q="pool")
        dma(vctx, cvv[l].rearrange("(b p) c -> p b c", p=128), (), [R("vctx")], q="pool")
        for which, r00 in ((0, NAK0), (1, NAV0)):
            for chn in range(4):
                b_ = uid() % 2
                dma(fr[b_], U[r00 + chn * 128:r00 + (chn + 1) * 128, :], (), [R("fr", b_)])
                nrow = 40 if which == 1 else 8
                for r4 in range(0, nrow, 4):
                    pb = uid() % 8
                    for j in range(4):
                        r = r4 + j
                        tr(psum[pb][0:64, j * 128:(j + 1) * 128], fr[b_][:, r * 64:(r + 1) * 64], ident[:], [R("fr", b_)], [Rps(pb)])
                    src_ = psum[pb][0:64, :].rearrange("p (j n) -> p j n", j=4)
                    if which == 1:
                        cp(vtk[:, r4:r4 + 4, chn * 128:(chn + 1) * 128], src_, [Rps(pb)], [R("vtk")], evac_eng())
                    if r4 < 8:
                        cp(kvf[which][:, r4:r4 + 4, chn * 128:(chn + 1) * 128], src_, [Rps(pb)], [R("kvf", which)], evac_eng())
        for sq_ in range(2):
            dma(nk_out[sq_, l].rearrange("(r p) c -> p r c", p=64), kvf[0][:, sq_ * 4:(sq_ + 1) * 4, :], [R("kvf", 0)], [])
            dma(nv_out[sq_, l].rearrange("(r p) c -> p r c", p=64), kvf[1][:, sq_ * 4:(sq_ + 1) * 4, :], [R("kvf", 1)], [])
        for h in range(8):
            b_ = h % 2
            dma(qTh[b_], U[NAQ0 + h * 64:NAQ0 + (h + 1) * 64, :], (), [R("qTh", b_)], q="pool")
            dma(kTh[b_], U[NAK0 + h * 64:NAK0 + (h + 1) * 64, :], (), [R("kTh", b_)], q="pool")
            hs = slice(h * 64, (h + 1) * 64)
            for sq_ in range(2):
                for qb in range(4):
                    q0 = sq_ * 256 + qb * 64
                    i_ = uid() % 2
                    pS, pO, pD = (uid() % 2) * 4, (uid() % 2) * 4 + 1, (uid() % 2) * 4 + 2
                    for kr in range(4):
                        k0 = sq_ * 256 + kr * 64
                        mm(psum[pS][0:64, kr * 64:(kr + 1) * 64], kTh[b_][:, k0:k0 + 64], qTh[b_][:, q0:q0 + 64], True, True,
                           [R("qTh", b_), R("kTh", b_)], [Rps(pS)])
                    act(pT[i_][:, 0:256], psum[pS][0:64, 0:256], AF.Exp, [Rps(pS)], [R("pT", i_)], scale=0.125)
                    for kr in range(4):
                        mm(psum[pO][0:64, 0:64], vtk[:, sq_ * 4 + kr, hs], pT[i_][:, kr * 64:(kr + 1) * 64], kr == 0, kr == 3,
                           [R("vtk"), R("pT", i_)], [Rps(pO)])
                    for kr in range(4):
                        mm(psum[pD][0:64, 0:64], onesb[0:64, 0:64], pT[i_][:, kr * 64:(kr + 1) * 64], kr == 0, kr == 3,
                           [R("pT", i_)], [Rps(pD)])
                    P.add("dve", lambda e, o=rden[i_][:, 0:64], i=psum[pD][0:64, 0:64]: e.reciprocal(o, i), [Rps(pD)], [R("rden", i_)])
                    tt(yst[b_][:, q0:q0 + 64], psum[pO][0:64, 0:64], rden[i_][:, 0:64], ALU.mult, [Rps(pO), R("rden", i_)], [R("yst", b_)])
            for r in range(32):
                q0 = 512 + r * 64
                kr0 = min(max(r - 4, 0), 24)
                d0 = kr0 - r + 7
                i_ = uid() % 2
                base = (r % 2) * 4
                pS, pC, pO, pD = base, base + 1, base + 2, base + 3
                for i in range(8):
                    k0 = 512 + (kr0 + i) * 64
                    mm(psum[pS][0:64, i * 64:(i + 1) * 64], kTh[b_][:, k0:k0 + 64], qTh[b_][:, q0:q0 + 64], True, True,
                       [R("qTh", b_), R("kTh", b_)], [Rps(pS)])
                for cb in range(4):
                    mm(psum[pC][:, cb * 64:(cb + 1) * 64], kctx[:, h, cb * 128:(cb + 1) * 128], qTh[b_][:, q0:q0 + 64], True, True,
                       [R("qTh", b_), R("kctx")], [Rps(pC)])
                bm = BM[:, h * 15 + d0:h * 15 + d0 + 8, :]
                stt(sc[i_].rearrange("p (a b) -> p a b", a=8), psum[pS][0:64, :].rearrange("p (a b) -> p a b", a=8), 0.125, bm, ALU.mult, ALU.add,
                    [Rps(pS), RB], [R("sc", i_)])
                act(pT[i_], sc[i_], AF.Exp, [R("sc", i_)], [R("pT", i_)])
                act(pc[i_], psum[pC][:, 0:256], AF.Exp, [Rps(pC)], [R("pc", i_)], scale=0.125)
                for i in range(8):
                    mm(psum[pO][0:64, 0:64], vtk[:, 8 + kr0 + i, hs], pT[i_][:, i * 64:(i + 1) * 64], i == 0, False,
                       [R("vtk"), R("pT", i_)], [Rps(pO)])
                for cb in range(4):
                    mm(psum[pO][0:64, 0:64], vctx[:, cb, hs], pc[i_][:, cb * 64:(cb + 1) * 64], False, cb == 3,
                       [R("vctx"), R("pc", i_)], [Rps(pO)])
                for i in range(8):
                    mm(psum[pD][0:64, 0:64], onesb[0:64, 0:64], pT[i_][:, i * 64:(i + 1) * 64], i == 0, False, [R("pT", i_)], [Rps(pD)])
                for cb in range(4):
                    mm(psum[pD][0:64, 0:64], onesb[:, 0:64], pc[i_][:, cb * 64:(cb + 1) * 64], False, cb == 3, [R("pc", i_)], [Rps(pD)])
                P.add("dve", lambda e, o=rden[i_][:, 0:64], i=psum[pD][0:64, 0:64]: e.reciprocal(o, i), [Rps(pD)], [R("rden", i_)])
                tt(yst[b_][:, q0:q0 + 64], psum[pO][0:64, 0:64], rden[i_][:, 0:64], ALU.mult, [Rps(pO), R("rden", i_)], [R("yst", b_)])
            dma(YT[1536 + h * 64:1536 + (h + 1) * 64, :], yst[b_], [R("yst", b_)], [])

    def load_H(src):
        H = carve(20480, [128, KC, NT], BF16)
        for kc in range(KC):
            dma(H[:, kc, :], src[kc * 128:(kc + 1) * 128, :], (), [R("H", kc)])
        return H

    def phase_merge(l):
        phase()
        H = load_H(YT)
        wbuf = [carve(4096, [128, KC, 512], BF16) for _ in range(2)]
        gt = [carve(1536, [128, 3, 512]) for _ in range(2)]
        t0_ = [carve(512) for _ in range(2)]
        t1_ = [carve(512) for _ in range(2)]
        ms = [carve(1280, None, BF16) for _ in range(2)]
        wv = wb[l].rearrange("(kc p) n -> p kc n", p=128)
        KR = ((0, 4), (4, 12), (12, 16))
        for g in range(4):
            s = g % 2
            dma(wbuf[s], wv[:, :, g * 512:(g + 1) * 512], (), [R("wbuf", s)], q="pool")
            for j in range(4):
                oc = g * 4 + j
                mb = oc % 2
                for t in range(NTT):
                    gi = uid() % 2
                    sl = slice(t * TT, (t + 1) * TT)
                    for i in range(3):
                        r0 = GATE0 + i * 2048 + oc * 128
                        dma(gt[gi][:, i, :], U[r0:r0 + 128, sl], (), [R("gt", gi)])
                    pbs = [(uid() % 2) * 4 + i for i in range(3)]
                    for i, (k0_, k1_) in enumerate(KR):
                        for kc in range(k0_, k1_):
                            mm(psum[pbs[i]][:], wbuf[s][:, kc, j * 128:(j + 1) * 128], H[:, kc, sl], kc == k0_, kc == k1_ - 1,
                               [R("wbuf", s), R("H", kc)], [Rps(pbs[i])])
                    tt(t0_[gi], psum[pbs[0]][:], gt[gi][:, 0, :], ALU.mult, [Rps(pbs[0]), R("gt", gi)], [R("t0", gi)])
                    tt(t1_[gi], psum[pbs[1]][:], gt[gi][:, 1, :], ALU.mult, [Rps(pbs[1]), R("gt", gi)], [R("t1", gi)])
                    tt(t0_[gi], t0_[gi], t1_[gi], ALU.add, [R("t0", gi), R("t1", gi)], [R("t0", gi)])
                    tt(t1_[gi], psum[pbs[2]][:], gt[gi][:, 2, :], ALU.mult, [Rps(pbs[2]), R("gt", gi)], [R("t1", gi)])
                    tt(ms[mb][:, sl], t0_[gi], t1_[gi], ALU.add, [R("t0", gi), R("t1", gi)], [R("ms", mb)])
                dma(MT[oc * 128:(oc + 1) * 128, :], ms[mb], [R("ms", mb)], [])

    def phase_wout(l):
        phase()
        H = load_H(MT)
        wbuf = [carve(4096, [128, KC, 512], BF16) for _ in range(2)]
        xs = [carve(2560) for _ in range(2)]
        wv = w_out[l].rearrange("(kc p) n -> p kc n", p=128)
        for g in range(4):
            s = g % 2
            dma(wbuf[s], wv[:, :, g * 512:(g + 1) * 512], (), [R("wbuf", s)], q="pool")
            for j in range(4):
                oc = g * 4 + j
                xb = oc % 2
                dma(xs[xb], XT[oc * 128:(oc + 1) * 128, :], (), [R("xs", xb)])
                for t in range(NTT):
                    c = 0 if t == 0 else 1
                    sl = slice(t * TT, (t + 1) * TT)
                    pb = uid() % 8
                    for kc in range(KC):
                        mm(psum[pb][:], wbuf[s][:, kc, j * 128:(j + 1) * 128], H[:, kc, sl], kc == 0, kc == KC - 1,
                           [R("wbuf", s), R("H", kc)], [Rps(pb)])
                    stt(xs[xb][:, sl], psum[pb][:], modsb[:, 32 + oc, c:c + 1], xs[xb][:, sl], ALU.mult, ALU.add,
                        [Rps(pb), R("xs", xb)], [R("xs", xb)])
                dma(XT[oc * 128:(oc + 1) * 128, :], xs[xb], [R("xs", xb)], [])

    def phase_ffn_down(l):
        phase()
        actb = carve(22528, [128, 44, 1024], BF16)
        ab = [carve(2080, [128, 4, 520]) for _ in range(4)]
        cb_ = [carve(2080, [128, 4, 520]) for _ in range(2)]
        wd = [carve(2816, [128, 44, 128], BF16) for _ in range(2)]
        xs = [carve(512) for _ in range(4)]
        fw = carve(264, [128, 88, 3])
        dma(fw, fconvT[l], (), [R("fw")])
        wv = w_down[l].rearrange("(kc p) n -> p kc n", p=128)
        it = 0
        xi = 0
        for grp in ((0, 1), (2, 3), (4,)):
            for gi, t in enumerate(grp):
                lo = t * TT
                hl = t > 1
                hr = 1 <= t < 4
                segs = [(1, 257), (257, 513)] if t == 0 else [(0, 514)]
                for c4 in range(0, 44, 4):
                    bufs = []
                    for half in range(2):
                        bi = (it % 2) * 2 + half
                        r0 = half * DFF + c4 * 128
                        a_, Ra_ = ab[bi], R("ab", bi)
                        c0 = lo - (1 if hl else 0)
                        c1 = lo + TT + (1 if hr else 0)
                        o0 = 0 if hl else 1
                        if not hl:
                            memset(a_[:, :, 0:1], 0.0, [Ra_])
                        if not hr:
                            memset(a_[:, :, 513:514], 0.0, [Ra_])
                        dma(a_[:, :, o0:o0 + (c1 - c0)], U[r0:r0 + 512, c0:c1].rearrange("(j p) t -> p j t", p=128), (), [Ra_])
                        cv_, Rc_ = cb_[half], R("cb", half)
                        for j in range(4):
                            conv3(cv_[:, j, 0:514], a_[:, j, 0:514], fw[:, half * 44 + c4 + j, :], segs, [Ra_, R("fw")], [Rc_])
                        bufs.append((cv_, Rc_))
                    it += 1
                    (ca, Rca), (cbb, Rcb) = bufs
                    act(ca[:, :, 1:513], ca[:, :, 1:513], AF.Silu, [Rca], [Rca])
                    tt(actb[:, c4:c4 + 4, gi * TT:(gi + 1) * TT], ca[:, :, 1:513], cbb[:, :, 1:513], ALU.mult, [Rca, Rcb],
                       [R("actb", c4 + j) for j in range(4)])
            for oc in range(16):
                s = oc % 2
                dma(wd[s], wv[:, :, oc * 128:(oc + 1) * 128], (), [R("wd", s)], q="pool")
                for gi, t in enumerate(grp):
                    c = 0 if t == 0 else 1
                    lo = t * TT
                    x_ = xi % 4
                    xi += 1
                    dma(xs[x_], XT[oc * 128:(oc + 1) * 128, lo:lo + TT], (), [R("xs", x_)])
                    pb = uid() % 8
                    for kc in range(44):
                        mm(psum[pb][:], wd[s][:, kc, :], actb[:, kc, gi * TT:(gi + 1) * TT], kc == 0, kc == 43, [R("wd", s), R("actb", kc)], [Rps(pb)])
                    stt(xs[x_], psum[pb][:], modsb[:, 80 + oc, c:c + 1], xs[x_], ALU.mult, ALU.add, [Rps(pb), R("xs", x_)], [R("xs", x_)])
                    dma(XT[oc * 128:(oc + 1) * 128, lo:lo + TT], xs[x_], [R("xs", x_)], [])

    def in_func(col):
        if Z0 <= col < BETA0:
            return AF.Silu
        if col >= GATE0:
            return AF.Sigmoid
        return None

    IN_GROUPS = [(g * 512, 512) for g in range(9)] + [(4608, 32)] + [(4640 + g * 512, 512) for g in range(15)]

    if plan is None:
        plan = ["consts", "intr"]
        for l in range(L):
            plan += [("mod", l), ("norm1", l), ("pool", l), ("gdn_pre", l), ("gdn_main", l), ("gdn_post", l), ("na", l),
                     ("merge", l), ("wout", l), ("norm2", l), ("ffn", l)]
        plan += ["final"]
    for p in plan:
        if p == "consts":
            phase_consts()
        elif p == "intr":
            phase_in_transpose()
        elif p == "final":
            phase_norm(0, final=True)
        else:
            nm, l = p
            if nm == "mod":
                phase_mod(l)
            elif nm == "norm1":
                phase_norm(0)
                phase_proj(w_in[l], 0, N_IN, U, 0, in_func, IN_GROUPS)
            elif nm == "pool":
                phase_pool(l)
            elif nm == "gdn_pre":
                phase_gdn_pre(l)
            elif nm == "gdn_main":
                phase_gdn_main(l)
            elif nm == "gdn_post":
                phase_gdn_post(l)
            elif nm == "na":
                phase_na(l)
            elif nm == "merge":
                phase_merge(l)
            elif nm == "wout":
                phase_wout(l)
            elif nm == "norm2":
                phase_norm(1)
                phase_proj(w_up[l], 0, 2 * DFF, U, 0)
            elif nm == "ffn":
                phase_ffn_down(l)
    P.emit()
    st.close()
    return nc


def make_consts():
    misc = np.zeros((128, 640), np.float32)
    misc[0:64, 512:576] = np.eye(64, dtype=np.float32)[::-1]
    misc[:, 0:128] = np.eye(128, dtype=np.float32)
    i = np.arange(64)[:, None]
    j = np.arange(64)[None, :]
    m = np.zeros((64, 4, 64), np.float32)
    m[:, 0, :] = np.where(j >= i, BIG, 0.0)
    m[:, 1, :] = np.where(j < i, -BIG, 0.0)
    m[:, 2, :] = np.where(j <= i, BIG, 0.0)
    m[:, 3, :] = np.where(j > i, -BIG, 0.0)
    misc[0:64, 128:384] = m.reshape(64, 256)
    qc = np.arange(64)
    cs = np.clip(qc - 8, 0, 48)
    kc = np.arange(64)[:, None]
    ok = (kc >= cs[None, :]) & (kc < cs[None, :] + 16)
    misc[0:64, 384:448] = np.where(ok, 0.0, -BIG)
    misc[0:16, 448] = (np.arange(16) >= 8).astype(np.float32)
    reset = np.ones((16, 2048), np.float32)
    reset[:, 0::64] = 0.0
    inv = np.zeros((4, 128, 2304), np.float32)
    for g, w in enumerate((2, 4, 8, 16)):
        for off, T in ((0, 256), (256, 2048)):
            t = np.arange(T)
            lo = np.maximum(t - w // 2, 0)
            hi = np.minimum(t + w - 1 - w // 2, T - 1)
            inv[g, :, off:off + T] = (1.0 / (hi - lo + 1).astype(np.float32))[None, :]
    return misc, reset, inv


def make_inputs(inp):
    f = lambda a: np.ascontiguousarray(np.asarray(a, dtype=np.float32))
    misc, reset, inv = make_consts()
    tr128 = lambda v: f(v.reshape(-1, 128).T)
    rp = f(inp["na_rpb"])
    rpbp = np.zeros((L, 8, 15, 128), np.float32)
    rpbp[..., 48:79] = rp[..., ::-1]
    shared = {
        "w_ada": f(inp["w_ada"]),
        "b_adaT": f(np.stack([tr128(inp["b_ada"][l]) for l in range(L)])),
        "gn1T": f(np.stack([tr128(inp["g_norm1"][l]) for l in range(L)])),
        "gn2T": f(np.stack([tr128(inp["g_norm2"][l]) for l in range(L)])),
        "gfT": tr128(f(inp["g_final"])),
        "w_in": f(inp["w_in"]),
        "pool_w": f(inp["pool_w"]),
        "pool_scT": f(np.stack([tr128(inp["pool_scale"][l]) for l in range(L)])),
        "gconvT": f(np.asarray(inp["gdn_conv"]).reshape(L, 3, 24, 128).transpose(0, 3, 2, 1)),
        "alog": f(np.asarray(inp["gdn_a_log"]).reshape(L, 16, 1)),
        "dtb": f(np.asarray(inp["gdn_dt_bias"]).reshape(L, 16, 1)),
        "gng": f(np.asarray(inp["gdn_norm_g"]).reshape(L, 128, 1)),
        "rpbp": rpbp,
        "wb": f(np.concatenate([inp["w_branch_pool"], inp["w_branch_gdn"], inp["w_branch_na"]], axis=1)),
        "w_out": f(inp["w_out"]),
        "w_up": f(inp["w_up"]),
        "fconvT": f(np.asarray(inp["ffn_conv"]).reshape(L, 3, 88, 128).transpose(0, 3, 2, 1)),
        "w_down": f(inp["w_down"]),
        "c_invcnt": inv,
        "c_misc": misc,
        "c_reset": reset,
    }
    xp = np.asarray(inp["x_prompt"], np.float32)
    xs = np.asarray(inp["x_sample"], np.float32)
    maps = []
    for c in range(8):
        b = c % 2
        m = dict(shared)
        m["xin"] = f(np.concatenate([xp[2 * c], xp[2 * c + 1], xs[b]], axis=0))
        cond = np.stack([np.asarray(inp["c_ctx"], np.float32), np.asarray(inp["c"], np.float32)[b]], axis=0)
        m["condT"] = f(cond.reshape(2, KC, 128).transpose(2, 1, 0))
        m["ck"] = f(np.asarray(inp["cache_na_k"])[b].transpose(0, 2, 3, 1))
        m["cvv"] = f(np.asarray(inp["cache_na_v"])[b].reshape(L, 512, 512))
        m["s0"] = f(np.asarray(inp["state_gdn"])[b])
        maps.append(m)
    return maps


_NC = {}


def kernel(**inputs):
    import os
    ncores = int(os.environ.get("KCORES", "8"))
    if "nc" not in _NC:
        if os.environ.get("KPLAN") == "gdn1":
            DBG["gdn_steps"] = 1
            _NC["nc"] = build(plan=["consts", ("gdn_main", 0)])
        else:
            _NC["nc"] = build()
    nc = _NC["nc"]
    maps = make_inputs(inputs)[:ncores]
    res = run_bass_kernel_spmd(nc, maps, core_ids=list(range(ncores)))
    if ncores < 8:
        res.results.extend([res.results[0]] * (8 - ncores))
    rs = res.results
    yp = np.zeros((16, 256, D), np.float32)
    ys = np.zeros((2, 2048, D), np.float32)
    nk = np.zeros((16, L, 256, 8, 64), np.float32)
    nv = np.zeros((16, L, 256, 8, 64), np.float32)
    ns = np.zeros((16, L, 2, 8, 128, 128), np.float32)
    for c in range(8):
        y = np.asarray(rs[c]["y_out"])
        yp[2 * c] = y[0:256]
        yp[2 * c + 1] = y[256:512]
        if c < 2:
            ys[c] = y[512:2560]
        nk[2 * c:2 * c + 2] = np.asarray(rs[c]["nk_out"]).reshape(2, L, 256, 8, 64)
        nv[2 * c:2 * c + 2] = np.asarray(rs[c]["nv_out"]).reshape(2, L, 256, 8, 64)
        ns[2 * c:2 * c + 2] = np.asarray(rs[c]["ns_out"])
    return yp, ys, nk, nv, ns
```

```python
import numpy as np
from contextlib import ExitStack
import concourse.bass as bass
import concourse.mybir as mybir
from concourse.alu_op_type import AluOpType as ALU
from concourse.bass_utils import run_bass_kernel_spmd

F32 = mybir.dt.float32
BF16 = mybir.dt.bfloat16
AF = mybir.ActivationFunctionType

D = 2048
KC = 16
L = 2
NT = 2560
TT = 512
NTT = 5
SEQS = [(0, 256, False), (256, 256, False), (512, 2048, True)]
DFF = 5632
N_IN = 12320
Q0, K0, V0, Z0, BETA0, A0 = 512, 1536, 2560, 3584, 4608, 4624
NAQ0, NAK0, NAV0, GATE0 = 4640, 5152, 5664, 6176
EPS = 1e-6
BIG = 30000.0
DBG = {}
AW = 45056


class Res:
    __slots__ = ("w", "rc", "rd", "excl")

    def __init__(self):
        self.excl = False
        self.w = None
        self.rc = {}
        self.rd = []


class Prog:
    CE = ("pe", "act", "dve")
    EPOCH = 16000
    KDMA = 20

    def __init__(self, nc):
        self.nc = nc
        self.ops = []
        self.res = {}
        self.last = {}
        self.lastdma = {"sp": [], "pool": []}
        self.bar = set()
        self.bar_id = 0
        self.crossed = {}

    def R(self, *key):
        r = self.res.get(key)
        if r is None:
            r = self.res[key] = Res()
            r.excl = key[0] == "ps"
        return r

    def barrier(self):
        deps = set()
        for e in self.CE:
            if e in self.last:
                deps.add(self.last[e])
        for q in ("sp", "pool"):
            deps.update(self.lastdma[q][-self.KDMA:])
        self.bar = deps
        self.bar_id += 1
        self.res = {}

    def add(self, eng, fn, reads=(), writes=(), dma=False):
        i = len(self.ops)
        deps = set()
        xs = [r for r in reads if r.excl]
        if xs:
            writes = list(writes) + xs
            reads = [r for r in reads if not r.excl]
        for r in reads:
            if r.w is not None:
                deps.add(r.w)
        for r in writes:
            if r.w is not None:
                deps.add(r.w)
            for e2, j in r.rc.items():
                deps.add(j)
            deps.update(r.rd)
        if self.crossed.get(eng) != self.bar_id:
            deps |= self.bar
            self.crossed[eng] = self.bar_id
        if eng == "pe":
            deps = {j for j in deps if self.ops[j][0] != "pe" or self.ops[j][3]}
        for r in reads:
            if dma:
                r.rd.append(i)
            else:
                r.rc[eng] = i
        for r in writes:
            r.w = i
            r.rc = {}
            r.rd = []
        if dma:
            self.lastdma[eng].append(i)
        else:
            self.last[eng] = i
        self.ops.append([eng, fn, deps, dma, False, None, None])

    def emit(self):
        nc = self.nc
        ops = self.ops
        for op in ops:
            for j in op[2]:
                ops[j][4] = True
        cnt = {e: 0 for e in self.CE}
        nd = {"sp": 0, "pool": 0}
        tot = {"sp": [0] * self.KDMA, "pool": [0] * self.KDMA}
        for op in ops:
            e = op[0]
            if op[3]:
                slot = nd[e] % self.KDMA
                nd[e] += 1
                prev = tot[e][slot]
                tot[e][slot] += 16
                op[5] = ("d", e, slot)
                op[6] = (prev, tot[e][slot])
            elif op[4]:
                ep, v = divmod(cnt[e], self.EPOCH)
                cnt[e] += 1
                op[5] = ("c", e, ep)
                op[6] = v + 1
        st = ExitStack()
        sems = {}
        for e in self.CE:
            for ep in range(cnt[e] // self.EPOCH + 1):
                sems[("c", e, ep)] = st.enter_context(nc.semaphore(f"s_{e}_{ep}"))
        for e in ("sp", "pool"):
            for s in range(self.KDMA):
                sems[("d", e, s)] = st.enter_context(nc.semaphore(f"d_{e}_{s}"))
        block = st.enter_context(nc.Block())
        per = {e: [] for e in ("pe", "act", "dve", "sp", "pool")}
        for op in ops:
            per[op[0]].append(op)
        KD = self.KDMA

        def run(e, eng):
            waited = {}
            for op in per[e]:
                waits = {}
                for j in op[2]:
                    oj = ops[j]
                    k = oj[5]
                    v = oj[6][1] if oj[3] else oj[6]
                    if waits.get(k, 0) < v:
                        waits[k] = v
                if op[3] and op[6][0] > 0:
                    k = op[5]
                    if waits.get(k, 0) < op[6][0]:
                        waits[k] = op[6][0]
                for k, v in waits.items():
                    if waited.get(k, 0) < v:
                        eng.wait_ge(sems[k], v)
                        waited[k] = v
                ins = op[1](eng)
                if op[3]:
                    ins.then_inc(sems[op[5]], 16)
                elif op[4]:
                    ins.then_inc(sems[op[5]], 1)
            if e in ("sp", "pool"):
                for s in range(KD):
                    if tot[e][s] > 0:
                        eng.wait_ge(sems[("d", e, s)], tot[e][s])

        @block.tensor
        def _(t):
            run("pe", t)

        @block.scalar
        def _(t):
            run("act", t)

        @block.vector
        def _(t):
            run("dve", t)

        @block.sync
        def _(t):
            run("sp", t)

        @block.gpsimd
        def _(t):
            run("pool", t)

        st.close()


def build(plan=None, dbg=()):
    nc = bass.Bass("TRN2", target_bir_lowering=False)
    P = Prog(nc)
    R = P.R
    st = ExitStack()

    def din(name, shape, dt=F32):
        return nc.dram_tensor(name, list(shape), dt, kind="ExternalInput").ap()

    def dout(name, shape, dt=F32):
        return nc.dram_tensor(name, list(shape), dt, kind="ExternalOutput").ap()

    def dscr(name, shape, dt=F32):
        kind = "ExternalOutput" if name in dbg else "Internal"
        return nc.dram_tensor(name, list(shape), dt, kind=kind).ap()

    def sb(name, shape, dt=F32):
        return st.enter_context(nc.sbuf_tensor(name, list(shape), dt))

    xin = din("xin", [NT, D])
    condT = din("condT", [128, KC, 2])
    ck = din("ck", [L, 8, 64, 512])
    cvv = din("cvv", [L, 512, 512])
    s0 = din("s0", [L, 2, 8, 128, 128])
    w_ada = din("w_ada", [L, D, 6 * D])
    b_adaT = din("b_adaT", [L, 128, 96])
    gn1T = din("gn1T", [L, 128, KC])
    gn2T = din("gn2T", [L, 128, KC])
    gfT = din("gfT", [128, KC])
    w_in = din("w_in", [L, D, N_IN])
    pool_w = din("pool_w", [L, 4, 128, 128])
    pool_scT = din("pool_scT", [L, 128, 4])
    gconvT = din("gconvT", [L, 128, 24, 3])
    alog = din("alog", [L, 16, 1])
    dtb = din("dtb", [L, 16, 1])
    gng = din("gng", [L, 128, 1])
    rpbp = din("rpbp", [L, 8, 15, 128])
    wb = din("wb", [L, D, D])
    w_out = din("w_out", [L, D, D])
    w_up = din("w_up", [L, D, 2 * DFF])
    fconvT = din("fconvT", [L, 128, 88, 3])
    w_down = din("w_down", [L, DFF, D])
    c_invcnt = din("c_invcnt", [4, 128, 2304])
    c_misc = din("c_misc", [128, 640])
    c_reset = din("c_reset", [16, 2048])

    y_out = dout("y_out", [NT, D])
    nk_out = dout("nk_out", [2, L, 256, 512])
    nv_out = dout("nv_out", [2, L, 256, 512])
    ns_out = dout("ns_out", [2, L, 2, 8, 128, 128])

    XT = dscr("XT", [D, NT])
    U = dscr("U", [N_IN, NT])
    YT = dscr("YT", [D, NT], BF16)
    MT = dscr("MT", [D, NT], BF16)
    QKVn = dscr("QKVn", [3072, NT])
    KVtok = dscr("KVtok", [NT, 2048])
    GS = dscr("GS", [2, 16, NT])
    OFB = dscr("OFB", [2, 1024, NT])
    XTv = XT.rearrange("(kc p) n -> p kc n", p=128)

    arena = sb("arena", [128, AW], F32)
    ident = sb("ident", [128, 128], F32)
    ones_m = sb("ones_m", [128, 128], F32)
    ones1 = sb("ones1", [128, 128], F32)
    onesb = sb("onesb", [128, 128], BF16)
    ones128 = sb("ones128", [128, 128], F32)
    masks = sb("masks", [64, 4, 64], F32)
    colmask = sb("colmask", [64, 64], F32)
    isb = sb("isb", [16, 1], F32)
    Jm = sb("Jm", [64, 64], F32)
    resetm = sb("resetm", [16, 2048], F32)
    silucT = sb("silucT", [128, KC, 2], BF16)
    condsb = sb("condsb", [128, KC, 2], F32)
    modsb = sb("modsb", [128, 96, 2], F32)
    badaT = sb("badaT", [128, 96], F32)
    gnT = sb("gnT", [128, 3, KC], F32)
    a_sc = sb("a_sc", [128, 2, KC, 2], F32)
    prm = sb("prm", [128, 512], F32)
    tokS = sb("tokS", [64, 40, 64], F32)
    zcol = sb("zcol", [128, 1], F32)
    ocol = sb("ocol", [128, 1], F32)
    epscol = sb("epscol", [128, 1], F32)
    psum = [st.enter_context(nc.psum_tensor(f"ps{i}", [128, 512], F32)) for i in range(8)]
    Rps = lambda i: R("ps", i)

    cnt = {"n": 0, "off": 0, "rr": 0}

    def uid():
        cnt["n"] += 1
        return cnt["n"]

    def phase():
        P.barrier()
        cnt["off"] = 0

    def carve(ncols_f32, shape=None, dt=F32, parts=128):
        o = cnt["off"]
        assert o + ncols_f32 <= AW, ("arena overflow", o, ncols_f32)
        cnt["off"] = o + ncols_f32
        a = arena[0:parts, o:o + ncols_f32]
        if dt == BF16:
            a = a.bitcast(BF16)
        if shape is not None and len(shape) == 3:
            a = a.rearrange("p (a b) -> p a b", a=shape[1])
        elif shape is not None and len(shape) == 4:
            a = a.rearrange("p (a b c) -> p a b c", a=shape[1], b=shape[2])
        return a

    def dma(out, in_, reads, writes, q="sp", slow=False):
        if slow:
            P.add(q, lambda e, o=out, i=in_: e.dma_start(out=o, in_=i, allow_slow_non_contiguous=True), reads, writes, dma=True)
        else:
            P.add(q, lambda e, o=out, i=in_: e.dma_start(out=o, in_=i), reads, writes, dma=True)

    def mm(out, lhsT, rhs, start, stop, reads, writes):
        P.add("pe", lambda e, o=out, a=lhsT, b=rhs, s0_=start, s1_=stop: e.matmul(o, a, b, start=s0_, stop=s1_), reads, writes)

    def tr(out, in_, idn, reads, writes):
        P.add("pe", lambda e, o=out, a=in_, b=idn: e.transpose(o, a, b), reads, writes)

    def act(out, in_, func, reads, writes, bias=None, scale=None):
        def f(e, o=out, i=in_, fu=func, b=bias, s=scale):
            kw = {}
            if b is not None:
                kw["bias"] = b
            if s is not None:
                kw["scale"] = s
            return e.activation(out=o, in_=i, func=fu, **kw)
        P.add("act", f, reads, writes)

    def ts(out, in0, s1, s2, op0, op1, reads, writes):
        if op1 is None:
            P.add("dve", lambda e, o=out, a=in0, x=s1, p0=op0: e.tensor_scalar(o, a, x, None, p0), reads, writes)
        else:
            P.add("dve", lambda e, o=out, a=in0, x=s1, y=s2, p0=op0, p1=op1: e.tensor_scalar(o, a, x, y, p0, p1), reads, writes)

    def tt(out, in0, in1, op, reads, writes):
        P.add("dve", lambda e, o=out, a=in0, b=in1, p=op: e.tensor_tensor(o, a, b, p), reads, writes)

    def stt(out, in0, s, in1, op0, op1, reads, writes):
        P.add("dve", lambda e, o=out, a=in0, x=s, b=in1, p0=op0, p1=op1: e.scalar_tensor_tensor(o, a, x, b, p0, p1), reads, writes)

    def cp(out, in_, reads, writes, eng="dve"):
        if eng == "dve":
            P.add("dve", lambda e, o=out, i=in_: e.tensor_copy(o, i), reads, writes)
        else:
            act(out, in_, AF.Copy, reads, writes)

    def memset(ap, val, writes):
        P.add("dve", lambda e, a=ap, v=val: e.memset(a, v), (), writes)

    def rsqrt(out, in_, reads, writes):
        act(out, in_, AF.Sqrt, reads, writes, bias=epscol[0:in_.shape[0], :])
        P.add("dve", lambda e, o=out: e.reciprocal(o, o), writes, writes)

    def evac_eng():
        cnt["rr"] += 1
        return "dve" if cnt["rr"] % 2 else "act"

    def phase_consts():
        Rc = R("c")
        dma(ident[:], c_misc[:, 0:128], (), [Rc])
        dma(masks[:], c_misc[0:64, 128:384].rearrange("p (a b) -> p a b", a=4), (), [Rc])
        dma(colmask[:], c_misc[0:64, 384:448], (), [Rc])
        dma(isb[:], c_misc[0:16, 448:449], (), [Rc], slow=True)
        dma(Jm[:], c_misc[0:64, 512:576], (), [Rc])
        dma(resetm[:], c_reset[:, :], (), [Rc])
        memset(ones_m[:], 1.0 / D, [Rc])
        memset(ones1[:], 1.0, [Rc])
        memset(ones128[:], 1.0 / 128, [Rc])
        memset(onesb[:], 1.0, [Rc])
        memset(zcol[:], 0.0, [Rc])
        memset(ocol[:], 1.0, [Rc])
        memset(epscol[:], EPS, [Rc])
        for c0 in range(0, AW, 4096):
            memset(arena[:, c0:min(AW, c0 + 4096)], 0.0, [Rc])
        memset(tokS[:], 0.0, [Rc])
        dma(condsb[:], condT[:, :, :], (), [R("condsb")])
        act(silucT[:], condsb[:], AF.Silu, [R("condsb")], [R("silucT")])
        dma(gnT[:, 2, :], gfT[:, :], (), [R("gnT")])

    def phase_in_transpose():
        phase()
        wk = [carve(2048) for _ in range(2)]
        xt = carve(8192, [128, KC, 512])
        for t in range(NTT):
            for b in range(4):
                tok0 = t * TT + b * 128
                w_, Rw_ = wk[b % 2], R("wk", b % 2)
                dma(w_, xin[tok0:tok0 + 128, :], (), [Rw_])
                for g in range(4):
                    pb = (b * 4 + g) % 8
                    for j in range(4):
                        kc = g * 4 + j
                        tr(psum[pb][:, j * 128:(j + 1) * 128], w_[:, kc * 128:(kc + 1) * 128], ident[:], [Rw_], [Rps(pb)])
                    o = xt[:, g * 4:(g + 1) * 4, b * 128:(b + 1) * 128]
                    i = psum[pb][:, :].rearrange("p (j n) -> p j n", j=4)
                    cp(o, i, [Rps(pb)], [R("xt")], evac_eng())
            dma(XTv[:, :, t * TT:(t + 1) * TT], xt, [R("xt")], [])

    def phase_mod(l):
        phase()
        wbuf = [carve(4096, [128, KC, 512], BF16) for _ in range(2)]
        dma(badaT[:], b_adaT[l], (), [R("badaT")])
        dma(gnT[:, 0, :], gn1T[l], (), [R("gnT")])
        dma(gnT[:, 1, :], gn2T[l], (), [R("gnT")])
        wv = w_ada[l].rearrange("(kc p) n -> p kc n", p=128)
        for g in range(24):
            s = g % 2
            dma(wbuf[s], wv[:, :, g * 512:(g + 1) * 512], (), [R("wbuf", s)], q="pool")
            for j in range(4):
                ch = g * 4 + j
                pb = ch % 8
                for kc in range(KC):
                    mm(psum[pb][:, 0:2], wbuf[s][:, kc, j * 128:(j + 1) * 128], silucT[:, kc, :], kc == 0, kc == KC - 1,
                       [R("wbuf", s)], [Rps(pb)])
                act(modsb[:, ch, :], psum[pb][:, 0:2], AF.Identity, [Rps(pb), R("badaT")], [R("modsb")], bias=badaT[:, ch:ch + 1])
        for ni, (sci, gi) in enumerate(((1, 0), (4, 1))):
            for c in range(2):
                ts(a_sc[:, ni, :, c], modsb[:, sci * 16:(sci + 1) * 16, c], 1.0, None, ALU.add, None, [R("modsb")], [R("a_sc")])
                tt(a_sc[:, ni, :, c], a_sc[:, ni, :, c], gnT[:, gi, :], ALU.mult, [R("a_sc"), R("gnT")], [R("a_sc")])

    def phase_norm(ni, final=False):
        phase()
        H = carve(20480, [128, KC, NT], BF16)
        xt = carve(8192, [128, KC, 512])
        sq = [carve(512) for _ in range(2)]
        tmp = [carve(512) for _ in range(3)]
        rstd = carve(512)
        wk = [carve(2048) for _ in range(2)]
        for t in range(NTT):
            c = 0 if t == 0 else 1
            dma(xt, XTv[:, :, t * TT:(t + 1) * TT], (), [R("xt")])
            pb = t % 2
            for kc in range(KC):
                s = kc % 2
                act(sq[s], xt[:, kc, :], AF.Square, [R("xt")], [R("sq", s)])
                mm(psum[pb][:], ones_m[:], sq[s], kc == 0, kc == KC - 1, [R("sq", s)], [Rps(pb)])
            rsqrt(rstd, psum[pb][:], [Rps(pb)], [R("rstd")])
            if not final:
                for kc in range(KC):
                    s = kc % 3
                    stt(tmp[s], xt[:, kc, :], a_sc[:, ni, kc, c:c + 1], rstd, ALU.mult, ALU.mult,
                        [R("xt"), R("rstd")], [R("tmp", s)])
                    act(H[:, kc, t * TT:(t + 1) * TT], tmp[s], AF.Identity, [R("tmp", s)], [R("H", kc)],
                        bias=modsb[:, ni * 48 + kc, c:c + 1])
            else:
                for kc in range(KC):
                    stt(xt[:, kc, :], xt[:, kc, :], gnT[:, 2, kc:kc + 1], rstd, ALU.mult, ALU.mult,
                        [R("xt"), R("rstd")], [R("xt")])
                for b in range(4):
                    w_, Rw_ = wk[b % 2], R("wk", b % 2)
                    for g in range(4):
                        pb2 = 2 + (b * 4 + g) % 6
                        for j in range(4):
                            kc = g * 4 + j
                            tr(psum[pb2][:, j * 128:(j + 1) * 128], xt[:, kc, b * 128:(b + 1) * 128], ident[:], [R("xt")], [Rps(pb2)])
                        cp(w_[:, g * 512:(g + 1) * 512], psum[pb2][:], [Rps(pb2)], [Rw_], evac_eng())
                    tok0 = t * TT + b * 128
                    dma(y_out[tok0:tok0 + 128, :], w_, [Rw_], [])
        return H

    def phase_proj(wmat, col0, ncols, dst, dst_row0, func_of_col=None, groups=None):
        P.barrier()
        cnt["off"] = 20480
        H = arena[:, 0:20480].bitcast(BF16).rearrange("p (a b) -> p a b", a=KC)
        wbuf = [carve(4096, [128, KC, 512], BF16) for _ in range(2)]
        stg = [carve(2560) for _ in range(2)]
        wv = wmat.rearrange("(kc p) n -> p kc n", p=128)
        if groups is None:
            groups = []
            g0 = col0
            while g0 < col0 + ncols:
                gw = min(512, col0 + ncols - g0)
                groups.append((g0, gw))
                g0 += gw
        for gi, (g0, gw) in enumerate(groups):
            s = gi % 2
            dma(wbuf[s][:, :, 0:gw], wv[:, :, g0:g0 + gw], (), [R("wbuf", s)], q="pool")
            for j in range(0, gw, 128):
                cw = min(128, gw - j)
                col = g0 + j
                wi = uid() % 2
                fu = func_of_col(col) if func_of_col else None
                for t in range(NTT):
                    pb = uid() % 8
                    for kc in range(KC):
                        mm(psum[pb][0:cw, :], wbuf[s][:, kc, j:j + cw], H[:, kc, t * TT:(t + 1) * TT], kc == 0, kc == KC - 1,
                           [R("wbuf", s), R("H", kc)], [Rps(pb)])
                    o = stg[wi][0:cw, t * TT:(t + 1) * TT]
                    if fu is not None:
                        act(o, psum[pb][0:cw, :], fu, [Rps(pb)], [R("stg", wi)])
                    else:
                        cp(o, psum[pb][0:cw, :], [Rps(pb)], [R("stg", wi)], evac_eng())
                r0 = dst_row0 + (col - col0)
                dma(dst[r0:r0 + cw, :], stg[wi][0:cw, :], [R("stg", wi)], [])

    def conv3(out, in_, w3, segs, reads, writes):
        ts(out, in_, w3[:, 1:2], None, ALU.mult, None, reads, writes)
        for (a, b) in segs:
            stt(out[:, a + 1:b], in_[:, a:b - 1], w3[:, 0:1], out[:, a + 1:b], ALU.mult, ALU.add, list(reads) + list(writes), writes)
            stt(out[:, a:b - 1], in_[:, a + 1:b], w3[:, 2:3], out[:, a:b - 1], ALU.mult, ALU.add, list(reads) + list(writes), writes)

    def phase_pool(l):
        phase()
        PAD = 16
        Wd = NT + 6 * PAD
        ub = carve(Wd)
        la = carve(Wd)
        lb = carve(Wd)
        ic = carve(2304)
        dT = [carve(1280, None, BF16) for _ in range(2)]
        ys = [carve(1280, None, BF16) for _ in range(2)]
        pw = carve(256, [128, 4, 128], BF16)
        dma(prm[:, 0:4], pool_scT[l], (), [R("prm")])
        dma(pw, pool_w[l].rearrange("g c d -> c g d"), (), [R("pw")], q="pool")
        memset(ub, 0.0, [R("ub")])
        memset(la, 0.0, [R("la")])
        memset(lb, 0.0, [R("lb")])
        offs = [a + PAD * (2 * si + 1) for si, (a, T, _) in enumerate(SEQS)]
        for g in range(4):
            for si, (a, T, _) in enumerate(SEQS):
                dma(ub[:, offs[si]:offs[si] + T], U[g * 128:(g + 1) * 128, a:a + T], (), [R("ub")])
            dma(ic, c_invcnt[g], (), [R("ic")])
            tt(la[:, 1:Wd], ub[:, 0:Wd - 1], ub[:, 1:Wd], ALU.add, [R("ub")], [R("la")])
            cur, Rcur, oth, Roth = la, R("la"), lb, R("lb")
            sh = 1
            for lev in range(g):
                tt(oth[:, sh:Wd - sh], cur[:, 0:Wd - 2 * sh], cur[:, 2 * sh:Wd], ALU.add, [Rcur], [Roth])
                cur, Rcur, oth, Roth = oth, Roth, cur, Rcur
                sh *= 2
            d_ = dT[g % 2]
            for si, (a, T, _) in enumerate(SEQS):
                o = offs[si]
                ioff = 0 if T == 256 else 256
                tt(cur[:, o:o + T], cur[:, o:o + T], ic[:, ioff:ioff + T], ALU.mult, [Rcur, R("ic")], [Rcur])
                tt(d_[:, a:a + T], cur[:, o:o + T], ub[:, o:o + T], ALU.subtract, [Rcur, R("ub")], [R("dT", g % 2)])
            y_ = ys[g % 2]
            for t in range(NTT):
                pb = uid() % 8
                mm(psum[pb][:], pw[:, g, :], d_[:, t * TT:(t + 1) * TT], True, True, [R("pw"), R("dT", g % 2)], [Rps(pb)])
                ts(y_[:, t * TT:(t + 1) * TT], psum[pb][:], prm[:, g:g + 1], None, ALU.mult, None, [Rps(pb), R("prm")], [R("ys", g % 2)])
            dma(YT[g * 128:(g + 1) * 128, :], y_, [R("ys", g % 2)], [])

    def phase_gdn_pre(l):
        phase()
        raw = [carve(2048) for _ in range(2)]
        cvb = [carve(2048) for _ in range(2)]
        sqb = [carve(512) for _ in range(2)]
        rn = [carve(512) for _ in range(2)]
        stg = [carve(512, [128, 4, 128]) for _ in range(2)]
        Bt = carve(2048, parts=16)
        At = carve(2048, parts=16)
        Gf = carve(2048, parts=16)
        Gp = carve(2048, parts=16)
        EG = carve(2048, parts=16)
        BE = carve(2048, parts=16)
        ED = carve(2048, parts=16)
        td = carve(2048, parts=16)
        cw_ = carve(72, [128, 24, 3])
        dma(cw_, gconvT[l], (), [R("cw")])
        dma(prm[0:16, 8:9], alog[l], (), [R("prm")])
        dma(prm[0:16, 9:10], dtb[l], (), [R("prm")])
        act(prm[0:16, 10:11], prm[0:16, 8:9], AF.Exp, [R("prm")], [R("prm2")])
        ts(prm[0:16, 10:11], prm[0:16, 10:11], -1.0, None, ALU.mult, None, [R("prm2")], [R("prm2")])
        it = 0
        for (a, T, _) in SEQS:
            for which, r00 in enumerate((Q0, K0, V0)):
                for h in range(8):
                    b_ = it % 2
                    it += 1
                    x_, Rx = raw[b_][:, 0:T], R("raw", b_)
                    c_, Rcv = cvb[b_][:, 0:T], R("cvb", b_)
                    r0 = r00 + h * 128
                    dma(x_, U[r0:r0 + 128, a:a + T], (), [Rx])
                    conv3(c_, x_, cw_[:, which * 8 + h, :], [(0, T)], [Rx, R("cw")], [Rcv])
                    act(c_, c_, AF.Silu, [Rcv], [Rcv])
                    if which < 2:
                        for sl in range(0, T, 512):
                            w_ = min(512, T - sl)
                            sb_ = uid() % 2
                            pb = uid() % 8
                            act(sqb[sb_][:, 0:w_], c_[:, sl:sl + w_], AF.Square, [Rcv], [R("sqb", sb_)])
                            mm(psum[pb][:, 0:w_], ones1[:], sqb[sb_][:, 0:w_], True, True, [R("sqb", sb_)], [Rps(pb)])
                            rsqrt(rn[sb_][:, 0:w_], psum[pb][:, 0:w_], [Rps(pb)], [R("rn", sb_)])
                            scl = (128.0 ** -0.5) if which == 0 else 1.0
                            stt(c_[:, sl:sl + w_], c_[:, sl:sl + w_], scl, rn[sb_][:, 0:w_], ALU.mult, ALU.mult, [Rcv, R("rn", sb_)], [Rcv])
                    dma(QKVn[which * 1024 + h * 128: which * 1024 + (h + 1) * 128, a:a + T], c_, [Rcv], [])
                    if which >= 1:
                        for g in range(T // 512 if T >= 512 else 1):
                            nb = min(4, T // 128)
                            pb = uid() % 8
                            si_ = uid() % 2
                            for j in range(nb):
                                blk = g * 4 + j
                                tr(psum[pb][:, j * 128:(j + 1) * 128], c_[:, blk * 128:(blk + 1) * 128], ident[:], [Rcv], [Rps(pb)])
                            cp(stg[si_][:, 0:nb, :], psum[pb][:, 0:nb * 128].rearrange("p (j n) -> p j n", j=nb), [Rps(pb)], [R("stg", si_)], evac_eng())
                            t0 = a + g * 512
                            cc = (which - 1) * 1024 + h * 128
                            dma(KVtok[t0:t0 + nb * 128, cc:cc + 128].rearrange("(j p) c -> p j c", p=128), stg[si_][:, 0:nb, :], [R("stg", si_)], [])
            Rg = R("gate")
            dma(Bt[:, 0:T], U[BETA0:BETA0 + 16, a:a + T], (), [Rg])
            dma(At[:, 0:T], U[A0:A0 + 16, a:a + T], (), [Rg])
            act(Bt[:, 0:T], Bt[:, 0:T], AF.Sigmoid, [Rg], [Rg])
            act(At[:, 0:T], At[:, 0:T], AF.Exp, [Rg, R("prm")], [Rg], bias=prm[0:16, 9:10])
            act(At[:, 0:T], At[:, 0:T], AF.Ln, [Rg], [Rg], bias=ocol[0:16, :])
            ts(At[:, 0:T], At[:, 0:T], prm[0:16, 10:11], None, ALU.mult, None, [Rg, R("prm2")], [Rg])
            P.add("dve", lambda e, o=Gf[:, 0:T], d0=resetm[:, 0:T], d1=At[:, 0:T]: e.tensor_tensor_scan(o, d0, d1, 0.0, ALU.mult, ALU.add), [Rg], [Rg])
            nch = T // 64
            tot = Gf[:, 63:T:64].unsqueeze(2).to_broadcast([16, nch, 64])
            v3 = lambda ap: ap[:, 0:T].rearrange("p (c j) -> p c j", j=64)
            stt(v3(td), v3(Gf), -2.0, tot, ALU.mult, ALU.add, [Rg], [Rg])
            tt(td[:, 0:T], td[:, 0:T], At[:, 0:T], ALU.add, [Rg], [Rg])
            stt(Gp[:, 0:T], td[:, 0:T], isb[:, 0:1], Gf[:, 0:T], ALU.mult, ALU.add, [Rg], [Rg])
            act(EG[:, 0:T], Gp[:, 0:T], AF.Exp, [Rg], [Rg])
            tt(BE[:, 0:T], Bt[:, 0:T], EG[:, 0:T], ALU.mult, [Rg], [Rg])
            tt(v3(td), tot, v3(Gp), ALU.subtract, [Rg], [Rg])
            act(ED[:, 0:T], td[:, 0:T], AF.Exp, [Rg], [Rg])
            dma(GS[0, :, a:a + T], Gp[:, 0:T], [Rg], [])
            dma(GS[1, :, a:a + T], EG[:, 0:T], [Rg], [])
            for c in range(nch):
                pb = uid() % 8
                for k_, src in enumerate((Gp, Bt, BE, ED)):
                    tr(psum[pb][0:64, k_ * 16:(k_ + 1) * 16], src[:, c * 64:(c + 1) * 64], ident[0:16, 0:16], [Rg], [Rps(pb)])
                cp(tokS[:, a // 64 + c, :], psum[pb][0:64, 0:64], [Rps(pb)], [R("tokS")], evac_eng())

    def phase_gdn_main(l):
        phase()
        S = carve(1024, [128, 8, 128])
        qT = [carve(512, [128, 8, 64]) for _ in range(2)]
        kT = [carve(512, [128, 8, 64]) for _ in range(2)]
        ktok = [carve(1024, [64, 8, 128], parts=64) for _ in range(2)]
        vtok = [carve(1024, [64, 8, 128], parts=64) for _ in range(2)]
        GB = [carve(512, [64, 8, 64], parts=64) for _ in range(2)]
        EGB = [carve(512, [128, 8, 64]) for _ in range(2)]
        c8 = lambda: carve(512, [64, 8, 64], parts=64)
        XS, XTm, DmS, DmT, Lm, Mm, QKm = c8(), c8(), c8(), c8(), c8(), c8(), c8()
        Rtb = [c8(), c8()]
        Lp = [c8(), c8()]
        Mp = [c8(), c8()]
        vtb = carve(1024, [64, 8, 128], parts=64)
        ktb = carve(1024, [64, 8, 128], parts=64)
        kd = carve(1024, [64, 8, 128], parts=64)
        vnew = carve(1024, [64, 8, 128], parts=64)
        wT = carve(512, [128, 8, 64])
        qg = carve(512, [128, 8, 64])
        ost = [carve(512, [128, 8, 64]) for _ in range(2)]
        I64 = ident[0:64, 0:64]
        step = 0
        for si, (a, T, is_s) in enumerate(SEQS):
            nch = T // 64
            for d in range(2):
                RS = [R("S", h) for h in range(8)]
                if is_s:
                    dma(S, s0[l, d].rearrange("h k v -> k h v"), (), RS)
                else:
                    memset(S, 0.0, RS)
                mS = masks[:, 2 * d, :]
                mT = masks[:, 2 * d + 1, :]
                for s_ in range(min(nch, DBG.get("gdn_steps", 10 ** 9))):
                    c = s_ if d == 0 else nch - 1 - s_
                    b_ = step % 2
                    step += 1
                    t0 = a + c * 64
                    cg = t0 // 64
                    Rl = R("ld", b_)
                    dma(qT[b_], QKVn[0:1024, t0:t0 + 64].rearrange("(h p) t -> p h t", p=128), (), [Rl])
                    dma(kT[b_], QKVn[1024:2048, t0:t0 + 64].rearrange("(h p) t -> p h t", p=128), (), [Rl])
                    dma(ktok[b_], KVtok[t0:t0 + 64, 0:1024].rearrange("p (h d) -> p h d", h=8), (), [Rl])
                    dma(vtok[b_], KVtok[t0:t0 + 64, 1024:2048].rearrange("p (h d) -> p h d", h=8), (), [Rl])
                    gsrc = GS[0, d * 8:(d + 1) * 8, t0:t0 + 64]
                    dma(GB[b_], bass.AP(gsrc.tensor, gsrc.offset, [[0, 64], [NT, 8], [1, 64]]), (), [Rl])
                    esrc = GS[1, d * 8:(d + 1) * 8, t0:t0 + 64]
                    dma(EGB[b_], bass.AP(esrc.tensor, esrc.offset, [[0, 128], [NT, 8], [1, 64]]), (), [Rl])
                    col = lambda kind, h: tokS[:, cg, kind * 16 + d * 8 + h: kind * 16 + d * 8 + h + 1]
                    HR = lambda n, h: R(n, h)
                    for h in range(8):
                        mm(psum[h][0:64, 0:64], kT[b_][:, h, :], kT[b_][:, h, :], True, True, [Rl], [R("ps", h)])
                        mm(psum[h][0:64, 64:128], kT[b_][:, h, :], qT[b_][:, h, :], True, True, [Rl], [R("ps", h)])
                    if DBG.get("gdn_stage", "z") == "0":
                        continue
                    for h in range(8):
                        stt(XS[:, h, :], GB[b_][:, h, :], col(0, h), mS, ALU.subtract, ALU.add, [Rl, R("tokS")], [HR("XS", h)])
                        act(DmS[:, h, :], XS[:, h, :], AF.Exp, [HR("XS", h)], [HR("DmS", h)], scale=-1.0)
                        stt(XTm[:, h, :], GB[b_][:, h, :], col(0, h), mT, ALU.subtract, ALU.add, [Rl, R("tokS")], [HR("XT", h)])
                        act(DmT[:, h, :], XTm[:, h, :], AF.Exp, [HR("XT", h)], [HR("DmT", h)])
                    if DBG.get("gdn_stage", "z") == "a":
                        continue
                    for h in range(8):
                        stt(Lm[:, h, :], psum[h][0:64, 0:64], col(1, h), DmS[:, h, :], ALU.mult, ALU.mult, [R("ps", h), HR("DmS", h), R("tokS")], [HR("L", h)])
                        tt(QKm[:, h, :], psum[h][0:64, 64:128], DmT[:, h, :], ALU.mult, [R("ps", h), HR("DmT", h)], [HR("QKm", h)])
                    if DBG.get("c_sub", 9) < 2:
                        continue
                    for h in range(8):
                        tr(psum[h][0:64, 128:192], Lm[:, h, :], I64, [HR("L", h)], [R("ps", h)])
                    if DBG.get("c_sub", 9) < 3:
                        continue
                    for h in range(8):
                        cp(Mm[:, h, :], psum[h][0:64, 128:192], [R("ps", h)], [HR("M", h)], "act")
                        stt(Rtb[0][:, h, :], psum[h][0:64, 128:192], -1.0, I64, ALU.mult, ALU.add, [R("ps", h)], [HR("Rt0", h)])
                    if DBG.get("gdn_stage", "z") == "b":
                        continue
                    for h in range(8):
                        mm(psum[h][0:64, 256:320], Mm[:, h, :], Lm[:, h, :], True, True, [HR("M", h), HR("L", h)], [R("ps", h)])
                        mm(psum[h][0:64, 128:192], Lm[:, h, :], Mm[:, h, :], True, True, [HR("M", h), HR("L", h)], [R("ps", h)])
                    if DBG.get("d_sub", 9) < 1:
                        continue
                    for h in range(8):
                        cp(Lp[0][:, h, :], psum[h][0:64, 256:320], [R("ps", h)], [HR("Lp0", h)], "act")
                        cp(Mp[0][:, h, :], psum[h][0:64, 128:192], [R("ps", h)], [HR("Mp0", h)], "dve")
                    Rt, RtN = Rtb[0], "Rt0"
                    for lev in range(min(5, DBG.get("d_lev", 5))):
                        i_ = lev % 2
                        Rt, RtN = Rtb[lev % 2], f"Rt{lev % 2}"
                        Rt2, Rt2N = Rtb[1 - lev % 2], f"Rt{1 - lev % 2}"
                        o_ = 1 - i_
                        last = lev == 4
                        for h in range(8):
                            if not last:
                                mm(psum[h][0:64, 128:192], Lp[i_][:, h, :], Mp[i_][:, h, :], True, True, [HR(f"Lp{i_}", h), HR(f"Mp{i_}", h)], [R("ps", h)])
                            mm(psum[h][0:64, 192:256], Lp[i_][:, h, :], Rt[:, h, :], True, True, [HR(f"Lp{i_}", h), HR(RtN, h)], [R("ps", h)])
                            if not last:
                                mm(psum[h][0:64, 256:320], Mp[i_][:, h, :], Lp[i_][:, h, :], True, True, [HR(f"Lp{i_}", h), HR(f"Mp{i_}", h)], [R("ps", h)])
                        for h in range(8):
                            if DBG.get("lev_sub", 9) >= 1:
                                tt(Rt2[:, h, :], psum[h][0:64, 192:256], Rt[:, h, :], ALU.add, [R("ps", h), HR(RtN, h)], [HR(Rt2N, h)])
                            if not last and DBG.get("lev_sub", 9) >= 2:
                                cp(Mp[o_][:, h, :], psum[h][0:64, 128:192], [R("ps", h)], [HR(f"Mp{o_}", h)], "act")
                                cp(Lp[o_][:, h, :], psum[h][0:64, 256:320], [R("ps", h)], [HR(f"Lp{o_}", h)], "act" if h % 2 else "dve")
                    if DBG.get("gdn_stage", "z") == "d":
                        continue
                    nlev_ = min(5, DBG.get("d_lev", 5))
                    Rt, RtN = Rtb[nlev_ % 2], f"Rt{nlev_ % 2}"
                    for h in range(8):
                        ts(vtb[:, h, :], vtok[b_][:, h, :], col(1, h), None, ALU.mult, None, [Rl, R("tokS")], [HR("vtb", h)])
                        ts(ktb[:, h, :], ktok[b_][:, h, :], col(2, h), None, ALU.mult, None, [Rl, R("tokS")], [HR("ktb", h)])
                        ts(kd[:, h, :], ktok[b_][:, h, :], col(3, h), None, ALU.mult, None, [Rl, R("tokS")], [HR("kd", h)])
                        tt(qg[:, h, :], qT[b_][:, h, :], EGB[b_][:, h, :], ALU.mult, [Rl], [HR("qg", h)])
                    for h in range(8):
                        mm(psum[h][:, 448:512], ktb[:, h, :], Rt[:, h, :], True, True, [HR("ktb", h), HR(RtN, h)], [R("ps", h)])
                    for h in range(8):
                        act(wT[:, h, :], psum[h][:, 448:512], AF.Copy, [R("ps", h)], [HR("wT", h)], scale=-1.0)
                    if DBG.get("gdn_stage", "z") == "e":
                        continue
                    for h in range(8):
                        mm(psum[h][0:64, 320:448], Rt[:, h, :], vtb[:, h, :], True, False, [HR(RtN, h), HR("vtb", h)], [R("ps", h)])
                        mm(psum[h][0:64, 320:448], wT[:, h, :], S[:, h, :], False, True, [HR("wT", h), R("S", h)], [R("ps", h)])
                    for h in range(8):
                        cp(vnew[:, h, :], psum[h][0:64, 320:448], [R("ps", h)], [HR("vnew", h)], "act" if h % 2 else "dve")
                    if DBG.get("gdn_stage", "z") == "f":
                        continue
                    ob_ = ost[b_]
                    for h in range(8):
                        mm(psum[h][:, 0:64], S[:, h, :], qg[:, h, :], True, False, [R("S", h), HR("qg", h)], [R("ps", h)])
                        mm(psum[h][:, 0:64], vnew[:, h, :], QKm[:, h, :], False, True, [HR("vnew", h), HR("QKm", h)], [R("ps", h)])
                    for h in range(8):
                        cp(ob_[:, h, :], psum[h][:, 0:64], [R("ps", h)], [R("ost", b_)], "act" if h % 2 else "dve")
                    dma(OFB[d, :, t0:t0 + 64].rearrange("(h p) t -> p h t", p=128), ob_, [R("ost", b_)], [], q="pool")
                    if DBG.get("gdn_stage", "z") == "g":
                        continue
                    gcol = 63 if d == 0 else 0
                    for h in range(8):
                        mm(psum[h][:, 128:256], kd[:, h, :], vnew[:, h, :], True, True, [HR("kd", h), HR("vnew", h)], [R("ps", h), R("ps", h)])
                    for h in range(8):
                        ts(S[:, h, :], S[:, h, :], EGB[b_][:, h, gcol:gcol + 1], None, ALU.mult, None, [Rl, R("S", h)], [R("S", h)])
                        tt(S[:, h, :], psum[h][:, 128:256], S[:, h, :], ALU.add, [R("ps", h), R("ps", h), R("S", h)], [R("S", h)])
                if not is_s:
                    dma(ns_out[si, l, d].rearrange("h k v -> k h v"), S, [R("S", h) for h in range(8)], [], q="pool")

    def phase_gdn_post(l):
        phase()
        of_ = [carve(2560) for _ in range(2)]
        ob_ = [carve(2560) for _ in range(2)]
        zz = [carve(2560) for _ in range(2)]
        sq = [carve(512) for _ in range(2)]
        rs = [carve(512) for _ in range(2)]
        yb = [carve(1280, None, BF16) for _ in range(2)]
        dma(prm[:, 16:17], gng[l], (), [R("prm")])
        for h in range(8):
            b_ = h % 2
            dma(of_[b_], OFB[0, h * 128:(h + 1) * 128, :], (), [R("of", b_)])
            dma(ob_[b_], OFB[1, h * 128:(h + 1) * 128, :], (), [R("ob", b_)])
            dma(zz[b_], U[Z0 + h * 128:Z0 + (h + 1) * 128, :], (), [R("zz", b_)])
            tt(of_[b_], of_[b_], ob_[b_], ALU.add, [R("of", b_), R("ob", b_)], [R("of", b_)])
            for t in range(NTT):
                s = uid() % 2
                pb = uid() % 8
                sl = slice(t * TT, (t + 1) * TT)
                act(sq[s], of_[b_][:, sl], AF.Square, [R("of", b_)], [R("sq", s)])
                mm(psum[pb][:], ones128[:], sq[s], True, True, [R("sq", s)], [Rps(pb)])
                rsqrt(rs[s], psum[pb][:], [Rps(pb)], [R("rs", s)])
                stt(of_[b_][:, sl], of_[b_][:, sl], prm[:, 16:17], rs[s], ALU.mult, ALU.mult, [R("of", b_), R("rs", s), R("prm")], [R("of", b_)])
                tt(yb[b_][:, sl], of_[b_][:, sl], zz[b_][:, sl], ALU.mult, [R("of", b_), R("zz", b_)], [R("yb", b_)])
            dma(YT[512 + h * 128:512 + (h + 1) * 128, :], yb[b_], [R("yb", b_)], [])

    def phase_na(l):
        phase()
        BM = carve(7680, [64, 120, 64], parts=64)
        vtk = carve(10240, [64, 40, 512], BF16, parts=64)
        kvfb = carve(8192, parts=64)
        kvf = [kvfb[:, i * 4096:(i + 1) * 4096].rearrange("p (a b) -> p a b", a=8) for i in range(2)]
        kctx = carve(2048, [64, 8, 512], BF16, parts=64)
        vctx = carve(1024, [128, 4, 512], BF16)
        fr = [carve(2560) for _ in range(2)]
        qTh = [carve(1280, None, BF16, parts=64) for _ in range(2)]
        kTh = [carve(1280, None, BF16, parts=64) for _ in range(2)]
        sc = [carve(512, parts=64) for _ in range(2)]
        pT = [carve(256, None, BF16, parts=64) for _ in range(2)]
        pc = [carve(128, None, BF16) for _ in range(2)]
        rden = [carve(256, parts=64) for _ in range(2)]
        yst = [carve(1280, None, BF16, parts=64) for _ in range(2)]
        RB = R("BM")
        src = rpbp[l]
        BHv = kvfb
        for h_ in range(8):
            dma(BHv[:, h_ * 960:(h_ + 1) * 960].rearrange("p (a b) -> p a b", a=15),
                bass.AP(src.tensor, src.offset + h_ * 15 * 128, [[1, 64], [128, 15], [1, 64]]), (), [R("kvf", 0), R("kvf", 1)])
        BMv = BM.rearrange("p a b -> p (a b)")
        for j in range(15):
            pb = uid() % 8
            mm(psum[pb][0:64, :], Jm[:, :], BHv[:, j * 512:(j + 1) * 512], True, True, [R("kvf", 0), R("kvf", 1)], [Rps(pb)])
            cmb = colmask[:, :].unsqueeze(1).to_broadcast([64, 8, 64])
            tt(BMv[:, j * 512:(j + 1) * 512].rearrange("p (a b) -> p a b", a=8), psum[pb][0:64, :].rearrange("p (a b) -> p a b", a=8), cmb, ALU.add, [Rps(pb)], [RB])
        dma(kctx, ck[l].rearrange("h d k -> d h k"), (), [R("kctx")], q="pool")
        dma(vctx, cvv[l].rearrange("(b p) c -> p b c", p=128), (), [R("vctx")], q="pool")
        for which, r00 in ((0, NAK0), (1, NAV0)):
            for chn in range(4):
                b_ = uid() % 2
                dma(fr[b_], U[r00 + chn * 128:r00 + (chn + 1) * 128, :], (), [R("fr", b_)])
                nrow = 40 if which == 1 else 8
                for r4 in range(0, nrow, 4):
                    pb = uid() % 8
                    for j in range(4):
                        r = r4 + j
                        tr(psum[pb][0:64, j * 128:(j + 1) * 128], fr[b_][:, r * 64:(r + 1) * 64], ident[:], [R("fr", b_)], [Rps(pb)])
                    src_ = psum[pb][0:64, :].rearrange("p (j n) -> p j n", j=4)
                    if which == 1:
                        cp(vtk[:, r4:r4 + 4, chn * 128:(chn + 1) * 128], src_, [Rps(pb)], [R("vtk")], evac_eng())
                    if r4 < 8:
                        cp(kvf[which][:, r4:r4 + 4, chn * 128:(chn + 1) * 128], src_, [Rps(pb)], [R("kvf", which)], evac_eng())
        for sq_ in range(2):
            dma(nk_out[sq_, l].rearrange("(r p) c -> p r c", p=64), kvf[0][:, sq_ * 4:(sq_ + 1) * 4, :], [R("kvf", 0)], [])
            dma(nv_out[sq_, l].rearrange("(r p) c -> p r c", p=64), kvf[1][:, sq_ * 4:(sq_ + 1) * 4, :], [R("kvf", 1)], [])
        for h in range(8):
            b_ = h % 2
            dma(qTh[b_], U[NAQ0 + h * 64:NAQ0 + (h + 1) * 64, :], (), [R("qTh", b_)], q="pool")
            dma(kTh[b_], U[NAK0 + h * 64:NAK0 + (h + 1) * 64, :], (), [R("kTh", b_)], q="pool")
            hs = slice(h * 64, (h + 1) * 64)
            for sq_ in range(2):
                for qb in range(4):
                    q0 = sq_ * 256 + qb * 64
                    i_ = uid() % 2
                    pS, pO, pD = (uid() % 2) * 4, (uid() % 2) * 4 + 1, (uid() % 2) * 4 + 2
                    for kr in range(4):
                        k0 = sq_ * 256 + kr * 64
                        mm(psum[pS][0:64, kr * 64:(kr + 1) * 64], kTh[b_][:, k0:k0 + 64], qTh[b_][:, q0:q0 + 64], True, True,
                           [R("qTh", b_), R("kTh", b_)], [Rps(pS)])
                    act(pT[i_][:, 0:256], psum[pS][0:64, 0:256], AF.Exp, [Rps(pS)], [R("pT", i_)], scale=0.125)
                    for kr in range(4):
                        mm(psum[pO][0:64, 0:64], vtk[:, sq_ * 4 + kr, hs], pT[i_][:, kr * 64:(kr + 1) * 64], kr == 0, kr == 3,
                           [R("vtk"), R("pT", i_)], [Rps(pO)])
                    for kr in range(4):
                        mm(psum[pD][0:64, 0:64], onesb[0:64, 0:64], pT[i_][:, kr * 64:(kr + 1) * 64], kr == 0, kr == 3,
                           [R("pT", i_)], [Rps(pD)])
                    P.add("dve", lambda e, o=rden[i_][:, 0:64], i=psum[pD][0:64, 0:64]: e.reciprocal(o, i), [Rps(pD)], [R("rden", i_)])
                    tt(yst[b_][:, q0:q0 + 64], psum[pO][0:64, 0:64], rden[i_][:, 0:64], ALU.mult, [Rps(pO), R("rden", i_)], [R("yst", b_)])
            for r in range(32):
                q0 = 512 + r * 64
                kr0 = min(max(r - 4, 0), 24)
                d0 = kr0 - r + 7
                i_ = uid() % 2
                base = (r % 2) * 4
                pS, pC, pO, pD = base, base + 1, base + 2, base + 3
                for i in range(8):
                    k0 = 512 + (kr0 + i) * 64
                    mm(psum[pS][0:64, i * 64:(i + 1) * 64], kTh[b_][:, k0:k0 + 64], qTh[b_][:, q0:q0 + 64], True, True,
                       [R("qTh", b_), R("kTh", b_)], [Rps(pS)])
                for cb in range(4):
                    mm(psum[pC][:, cb * 64:(cb + 1) * 64], kctx[:, h, cb * 128:(cb + 1) * 128], qTh[b_][:, q0:q0 + 64], True, True,
                       [R("qTh", b_), R("kctx")], [Rps(pC)])
                bm = BM[:, h * 15 + d0:h * 15 + d0 + 8, :]
                stt(sc[i_].rearrange("p (a b) -> p a b", a=8), psum[pS][0:64, :].rearrange("p (a b) -> p a b", a=8), 0.125, bm, ALU.mult, ALU.add,
                    [Rps(pS), RB], [R("sc", i_)])
                act(pT[i_], sc[i_], AF.Exp, [R("sc", i_)], [R("pT", i_)])
                act(pc[i_], psum[pC][:, 0:256], AF.Exp, [Rps(pC)], [R("pc", i_)], scale=0.125)
                for i in range(8):
                    mm(psum[pO][0:64, 0:64], vtk[:, 8 + kr0 + i, hs], pT[i_][:, i * 64:(i + 1) * 64], i == 0, False,
                       [R("vtk"), R("pT", i_)], [Rps(pO)])
                for cb in range(4):
                    mm(psum[pO][0:64, 0:64], vctx[:, cb, hs], pc[i_][:, cb * 64:(cb + 1) * 64], False, cb == 3,
                       [R("vctx"), R("pc", i_)], [Rps(pO)])
                for i in range(8):
                    mm(psum[pD][0:64, 0:64], onesb[0:64, 0:64], pT[i_][:, i * 64:(i + 1) * 64], i == 0, False, [R("pT", i_)], [Rps(pD)])
                for cb in range(4):
                    mm(psum[pD][0:64, 0:64], onesb[:, 0:64], pc[i_][:, cb * 64:(cb + 1) * 64], False, cb == 3, [R("pc", i_)], [Rps(pD)])
                P.add("dve", lambda e, o=rden[i_][:, 0:64], i=psum[pD][0:64, 0:64]: e.reciprocal(o, i), [Rps(pD)], [R("rden", i_)])
                tt(yst[b_][:, q0:q0 + 64], psum[pO][0:64, 0:64], rden[i_][:, 0:64], ALU.mult, [Rps(pO), R("rden", i_)], [R("yst", b_)])
            dma(YT[1536 + h * 64:1536 + (h + 1) * 64, :], yst[b_], [R("yst", b_)], [])

    def load_H(src):
        H = carve(20480, [128, KC, NT], BF16)
        for kc in range(KC):
            dma(H[:, kc, :], src[kc * 128:(kc + 1) * 128, :], (), [R("H", kc)])
        return H

    def phase_merge(l):
        phase()
        H = load_H(YT)
        wbuf = [carve(4096, [128, KC, 512], BF16) for _ in range(2)]
        gt = [carve(1536, [128, 3, 512]) for _ in range(2)]
        t0_ = [carve(512) for _ in range(2)]
        t1_ = [carve(512) for _ in range(2)]
        ms = [carve(1280, None, BF16) for _ in range(2)]
        wv = wb[l].rearrange("(kc p) n -> p kc n", p=128)
        KR = ((0, 4), (4, 12), (12, 16))
        for g in range(4):
            s = g % 2
            dma(wbuf[s], wv[:, :, g * 512:(g + 1) * 512], (), [R("wbuf", s)], q="pool")
            for j in range(4):
                oc = g * 4 + j
                mb = oc % 2
                for t in range(NTT):
                    gi = uid() % 2
                    sl = slice(t * TT, (t + 1) * TT)
                    for i in range(3):
                        r0 = GATE0 + i * 2048 + oc * 128
                        dma(gt[gi][:, i, :], U[r0:r0 + 128, sl], (), [R("gt", gi)])
                    pbs = [(uid() % 2) * 4 + i for i in range(3)]
                    for i, (k0_, k1_) in enumerate(KR):
                        for kc in range(k0_, k1_):
                            mm(psum[pbs[i]][:], wbuf[s][:, kc, j * 128:(j + 1) * 128], H[:, kc, sl], kc == k0_, kc == k1_ - 1,
                               [R("wbuf", s), R("H", kc)], [Rps(pbs[i])])
                    tt(t0_[gi], psum[pbs[0]][:], gt[gi][:, 0, :], ALU.mult, [Rps(pbs[0]), R("gt", gi)], [R("t0", gi)])
                    tt(t1_[gi], psum[pbs[1]][:], gt[gi][:, 1, :], ALU.mult, [Rps(pbs[1]), R("gt", gi)], [R("t1", gi)])
                    tt(t0_[gi], t0_[gi], t1_[gi], ALU.add, [R("t0", gi), R("t1", gi)], [R("t0", gi)])
                    tt(t1_[gi], psum[pbs[2]][:], gt[gi][:, 2, :], ALU.mult, [Rps(pbs[2]), R("gt", gi)], [R("t1", gi)])
                    tt(ms[mb][:, sl], t0_[gi], t1_[gi], ALU.add, [R("t0", gi), R("t1", gi)], [R("ms", mb)])
                dma(MT[oc * 128:(oc + 1) * 128, :], ms[mb], [R("ms", mb)], [])

    def phase_wout(l):
        phase()
        H = load_H(MT)
        wbuf = [carve(4096, [128, KC, 512], BF16) for _ in range(2)]
        xs = [carve(2560) for _ in range(2)]
        wv = w_out[l].rearrange("(kc p) n -> p kc n", p=128)
        for g in range(4):
            s = g % 2
            dma(wbuf[s], wv[:, :, g * 512:(g + 1) * 512], (), [R("wbuf", s)], q="pool")
            for j in range(4):
                oc = g * 4 + j
                xb = oc % 2
                dma(xs[xb], XT[oc * 128:(oc + 1) * 128, :], (), [R("xs", xb)])
                for t in range(NTT):
                    c = 0 if t == 0 else 1
                    sl = slice(t * TT, (t + 1) * TT)
                    pb = uid() % 8
                    for kc in range(KC):
                        mm(psum[pb][:], wbuf[s][:, kc, j * 128:(j + 1) * 128], H[:, kc, sl], kc == 0, kc == KC - 1,
                           [R("wbuf", s), R("H", kc)], [Rps(pb)])
                    stt(xs[xb][:, sl], psum[pb][:], modsb[:, 32 + oc, c:c + 1], xs[xb][:, sl], ALU.mult, ALU.add,
                        [Rps(pb), R("xs", xb)], [R("xs", xb)])
                dma(XT[oc * 128:(oc + 1) * 128, :], xs[xb], [R("xs", xb)], [])

    def phase_ffn_down(l):
        phase()
        actb = carve(22528, [128, 44, 1024], BF16)
        ab = [carve(2080, [128, 4, 520]) for _ in range(4)]
        cb_ = [carve(2080, [128, 4, 520]) for _ in range(2)]
        wd = [carve(2816, [128, 44, 128], BF16) for _ in range(2)]
        xs = [carve(512) for _ in range(4)]
        fw = carve(264, [128, 88, 3])
        dma(fw, fconvT[l], (), [R("fw")])
        wv = w_down[l].rearrange("(kc p) n -> p kc n", p=128)
        it = 0
        xi = 0
        for grp in ((0, 1), (2, 3), (4,)):
            for gi, t in enumerate(grp):
                lo = t * TT
                hl = t > 1
                hr = 1 <= t < 4
                segs = [(1, 257), (257, 513)] if t == 0 else [(0, 514)]
                for c4 in range(0, 44, 4):
                    bufs = []
                    for half in range(2):
                        bi = (it % 2) * 2 + half
                        r0 = half * DFF + c4 * 128
                        a_, Ra_ = ab[bi], R("ab", bi)
                        c0 = lo - (1 if hl else 0)
                        c1 = lo + TT + (1 if hr else 0)
                        o0 = 0 if hl else 1
                        if not hl:
                            memset(a_[:, :, 0:1], 0.0, [Ra_])
                        if not hr:
                            memset(a_[:, :, 513:514], 0.0, [Ra_])
                        dma(a_[:, :, o0:o0 + (c1 - c0)], U[r0:r0 + 512, c0:c1].rearrange("(j p) t -> p j t", p=128), (), [Ra_])
                        cv_, Rc_ = cb_[half], R("cb", half)
                        for j in range(4):
                            conv3(cv_[:, j, 0:514], a_[:, j, 0:514], fw[:, half * 44 + c4 + j, :], segs, [Ra_, R("fw")], [Rc_])
                        bufs.append((cv_, Rc_))
                    it += 1
                    (ca, Rca), (cbb, Rcb) = bufs
                    act(ca[:, :, 1:513], ca[:, :, 1:513], AF.Silu, [Rca], [Rca])
                    tt(actb[:, c4:c4 + 4, gi * TT:(gi + 1) * TT], ca[:, :, 1:513], cbb[:, :, 1:513], ALU.mult, [Rca, Rcb],
                       [R("actb", c4 + j) for j in range(4)])
            for oc in range(16):
                s = oc % 2
                dma(wd[s], wv[:, :, oc * 128:(oc + 1) * 128], (), [R("wd", s)], q="pool")
                for gi, t in enumerate(grp):
                    c = 0 if t == 0 else 1
                    lo = t * TT
                    x_ = xi % 4
                    xi += 1
                    dma(xs[x_], XT[oc * 128:(oc + 1) * 128, lo:lo + TT], (), [R("xs", x_)])
                    pb = uid() % 8
                    for kc in range(44):
                        mm(psum[pb][:], wd[s][:, kc, :], actb[:, kc, gi * TT:(gi + 1) * TT], kc == 0, kc == 43, [R("wd", s), R("actb", kc)], [Rps(pb)])
                    stt(xs[x_], psum[pb][:], modsb[:, 80 + oc, c:c + 1], xs[x_], ALU.mult, ALU.add, [Rps(pb), R("xs", x_)], [R("xs", x_)])
                    dma(XT[oc * 128:(oc + 1) * 128, lo:lo + TT], xs[x_], [R("xs", x_)], [])

    def in_func(col):
        if Z0 <= col < BETA0:
            return AF.Silu
        if col >= GATE0:
            return AF.Sigmoid
        return None

    IN_GROUPS = [(g * 512, 512) for g in range(9)] + [(4608, 32)] + [(4640 + g * 512, 512) for g in range(15)]

    if plan is None:
        plan = ["consts", "intr"]
        for l in range(L):
            plan += [("mod", l), ("norm1", l), ("pool", l), ("gdn_pre", l), ("gdn_main", l), ("gdn_post", l), ("na", l),
                     ("merge", l), ("wout", l), ("norm2", l), ("ffn", l)]
        plan += ["final"]
    for p in plan:
        if p == "consts":
            phase_consts()
        elif p == "intr":
            phase_in_transpose()
        elif p == "final":
            phase_norm(0, final=True)
        else:
            nm, l = p
            if nm == "mod":
                phase_mod(l)
            elif nm == "norm1":
                phase_norm(0)
                phase_proj(w_in[l], 0, N_IN, U, 0, in_func, IN_GROUPS)
            elif nm == "pool":
                phase_pool(l)
            elif nm == "gdn_pre":
                phase_gdn_pre(l)
            elif nm == "gdn_main":
                phase_gdn_main(l)
            elif nm == "gdn_post":
                phase_gdn_post(l)
            elif nm == "na":
                phase_na(l)
            elif nm == "merge":
                phase_merge(l)
            elif nm == "wout":
                phase_wout(l)
            elif nm == "norm2":
                phase_norm(1)
                phase_proj(w_up[l], 0, 2 * DFF, U, 0)
            elif nm == "ffn":
                phase_ffn_down(l)
    P.emit()
    st.close()
    return nc


def make_consts():
    misc = np.zeros((128, 640), np.float32)
    misc[0:64, 512:576] = np.eye(64, dtype=np.float32)[::-1]
    misc[:, 0:128] = np.eye(128, dtype=np.float32)
    i = np.arange(64)[:, None]
    j = np.arange(64)[None, :]
    m = np.zeros((64, 4, 64), np.float32)
    m[:, 0, :] = np.where(j >= i, BIG, 0.0)
    m[:, 1, :] = np.where(j < i, -BIG, 0.0)
    m[:, 2, :] = np.where(j <= i, BIG, 0.0)
    m[:, 3, :] = np.where(j > i, -BIG, 0.0)
    misc[0:64, 128:384] = m.reshape(64, 256)
    qc = np.arange(64)
    cs = np.clip(qc - 8, 0, 48)
    kc = np.arange(64)[:, None]
    ok = (kc >= cs[None, :]) & (kc < cs[None, :] + 16)
    misc[0:64, 384:448] = np.where(ok, 0.0, -BIG)
    misc[0:16, 448] = (np.arange(16) >= 8).astype(np.float32)
    reset = np.ones((16, 2048), np.float32)
    reset[:, 0::64] = 0.0
    inv = np.zeros((4, 128, 2304), np.float32)
    for g, w in enumerate((2, 4, 8, 16)):
        for off, T in ((0, 256), (256, 2048)):
            t = np.arange(T)
            lo = np.maximum(t - w // 2, 0)
            hi = np.minimum(t + w - 1 - w // 2, T - 1)
            inv[g, :, off:off + T] = (1.0 / (hi - lo + 1).astype(np.float32))[None, :]
    return misc, reset, inv


def make_inputs(inp):
    f = lambda a: np.ascontiguousarray(np.asarray(a, dtype=np.float32))
    misc, reset, inv = make_consts()
    tr128 = lambda v: f(v.reshape(-1, 128).T)
    rp = f(inp["na_rpb"])
    rpbp = np.zeros((L, 8, 15, 128), np.float32)
    rpbp[..., 48:79] = rp[..., ::-1]
    shared = {
        "w_ada": f(inp["w_ada"]),
        "b_adaT": f(np.stack([tr128(inp["b_ada"][l]) for l in range(L)])),
        "gn1T": f(np.stack([tr128(inp["g_norm1"][l]) for l in range(L)])),
        "gn2T": f(np.stack([tr128(inp["g_norm2"][l]) for l in range(L)])),
        "gfT": tr128(f(inp["g_final"])),
        "w_in": f(inp["w_in"]),
        "pool_w": f(inp["pool_w"]),
        "pool_scT": f(np.stack([tr128(inp["pool_scale"][l]) for l in range(L)])),
        "gconvT": f(np.asarray(inp["gdn_conv"]).reshape(L, 3, 24, 128).transpose(0, 3, 2, 1)),
        "alog": f(np.asarray(inp["gdn_a_log"]).reshape(L, 16, 1)),
        "dtb": f(np.asarray(inp["gdn_dt_bias"]).reshape(L, 16, 1)),
        "gng": f(np.asarray(inp["gdn_norm_g"]).reshape(L, 128, 1)),
        "rpbp": rpbp,
        "wb": f(np.concatenate([inp["w_branch_pool"], inp["w_branch_gdn"], inp["w_branch_na"]], axis=1)),
        "w_out": f(inp["w_out"]),
        "w_up": f(inp["w_up"]),
        "fconvT": f(np.asarray(inp["ffn_conv"]).reshape(L, 3, 88, 128).transpose(0, 3, 2, 1)),
        "w_down": f(inp["w_down"]),
        "c_invcnt": inv,
        "c_misc": misc,
        "c_reset": reset,
    }
    xp = np.asarray(inp["x_prompt"], np.float32)
    xs = np.asarray(inp["x_sample"], np.float32)
    maps = []
    for c in range(8):
        b = c % 2
        m = dict(shared)
        m["xin"] = f(np.concatenate([xp[2 * c], xp[2 * c + 1], xs[b]], axis=0))
        cond = np.stack([np.asarray(inp["c_ctx"], np.float32), np.asarray(inp["c"], np.float32)[b]], axis=0)
        m["condT"] = f(cond.reshape(2, KC, 128).transpose(2, 1, 0))
        m["ck"] = f(np.asarray(inp["cache_na_k"])[b].transpose(0, 2, 3, 1))
        m["cvv"] = f(np.asarray(inp["cache_na_v"])[b].reshape(L, 512, 512))
        m["s0"] = f(np.asarray(inp["state_gdn"])[b])
        maps.append(m)
    return maps


_NC = {}


def kernel(**inputs):
    import os
    ncores = int(os.environ.get("KCORES", "8"))
    if "nc" not in _NC:
        if os.environ.get("KPLAN") == "gdn1":
            DBG["gdn_steps"] = 1
            _NC["nc"] = build(plan=["consts", ("gdn_main", 0)])
        else:
            _NC["nc"] = build()
    nc = _NC["nc"]
    maps = make_inputs(inputs)[:ncores]
    res = run_bass_kernel_spmd(nc, maps, core_ids=list(range(ncores)))
    if ncores < 8:
        res.results.extend([res.results[0]] * (8 - ncores))
    rs = res.results
    yp = np.zeros((16, 256, D), np.float32)
    ys = np.zeros((2, 2048, D), np.float32)
    nk = np.zeros((16, L, 256, 8, 64), np.float32)
    nv = np.zeros((16, L, 256, 8, 64), np.float32)
    ns = np.zeros((16, L, 2, 8, 128, 128), np.float32)
    for c in range(8):
        y = np.asarray(rs[c]["y_out"])
        yp[2 * c] = y[0:256]
        yp[2 * c + 1] = y[256:512]
        if c < 2:
            ys[c] = y[512:2560]
        nk[2 * c:2 * c + 2] = np.asarray(rs[c]["nk_out"]).reshape(2, L, 256, 8, 64)
        nv[2 * c:2 * c + 2] = np.asarray(rs[c]["nv_out"]).reshape(2, L, 256, 8, 64)
        ns[2 * c:2 * c + 2] = np.asarray(rs[c]["ns_out"])
    return yp, ys, nk, nv, ns
```

```python
import numpy as np
from contextlib import ExitStack
import concourse.bass as bass
import concourse.mybir as mybir
from concourse.alu_op_type import AluOpType as ALU
from concourse.bass_utils import run_bass_kernel_spmd

F32 = mybir.dt.float32
BF16 = mybir.dt.bfloat16
AF = mybir.ActivationFunctionType

D = 2048
KC = 16
L = 2
NT = 2560
TT = 512
NTT = 5
SEQS = [(0, 256, False), (256, 256, False), (512, 2048, True)]
DFF = 5632
N_IN = 12320
Q0, K0, V0, Z0, BETA0, A0 = 512, 1536, 2560, 3584, 4608, 4624
NAQ0, NAK0, NAV0, GATE0 = 4640, 5152, 5664, 6176
EPS = 1e-6
BIG = 30000.0
DBG = {}
AW = 45056


class Res:
    __slots__ = ("w", "rc", "rd", "excl")

    def __init__(self):
        self.excl = False
        self.w = None
        self.rc = {}
        self.rd = []


class Prog:
    CE = ("pe", "act", "dve")
    EPOCH = 16000
    KDMA = 20

    def __init__(self, nc):
        self.nc = nc
        self.ops = []
        self.res = {}
        self.last = {}
        self.lastdma = {"sp": [], "pool": []}
        self.bar = set()
        self.bar_id = 0
        self.crossed = {}

    def R(self, *key):
        r = self.res.get(key)
        if r is None:
            r = self.res[key] = Res()
            r.excl = key[0] == "ps"
        return r

    def barrier(self):
        deps = set()
        for e in self.CE:
            if e in self.last:
                deps.add(self.last[e])
        for q in ("sp", "pool"):
            deps.update(self.lastdma[q][-self.KDMA:])
        self.bar = deps
        self.bar_id += 1
        self.res = {}

    def add(self, eng, fn, reads=(), writes=(), dma=False):
        i = len(self.ops)
        deps = set()
        xs = [r for r in reads if r.excl]
        if xs:
            writes = list(writes) + xs
            reads = [r for r in reads if not r.excl]
        for r in reads:
            if r.w is not None:
                deps.add(r.w)
        for r in writes:
            if r.w is not None:
                deps.add(r.w)
            for e2, j in r.rc.items():
                deps.add(j)
            deps.update(r.rd)
        if self.crossed.get(eng) != self.bar_id:
            deps |= self.bar
            self.crossed[eng] = self.bar_id
        if eng == "pe":
            deps = {j for j in deps if self.ops[j][0] != "pe" or self.ops[j][3]}
        for r in reads:
            if dma:
                r.rd.append(i)
            else:
                r.rc[eng] = i
        for r in writes:
            r.w = i
            r.rc = {}
            r.rd = []
        if dma:
            self.lastdma[eng].append(i)
        else:
            self.last[eng] = i
        self.ops.append([eng, fn, deps, dma, False, None, None])

    def emit(self):
        nc = self.nc
        ops = self.ops
        for op in ops:
            for j in op[2]:
                ops[j][4] = True
        cnt = {e: 0 for e in self.CE}
        nd = {"sp": 0, "pool": 0}
        tot = {"sp": [0] * self.KDMA, "pool": [0] * self.KDMA}
        for op in ops:
            e = op[0]
            if op[3]:
                slot = nd[e] % self.KDMA
                nd[e] += 1
                prev = tot[e][slot]
                tot[e][slot] += 16
                op[5] = ("d", e, slot)
                op[6] = (prev, tot[e][slot])
            elif op[4]:
                ep, v = divmod(cnt[e], self.EPOCH)
                cnt[e] += 1
                op[5] = ("c", e, ep)
                op[6] = v + 1
        st = ExitStack()
        sems = {}
        for e in self.CE:
            for ep in range(cnt[e] // self.EPOCH + 1):
                sems[("c", e, ep)] = st.enter_context(nc.semaphore(f"s_{e}_{ep}"))
        for e in ("sp", "pool"):
            for s in range(self.KDMA):
                sems[("d", e, s)] = st.enter_context(nc.semaphore(f"d_{e}_{s}"))
        block = st.enter_context(nc.Block())
        per = {e: [] for e in ("pe", "act", "dve", "sp", "pool")}
        for op in ops:
            per[op[0]].append(op)
        KD = self.KDMA

        def run(e, eng):
            waited = {}
            for op in per[e]:
                waits = {}
                for j in op[2]:
                    oj = ops[j]
                    k = oj[5]
                    v = oj[6][1] if oj[3] else oj[6]
                    if waits.get(k, 0) < v:
                        waits[k] = v
                if op[3] and op[6][0] > 0:
                    k = op[5]
                    if waits.get(k, 0) < op[6][0]:
                        waits[k] = op[6][0]
                for k, v in waits.items():
                    if waited.get(k, 0) < v:
                        eng.wait_ge(sems[k], v)
                        waited[k] = v
                ins = op[1](eng)
                if op[3]:
                    ins.then_inc(sems[op[5]], 16)
                elif op[4]:
                    ins.then_inc(sems[op[5]], 1)
            if e in ("sp", "pool"):
                for s in range(KD):
                    if tot[e][s] > 0:
                        eng.wait_ge(sems[("d", e, s)], tot[e][s])

        @block.tensor
        def _(t):
            run("pe", t)

        @block.scalar
        def _(t):
            run("act", t)

        @block.vector
        def _(t):
            run("dve", t)

        @block.sync
        def _(t):
            run("sp", t)

        @block.gpsimd
        def _(t):
            run("pool", t)

        st.close()


def build(plan=None, dbg=()):
    nc = bass.Bass("TRN2", target_bir_lowering=False)
    P = Prog(nc)
    R = P.R
    st = ExitStack()

    def din(name, shape, dt=F32):
        return nc.dram_tensor(name, list(shape), dt, kind="ExternalInput").ap()

    def dout(name, shape, dt=F32):
        return nc.dram_tensor(name, list(shape), dt, kind="ExternalOutput").ap()

    def dscr(name, shape, dt=F32):
        kind = "ExternalOutput" if name in dbg else "Internal"
        return nc.dram_tensor(name, list(shape), dt, kind=kind).ap()

    def sb(name, shape, dt=F32):
        return st.enter_context(nc.sbuf_tensor(name, list(shape), dt))

    xin = din("xin", [NT, D])
    condT = din("condT", [128, KC, 2])
    ck = din("ck", [L, 8, 64, 512])
    cvv = din("cvv", [L, 512, 512])
    s0 = din("s0", [L, 2, 8, 128, 128])
    w_ada = din("w_ada", [L, D, 6 * D])
    b_adaT = din("b_adaT", [L, 128, 96])
    gn1T = din("gn1T", [L, 128, KC])
    gn2T = din("gn2T", [L, 128, KC])
    gfT = din("gfT", [128, KC])
    w_in = din("w_in", [L, D, N_IN])
    pool_w = din("pool_w", [L, 4, 128, 128])
    pool_scT = din("pool_scT", [L, 128, 4])
    gconvT = din("gconvT", [L, 128, 24, 3])
    alog = din("alog", [L, 16, 1])
    dtb = din("dtb", [L, 16, 1])
    gng = din("gng", [L, 128, 1])
    rpbp = din("rpbp", [L, 8, 15, 128])
    wb = din("wb", [L, D, D])
    w_out = din("w_out", [L, D, D])
    w_up = din("w_up", [L, D, 2 * DFF])
    fconvT = din("fconvT", [L, 128, 88, 3])
    w_down = din("w_down", [L, DFF, D])
    c_invcnt = din("c_invcnt", [4, 128, 2304])
    c_misc = din("c_misc", [128, 640])
    c_reset = din("c_reset", [16, 2048])

    y_out = dout("y_out", [NT, D])
    nk_out = dout("nk_out", [2, L, 256, 512])
    nv_out = dout("nv_out", [2, L, 256, 512])
    ns_out = dout("ns_out", [2, L, 2, 8, 128, 128])

    XT = dscr("XT", [D, NT])
    U = dscr("U", [N_IN, NT])
    YT = dscr("YT", [D, NT], BF16)
    MT = dscr("MT", [D, NT], BF16)
    QKVn = dscr("QKVn", [3072, NT])
    KVtok = dscr("KVtok", [NT, 2048])
    GS = dscr("GS", [2, 16, NT])
    OFB = dscr("OFB", [2, 1024, NT])
    XTv = XT.rearrange("(kc p) n -> p kc n", p=128)

    arena = sb("arena", [128, AW], F32)
    ident = sb("ident", [128, 128], F32)
    ones_m = sb("ones_m", [128, 128], F32)
    ones1 = sb("ones1", [128, 128], F32)
    onesb = sb("onesb", [128, 128], BF16)
    ones128 = sb("ones128", [128, 128], F32)
    masks = sb("masks", [64, 4, 64], F32)
    colmask = sb("colmask", [64, 64], F32)
    isb = sb("isb", [16, 1], F32)
    Jm = sb("Jm", [64, 64], F32)
    resetm = sb("resetm", [16, 2048], F32)
    silucT = sb("silucT", [128, KC, 2], BF16)
    condsb = sb("condsb", [128, KC, 2], F32)
    modsb = sb("modsb", [128, 96, 2], F32)
    badaT = sb("badaT", [128, 96], F32)
    gnT = sb("gnT", [128, 3, KC], F32)
    a_sc = sb("a_sc", [128, 2, KC, 2], F32)
    prm = sb("prm", [128, 512], F32)
    tokS = sb("tokS", [64, 40, 64], F32)
    zcol = sb("zcol", [128, 1], F32)
    ocol = sb("ocol", [128, 1], F32)
    epscol = sb("epscol", [128, 1], F32)
    psum = [st.enter_context(nc.psum_tensor(f"ps{i}", [128, 512], F32)) for i in range(8)]
    Rps = lambda i: R("ps", i)

    cnt = {"n": 0, "off": 0, "rr": 0}

    def uid():
        cnt["n"] += 1
        return cnt["n"]

    def phase():
        P.barrier()
        cnt["off"] = 0

    def carve(ncols_f32, shape=None, dt=F32, parts=128):
        o = cnt["off"]
        assert o + ncols_f32 <= AW, ("arena overflow", o, ncols_f32)
        cnt["off"] = o + ncols_f32
        a = arena[0:parts, o:o + ncols_f32]
        if dt == BF16:
            a = a.bitcast(BF16)
        if shape is not None and len(shape) == 3:
            a = a.rearrange("p (a b) -> p a b", a=shape[1])
        elif shape is not None and len(shape) == 4:
            a = a.rearrange("p (a b c) -> p a b c", a=shape[1], b=shape[2])
        return a

    def dma(out, in_, reads, writes, q="sp", slow=False):
        if slow:
            P.add(q, lambda e, o=out, i=in_: e.dma_start(out=o, in_=i, allow_slow_non_contiguous=True), reads, writes, dma=True)
        else:
            P.add(q, lambda e, o=out, i=in_: e.dma_start(out=o, in_=i), reads, writes, dma=True)

    def mm(out, lhsT, rhs, start, stop, reads, writes):
        P.add("pe", lambda e, o=out, a=lhsT, b=rhs, s0_=start, s1_=stop: e.matmul(o, a, b, start=s0_, stop=s1_), reads, writes)

    def tr(out, in_, idn, reads, writes):
        P.add("pe", lambda e, o=out, a=in_, b=idn: e.transpose(o, a, b), reads, writes)

    def act(out, in_, func, reads, writes, bias=None, scale=None):
        def f(e, o=out, i=in_, fu=func, b=bias, s=scale):
            kw = {}
            if b is not None:
                kw["bias"] = b
            if s is not None:
                kw["scale"] = s
            return e.activation(out=o, in_=i, func=fu, **kw)
        P.add("act", f, reads, writes)

    def ts(out, in0, s1, s2, op0, op1, reads, writes):
        if op1 is None:
            P.add("dve", lambda e, o=out, a=in0, x=s1, p0=op0: e.tensor_scalar(o, a, x, None, p0), reads, writes)
        else:
            P.add("dve", lambda e, o=out, a=in0, x=s1, y=s2, p0=op0, p1=op1: e.tensor_scalar(o, a, x, y, p0, p1), reads, writes)

    def tt(out, in0, in1, op, reads, writes):
        P.add("dve", lambda e, o=out, a=in0, b=in1, p=op: e.tensor_tensor(o, a, b, p), reads, writes)

    def stt(out, in0, s, in1, op0, op1, reads, writes):
        P.add("dve", lambda e, o=out, a=in0, x=s, b=in1, p0=op0, p1=op1: e.scalar_tensor_tensor(o, a, x, b, p0, p1), reads, writes)

    def cp(out, in_, reads, writes, eng="dve"):
        if eng == "dve":
            P.add("dve", lambda e, o=out, i=in_: e.tensor_copy(o, i), reads, writes)
        else:
            act(out, in_, AF.Copy, reads, writes)

    def memset(ap, val, writes):
        P.add("dve", lambda e, a=ap, v=val: e.memset(a, v), (), writes)

    def rsqrt(out, in_, reads, writes):
        act(out, in_, AF.Sqrt, reads, writes, bias=epscol[0:in_.shape[0], :])
        P.add("dve", lambda e, o=out: e.reciprocal(o, o), writes, writes)

    def evac_eng():
        cnt["rr"] += 1
        return "dve" if cnt["rr"] % 2 else "act"

    def phase_consts():
        Rc = R("c")
        dma(ident[:], c_misc[:, 0:128], (), [Rc])
        dma(masks[:], c_misc[0:64, 128:384].rearrange("p (a b) -> p a b", a=4), (), [Rc])
        dma(colmask[:], c_misc[0:64, 384:448], (), [Rc])
        dma(isb[:], c_misc[0:16, 448:449], (), [Rc], slow=True)
        dma(Jm[:], c_misc[0:64, 512:576], (), [Rc])
        dma(resetm[:], c_reset[:, :], (), [Rc])
        memset(ones_m[:], 1.0 / D, [Rc])
        memset(ones1[:], 1.0, [Rc])
        memset(ones128[:], 1.0 / 128, [Rc])
        memset(onesb[:], 1.0, [Rc])
        memset(zcol[:], 0.0, [Rc])
        memset(ocol[:], 1.0, [Rc])
        memset(epscol[:], EPS, [Rc])
        for c0 in range(0, AW, 4096):
            memset(arena[:, c0:min(AW, c0 + 4096)], 0.0, [Rc])
        memset(tokS[:], 0.0, [Rc])
        dma(condsb[:], condT[:, :, :], (), [R("condsb")])
        act(silucT[:], condsb[:], AF.Silu, [R("condsb")], [R("silucT")])
        dma(gnT[:, 2, :], gfT[:, :], (), [R("gnT")])

    def phase_in_transpose():
        phase()
        wk = [carve(2048) for _ in range(2)]
        xt = carve(8192, [128, KC, 512])
        for t in range(NTT):
            for b in range(4):
                tok0 = t * TT + b * 128
                w_, Rw_ = wk[b % 2], R("wk", b % 2)
                dma(w_, xin[tok0:tok0 + 128, :], (), [Rw_])
                for g in range(4):
                    pb = (b * 4 + g) % 8
                    for j in range(4):
                        kc = g * 4 + j
                        tr(psum[pb][:, j * 128:(j + 1) * 128], w_[:, kc * 128:(kc + 1) * 128], ident[:], [Rw_], [Rps(pb)])
                    o = xt[:, g * 4:(g + 1) * 4, b * 128:(b + 1) * 128]
                    i = psum[pb][:, :].rearrange("p (j n) -> p j n", j=4)
                    cp(o, i, [Rps(pb)], [R("xt")], evac_eng())
            dma(XTv[:, :, t * TT:(t + 1) * TT], xt, [R("xt")], [])

    def phase_mod(l):
        phase()
        wbuf = [carve(4096, [128, KC, 512], BF16) for _ in range(2)]
        dma(badaT[:], b_adaT[l], (), [R("badaT")])
        dma(gnT[:, 0, :], gn1T[l], (), [R("gnT")])
        dma(gnT[:, 1, :], gn2T[l], (), [R("gnT")])
        wv = w_ada[l].rearrange("(kc p) n -> p kc n", p=128)
        for g in range(24):
            s = g % 2
            dma(wbuf[s], wv[:, :, g * 512:(g + 1) * 512], (), [R("wbuf", s)], q="pool")
            for j in range(4):
                ch = g * 4 + j
                pb = ch % 8
                for kc in range(KC):
                    mm(psum[pb][:, 0:2], wbuf[s][:, kc, j * 128:(j + 1) * 128], silucT[:, kc, :], kc == 0, kc == KC - 1,
                       [R("wbuf", s)], [Rps(pb)])
                act(modsb[:, ch, :], psum[pb][:, 0:2], AF.Identity, [Rps(pb), R("badaT")], [R("modsb")], bias=badaT[:, ch:ch + 1])
        for ni, (sci, gi) in enumerate(((1, 0), (4, 1))):
            for c in range(2):
                ts(a_sc[:, ni, :, c], modsb[:, sci * 16:(sci + 1) * 16, c], 1.0, None, ALU.add, None, [R("modsb")], [R("a_sc")])
                tt(a_sc[:, ni, :, c], a_sc[:, ni, :, c], gnT[:, gi, :], ALU.mult, [R("a_sc"), R("gnT")], [R("a_sc")])

    def phase_norm(ni, final=False):
        phase()
        H = carve(20480, [128, KC, NT], BF16)
        xt = carve(8192, [128, KC, 512])
        sq = [carve(512) for _ in range(2)]
        tmp = [carve(512) for _ in range(3)]
        rstd = carve(512)
        wk = [carve(2048) for _ in range(2)]
        for t in range(NTT):
            c = 0 if t == 0 else 1
            dma(xt, XTv[:, :, t * TT:(t + 1) * TT], (), [R("xt")])
            pb = t % 2
            for kc in range(KC):
                s = kc % 2
                act(sq[s], xt[:, kc, :], AF.Square, [R("xt")], [R("sq", s)])
                mm(psum[pb][:], ones_m[:], sq[s], kc == 0, kc == KC - 1, [R("sq", s)], [Rps(pb)])
            rsqrt(rstd, psum[pb][:], [Rps(pb)], [R("rstd")])
            if not final:
                for kc in range(KC):
                    s = kc % 3
                    stt(tmp[s], xt[:, kc, :], a_sc[:, ni, kc, c:c + 1], rstd, ALU.mult, ALU.mult,
                        [R("xt"), R("rstd")], [R("tmp", s)])
                    act(H[:, kc, t * TT:(t + 1) * TT], tmp[s], AF.Identity, [R("tmp", s)], [R("H", kc)],
                        bias=modsb[:, ni * 48 + kc, c:c + 1])
            else:
                for kc in range(KC):
                    stt(xt[:, kc, :], xt[:, kc, :], gnT[:, 2, kc:kc + 1], rstd, ALU.mult, ALU.mult,
                        [R("xt"), R("rstd")], [R("xt")])
                for b in range(4):
                    w_, Rw_ = wk[b % 2], R("wk", b % 2)
                    for g in range(4):
                        pb2 = 2 + (b * 4 + g) % 6
                        for j in range(4):
                            kc = g * 4 + j
                            tr(psum[pb2][:, j * 128:(j + 1) * 128], xt[:, kc, b * 128:(b + 1) * 128], ident[:], [R("xt")], [Rps(pb2)])
                        cp(w_[:, g * 512:(g + 1) * 512], psum[pb2][:], [Rps(pb2)], [Rw_], evac_eng())
                    tok0 = t * TT + b * 128
                    dma(y_out[tok0:tok0 + 128, :], w_, [Rw_], [], q="pool")
        return H

    def phase_proj(wmat, col0, ncols, dst, dst_row0, func_of_col=None, groups=None):
        P.barrier()
        cnt["off"] = 20480
        H = arena[:, 0:20480].bitcast(BF16).rearrange("p (a b) -> p a b", a=KC)
        wbuf = [carve(4096, [128, KC, 512], BF16) for _ in range(2)]
        stg = [carve(2560) for _ in range(2)]
        wv = wmat.rearrange("(kc p) n -> p kc n", p=128)
        if groups is None:
            groups = []
            g0 = col0
            while g0 < col0 + ncols:
                gw = min(512, col0 + ncols - g0)
                groups.append((g0, gw))
                g0 += gw
        for gi, (g0, gw) in enumerate(groups):
            s = gi % 2
            dma(wbuf[s][:, :, 0:gw], wv[:, :, g0:g0 + gw], (), [R("wbuf", s)], q="pool")
            for j in range(0, gw, 128):
                cw = min(128, gw - j)
                col = g0 + j
                wi = uid() % 2
                fu = func_of_col(col) if func_of_col else None
                for t in range(NTT):
                    pb = uid() % 8
                    for kc in range(KC):
                        mm(psum[pb][0:cw, :], wbuf[s][:, kc, j:j + cw], H[:, kc, t * TT:(t + 1) * TT], kc == 0, kc == KC - 1,
                           [R("wbuf", s), R("H", kc)], [Rps(pb)])
                    o = stg[wi][0:cw, t * TT:(t + 1) * TT]
                    if fu is not None:
                        act(o, psum[pb][0:cw, :], fu, [Rps(pb)], [R("stg", wi)])
                    else:
                        cp(o, psum[pb][0:cw, :], [Rps(pb)], [R("stg", wi)], evac_eng())
                r0 = dst_row0 + (col - col0)
                dma(dst[r0:r0 + cw, :], stg[wi][0:cw, :], [R("stg", wi)], [])

    def conv3(out, in_, w3, segs, reads, writes):
        ts(out, in_, w3[:, 1:2], None, ALU.mult, None, reads, writes)
        for (a, b) in segs:
            stt(out[:, a + 1:b], in_[:, a:b - 1], w3[:, 0:1], out[:, a + 1:b], ALU.mult, ALU.add, list(reads) + list(writes), writes)
            stt(out[:, a:b - 1], in_[:, a + 1:b], w3[:, 2:3], out[:, a:b - 1], ALU.mult, ALU.add, list(reads) + list(writes), writes)

    def phase_pool(l):
        phase()
        PAD = 16
        Wd = NT + 6 * PAD
        ub = carve(Wd)
        la = carve(Wd)
        lb = carve(Wd)
        ic = carve(2304)
        dT = [carve(1280, None, BF16) for _ in range(2)]
        ys = [carve(1280, None, BF16) for _ in range(2)]
        pw = carve(256, [128, 4, 128], BF16)
        dma(prm[:, 0:4], pool_scT[l], (), [R("prm")])
        dma(pw, pool_w[l].rearrange("g c d -> c g d"), (), [R("pw")], q="pool")
        memset(ub, 0.0, [R("ub")])
        memset(la, 0.0, [R("la")])
        memset(lb, 0.0, [R("lb")])
        offs = [a + PAD * (2 * si + 1) for si, (a, T, _) in enumerate(SEQS)]
        for g in range(4):
            for si, (a, T, _) in enumerate(SEQS):
                dma(ub[:, offs[si]:offs[si] + T], U[g * 128:(g + 1) * 128, a:a + T], (), [R("ub")])
            dma(ic, c_invcnt[g], (), [R("ic")])
            tt(la[:, 1:Wd], ub[:, 0:Wd - 1], ub[:, 1:Wd], ALU.add, [R("ub")], [R("la")])
            cur, Rcur, oth, Roth = la, R("la"), lb, R("lb")
            sh = 1
            for lev in range(g):
                tt(oth[:, sh:Wd - sh], cur[:, 0:Wd - 2 * sh], cur[:, 2 * sh:Wd], ALU.add, [Rcur], [Roth])
                cur, Rcur, oth, Roth = oth, Roth, cur, Rcur
                sh *= 2
            d_ = dT[g % 2]
            for si, (a, T, _) in enumerate(SEQS):
                o = offs[si]
                ioff = 0 if T == 256 else 256
                tt(cur[:, o:o + T], cur[:, o:o + T], ic[:, ioff:ioff + T], ALU.mult, [Rcur, R("ic")], [Rcur])
                tt(d_[:, a:a + T], cur[:, o:o + T], ub[:, o:o + T], ALU.subtract, [Rcur, R("ub")], [R("dT", g % 2)])
            y_ = ys[g % 2]
            for t in range(NTT):
                pb = uid() % 8
                mm(psum[pb][:], pw[:, g, :], d_[:, t * TT:(t + 1) * TT], True, True, [R("pw"), R("dT", g % 2)], [Rps(pb)])
                ts(y_[:, t * TT:(t + 1) * TT], psum[pb][:], prm[:, g:g + 1], None, ALU.mult, None, [Rps(pb), R("prm")], [R("ys", g % 2)])
            dma(YT[g * 128:(g + 1) * 128, :], y_, [R("ys", g % 2)], [], q="pool")

    def phase_gdn_pre(l):
        phase()
        raw = [carve(2048) for _ in range(2)]
        cvb = [carve(2048) for _ in range(2)]
        sqb = [carve(512) for _ in range(2)]
        rn = [carve(512) for _ in range(2)]
        stg = [carve(512, [128, 4, 128]) for _ in range(2)]
        Bt = carve(2048, parts=16)
        At = carve(2048, parts=16)
        Gf = carve(2048, parts=16)
        Gp = carve(2048, parts=16)
        EG = carve(2048, parts=16)
        BE = carve(2048, parts=16)
        ED = carve(2048, parts=16)
        td = carve(2048, parts=16)
        cw_ = carve(72, [128, 24, 3])
        dma(cw_, gconvT[l], (), [R("cw")])
        dma(prm[0:16, 8:9], alog[l], (), [R("prm")])
        dma(prm[0:16, 9:10], dtb[l], (), [R("prm")])
        act(prm[0:16, 10:11], prm[0:16, 8:9], AF.Exp, [R("prm")], [R("prm2")])
        ts(prm[0:16, 10:11], prm[0:16, 10:11], -1.0, None, ALU.mult, None, [R("prm2")], [R("prm2")])
        it = 0
        for (a, T, _) in SEQS:
            for which, r00 in enumerate((Q0, K0, V0)):
                for h in range(8):
                    b_ = it % 2
                    it += 1
                    x_, Rx = raw[b_][:, 0:T], R("raw", b_)
                    c_, Rcv = cvb[b_][:, 0:T], R("cvb", b_)
                    r0 = r00 + h * 128
                    dma(x_, U[r0:r0 + 128, a:a + T], (), [Rx])
                    conv3(c_, x_, cw_[:, which * 8 + h, :], [(0, T)], [Rx, R("cw")], [Rcv])
                    act(c_, c_, AF.Silu, [Rcv], [Rcv])
                    if which < 2:
                        for sl in range(0, T, 512):
                            w_ = min(512, T - sl)
                            sb_ = uid() % 2
                            pb = uid() % 8
                            act(sqb[sb_][:, 0:w_], c_[:, sl:sl + w_], AF.Square, [Rcv], [R("sqb", sb_)])
                            mm(psum[pb][:, 0:w_], ones1[:], sqb[sb_][:, 0:w_], True, True, [R("sqb", sb_)], [Rps(pb)])
                            rsqrt(rn[sb_][:, 0:w_], psum[pb][:, 0:w_], [Rps(pb)], [R("rn", sb_)])
                            scl = (128.0 ** -0.5) if which == 0 else 1.0
                            stt(c_[:, sl:sl + w_], c_[:, sl:sl + w_], scl, rn[sb_][:, 0:w_], ALU.mult, ALU.mult, [Rcv, R("rn", sb_)], [Rcv])
                    dma(QKVn[which * 1024 + h * 128: which * 1024 + (h + 1) * 128, a:a + T], c_, [Rcv], [], q="pool")
                    if which >= 1:
                        for g in range(T // 512 if T >= 512 else 1):
                            nb = min(4, T // 128)
                            pb = uid() % 8
                            si_ = uid() % 2
                            for j in range(nb):
                                blk = g * 4 + j
                                tr(psum[pb][:, j * 128:(j + 1) * 128], c_[:, blk * 128:(blk + 1) * 128], ident[:], [Rcv], [Rps(pb)])
                            cp(stg[si_][:, 0:nb, :], psum[pb][:, 0:nb * 128].rearrange("p (j n) -> p j n", j=nb), [Rps(pb)], [R("stg", si_)], evac_eng())
                            t0 = a + g * 512
                            cc = (which - 1) * 1024 + h * 128
                            dma(KVtok[t0:t0 + nb * 128, cc:cc + 128].rearrange("(j p) c -> p j c", p=128), stg[si_][:, 0:nb, :], [R("stg", si_)], [], q="pool")
            Rg = R("gate")
            dma(Bt[:, 0:T], U[BETA0:BETA0 + 16, a:a + T], (), [Rg])
            dma(At[:, 0:T], U[A0:A0 + 16, a:a + T], (), [Rg])
            act(Bt[:, 0:T], Bt[:, 0:T], AF.Sigmoid, [Rg], [Rg])
            act(At[:, 0:T], At[:, 0:T], AF.Exp, [Rg, R("prm")], [Rg], bias=prm[0:16, 9:10])
            act(At[:, 0:T], At[:, 0:T], AF.Ln, [Rg], [Rg], bias=ocol[0:16, :])
            ts(At[:, 0:T], At[:, 0:T], prm[0:16, 10:11], None, ALU.mult, None, [Rg, R("prm2")], [Rg])
            P.add("dve", lambda e, o=Gf[:, 0:T], d0=resetm[:, 0:T], d1=At[:, 0:T]: e.tensor_tensor_scan(o, d0, d1, 0.0, ALU.mult, ALU.add), [Rg], [Rg])
            nch = T // 64
            tot = Gf[:, 63:T:64].unsqueeze(2).to_broadcast([16, nch, 64])
            v3 = lambda ap: ap[:, 0:T].rearrange("p (c j) -> p c j", j=64)
            stt(v3(td), v3(Gf), -2.0, tot, ALU.mult, ALU.add, [Rg], [Rg])
            tt(td[:, 0:T], td[:, 0:T], At[:, 0:T], ALU.add, [Rg], [Rg])
            stt(Gp[:, 0:T], td[:, 0:T], isb[:, 0:1], Gf[:, 0:T], ALU.mult, ALU.add, [Rg], [Rg])
            act(EG[:, 0:T], Gp[:, 0:T], AF.Exp, [Rg], [Rg])
            tt(BE[:, 0:T], Bt[:, 0:T], EG[:, 0:T], ALU.mult, [Rg], [Rg])
            tt(v3(td), tot, v3(Gp), ALU.subtract, [Rg], [Rg])
            act(ED[:, 0:T], td[:, 0:T], AF.Exp, [Rg], [Rg])
            dma(GS[0, :, a:a + T], Gp[:, 0:T], [Rg], [])
            dma(GS[1, :, a:a + T], EG[:, 0:T], [Rg], [])
            for c in range(nch):
                pb = uid() % 8
                for k_, src in enumerate((Gp, Bt, BE, ED)):
                    tr(psum[pb][0:64, k_ * 16:(k_ + 1) * 16], src[:, c * 64:(c + 1) * 64], ident[0:16, 0:16], [Rg], [Rps(pb)])
                cp(tokS[:, a // 64 + c, :], psum[pb][0:64, 0:64], [Rps(pb)], [R("tokS")], evac_eng())

    def phase_gdn_main(l):
        phase()
        S = carve(1024, [128, 8, 128])
        qT = [carve(512, [128, 8, 64]) for _ in range(2)]
        kT = [carve(512, [128, 8, 64]) for _ in range(2)]
        ktok = [carve(1024, [64, 8, 128], parts=64) for _ in range(2)]
        vtok = [carve(1024, [64, 8, 128], parts=64) for _ in range(2)]
        GB = [carve(512, [64, 8, 64], parts=64) for _ in range(2)]
        EGB = [carve(512, [128, 8, 64]) for _ in range(2)]
        c8 = lambda: carve(512, [64, 8, 64], parts=64)
        XS, XTm, DmS, DmT, Lm, Mm, QKm = c8(), c8(), c8(), c8(), c8(), c8(), c8()
        Rtb = [c8(), c8()]
        Lp = [c8(), c8()]
        Mp = [c8(), c8()]
        vtb = carve(1024, [64, 8, 128], parts=64)
        ktb = carve(1024, [64, 8, 128], parts=64)
        kd = carve(1024, [64, 8, 128], parts=64)
        vnew = carve(1024, [64, 8, 128], parts=64)
        wT = carve(512, [128, 8, 64])
        qg = carve(512, [128, 8, 64])
        ost = [carve(512, [128, 8, 64]) for _ in range(2)]
        I64 = ident[0:64, 0:64]
        step = 0
        for si, (a, T, is_s) in enumerate(SEQS):
            nch = T // 64
            for d in range(2):
                RS = [R("S", h) for h in range(8)]
                if is_s:
                    dma(S, s0[l, d].rearrange("h k v -> k h v"), (), RS)
                else:
                    memset(S, 0.0, RS)
                mS = masks[:, 2 * d, :]
                mT = masks[:, 2 * d + 1, :]
                for s_ in range(min(nch, DBG.get("gdn_steps", 10 ** 9))):
                    c = s_ if d == 0 else nch - 1 - s_
                    b_ = step % 2
                    step += 1
                    t0 = a + c * 64
                    cg = t0 // 64
                    Rl = R("ld", b_)
                    dma(qT[b_], QKVn[0:1024, t0:t0 + 64].rearrange("(h p) t -> p h t", p=128), (), [Rl])
                    dma(kT[b_], QKVn[1024:2048, t0:t0 + 64].rearrange("(h p) t -> p h t", p=128), (), [Rl])
                    dma(ktok[b_], KVtok[t0:t0 + 64, 0:1024].rearrange("p (h d) -> p h d", h=8), (), [Rl])
                    dma(vtok[b_], KVtok[t0:t0 + 64, 1024:2048].rearrange("p (h d) -> p h d", h=8), (), [Rl])
                    gsrc = GS[0, d * 8:(d + 1) * 8, t0:t0 + 64]
                    dma(GB[b_], bass.AP(gsrc.tensor, gsrc.offset, [[0, 64], [NT, 8], [1, 64]]), (), [Rl])
                    esrc = GS[1, d * 8:(d + 1) * 8, t0:t0 + 64]
                    dma(EGB[b_], bass.AP(esrc.tensor, esrc.offset, [[0, 128], [NT, 8], [1, 64]]), (), [Rl])
                    col = lambda kind, h: tokS[:, cg, kind * 16 + d * 8 + h: kind * 16 + d * 8 + h + 1]
                    HR = lambda n, h: R(n, h)
                    for h in range(8):
                        mm(psum[h][0:64, 0:64], kT[b_][:, h, :], kT[b_][:, h, :], True, True, [Rl], [R("ps", h)])
                        mm(psum[h][0:64, 64:128], kT[b_][:, h, :], qT[b_][:, h, :], True, True, [Rl], [R("ps", h)])
                    if DBG.get("gdn_stage", "z") == "0":
                        continue
                    for h in range(8):
                        stt(XS[:, h, :], GB[b_][:, h, :], col(0, h), mS, ALU.subtract, ALU.add, [Rl, R("tokS")], [HR("XS", h)])
                        act(DmS[:, h, :], XS[:, h, :], AF.Exp, [HR("XS", h)], [HR("DmS", h)], scale=-1.0)
                        stt(XTm[:, h, :], GB[b_][:, h, :], col(0, h), mT, ALU.subtract, ALU.add, [Rl, R("tokS")], [HR("XT", h)])
                        act(DmT[:, h, :], XTm[:, h, :], AF.Exp, [HR("XT", h)], [HR("DmT", h)])
                    if DBG.get("gdn_stage", "z") == "a":
                        continue
                    for h in range(8):
                        stt(Lm[:, h, :], psum[h][0:64, 0:64], col(1, h), DmS[:, h, :], ALU.mult, ALU.mult, [R("ps", h), HR("DmS", h), R("tokS")], [HR("L", h)])
                        tt(QKm[:, h, :], psum[h][0:64, 64:128], DmT[:, h, :], ALU.mult, [R("ps", h), HR("DmT", h)], [HR("QKm", h)])
                    if DBG.get("c_sub", 9) < 2:
                        continue
                    for h in range(8):
                        tr(psum[h][0:64, 128:192], Lm[:, h, :], I64, [HR("L", h)], [R("ps", h)])
                    if DBG.get("c_sub", 9) < 3:
                        continue
                    for h in range(8):
                        cp(Mm[:, h, :], psum[h][0:64, 128:192], [R("ps", h)], [HR("M", h)], "act")
                        stt(Rtb[0][:, h, :], psum[h][0:64, 128:192], -1.0, I64, ALU.mult, ALU.add, [R("ps", h)], [HR("Rt0", h)])
                    if DBG.get("gdn_stage", "z") == "b":
                        continue
                    for h in range(8):
                        mm(psum[h][0:64, 256:320], Mm[:, h, :], Lm[:, h, :], True, True, [HR("M", h), HR("L", h)], [R("ps", h)])
                        mm(psum[h][0:64, 128:192], Lm[:, h, :], Mm[:, h, :], True, True, [HR("M", h), HR("L", h)], [R("ps", h)])
                    if DBG.get("d_sub", 9) < 1:
                        continue
                    for h in range(8):
                        cp(Lp[0][:, h, :], psum[h][0:64, 256:320], [R("ps", h)], [HR("Lp0", h)], "act")
                        cp(Mp[0][:, h, :], psum[h][0:64, 128:192], [R("ps", h)], [HR("Mp0", h)], "dve")
                    Rt, RtN = Rtb[0], "Rt0"
                    for lev in range(min(5, DBG.get("d_lev", 5))):
                        i_ = lev % 2
                        Rt, RtN = Rtb[lev % 2], f"Rt{lev % 2}"
                        Rt2, Rt2N = Rtb[1 - lev % 2], f"Rt{1 - lev % 2}"
                        o_ = 1 - i_
                        last = lev == 4
                        for h in range(8):
                            if not last:
                                mm(psum[h][0:64, 128:192], Lp[i_][:, h, :], Mp[i_][:, h, :], True, True, [HR(f"Lp{i_}", h), HR(f"Mp{i_}", h)], [R("ps", h)])
                            mm(psum[h][0:64, 192:256], Lp[i_][:, h, :], Rt[:, h, :], True, True, [HR(f"Lp{i_}", h), HR(RtN, h)], [R("ps", h)])
                            if not last:
                                mm(psum[h][0:64, 256:320], Mp[i_][:, h, :], Lp[i_][:, h, :], True, True, [HR(f"Lp{i_}", h), HR(f"Mp{i_}", h)], [R("ps", h)])
                        for h in range(8):
                            if DBG.get("lev_sub", 9) >= 1:
                                tt(Rt2[:, h, :], psum[h][0:64, 192:256], Rt[:, h, :], ALU.add, [R("ps", h), HR(RtN, h)], [HR(Rt2N, h)])
                            if not last and DBG.get("lev_sub", 9) >= 2:
                                cp(Mp[o_][:, h, :], psum[h][0:64, 128:192], [R("ps", h)], [HR(f"Mp{o_}", h)], "act")
                                cp(Lp[o_][:, h, :], psum[h][0:64, 256:320], [R("ps", h)], [HR(f"Lp{o_}", h)], "act" if h % 2 else "dve")
                    if DBG.get("gdn_stage", "z") == "d":
                        continue
                    nlev_ = min(5, DBG.get("d_lev", 5))
                    Rt, RtN = Rtb[nlev_ % 2], f"Rt{nlev_ % 2}"
                    for h in range(8):
                        ts(vtb[:, h, :], vtok[b_][:, h, :], col(1, h), None, ALU.mult, None, [Rl, R("tokS")], [HR("vtb", h)])
                        ts(ktb[:, h, :], ktok[b_][:, h, :], col(2, h), None, ALU.mult, None, [Rl, R("tokS")], [HR("ktb", h)])
                        ts(kd[:, h, :], ktok[b_][:, h, :], col(3, h), None, ALU.mult, None, [Rl, R("tokS")], [HR("kd", h)])
                        tt(qg[:, h, :], qT[b_][:, h, :], EGB[b_][:, h, :], ALU.mult, [Rl], [HR("qg", h)])
                    for h in range(8):
                        mm(psum[h][:, 448:512], ktb[:, h, :], Rt[:, h, :], True, True, [HR("ktb", h), HR(RtN, h)], [R("ps", h)])
                    for h in range(8):
                        act(wT[:, h, :], psum[h][:, 448:512], AF.Copy, [R("ps", h)], [HR("wT", h)], scale=-1.0)
                    if DBG.get("gdn_stage", "z") == "e":
                        continue
                    for h in range(8):
                        mm(psum[h][0:64, 320:448], Rt[:, h, :], vtb[:, h, :], True, False, [HR(RtN, h), HR("vtb", h)], [R("ps", h)])
                        mm(psum[h][0:64, 320:448], wT[:, h, :], S[:, h, :], False, True, [HR("wT", h), R("S", h)], [R("ps", h)])
                    for h in range(8):
                        cp(vnew[:, h, :], psum[h][0:64, 320:448], [R("ps", h)], [HR("vnew", h)], "act" if h % 2 else "dve")
                    if DBG.get("gdn_stage", "z") == "f":
                        continue
                    ob_ = ost[b_]
                    for h in range(8):
                        mm(psum[h][:, 0:64], S[:, h, :], qg[:, h, :], True, False, [R("S", h), HR("qg", h)], [R("ps", h)])
                        mm(psum[h][:, 0:64], vnew[:, h, :], QKm[:, h, :], False, True, [HR("vnew", h), HR("QKm", h)], [R("ps", h)])
                    for h in range(8):
                        cp(ob_[:, h, :], psum[h][:, 0:64], [R("ps", h)], [R("ost", b_)], "act" if h % 2 else "dve")
                    dma(OFB[d, :, t0:t0 + 64].rearrange("(h p) t -> p h t", p=128), ob_, [R("ost", b_)], [], q="pool")
                    if DBG.get("gdn_stage", "z") == "g":
                        continue
                    gcol = 63 if d == 0 else 0
                    for h in range(8):
                        mm(psum[h][:, 128:256], kd[:, h, :], vnew[:, h, :], True, True, [HR("kd", h), HR("vnew", h)], [R("ps", h), R("ps", h)])
                    for h in range(8):
                        ts(S[:, h, :], S[:, h, :], EGB[b_][:, h, gcol:gcol + 1], None, ALU.mult, None, [Rl, R("S", h)], [R("S", h)])
                        tt(S[:, h, :], psum[h][:, 128:256], S[:, h, :], ALU.add, [R("ps", h), R("ps", h), R("S", h)], [R("S", h)])
                if not is_s:
                    dma(ns_out[si, l, d].rearrange("h k v -> k h v"), S, [R("S", h) for h in range(8)], [], q="pool")

    def phase_gdn_post(l):
        phase()
        of_ = [carve(2560) for _ in range(2)]
        ob_ = [carve(2560) for _ in range(2)]
        zz = [carve(2560) for _ in range(2)]
        sq = [carve(512) for _ in range(2)]
        rs = [carve(512) for _ in range(2)]
        yb = [carve(1280, None, BF16) for _ in range(2)]
        dma(prm[:, 16:17], gng[l], (), [R("prm")])
        for h in range(8):
            b_ = h % 2
            dma(of_[b_], OFB[0, h * 128:(h + 1) * 128, :], (), [R("of", b_)])
            dma(ob_[b_], OFB[1, h * 128:(h + 1) * 128, :], (), [R("ob", b_)])
            dma(zz[b_], U[Z0 + h * 128:Z0 + (h + 1) * 128, :], (), [R("zz", b_)])
            tt(of_[b_], of_[b_], ob_[b_], ALU.add, [R("of", b_), R("ob", b_)], [R("of", b_)])
            for t in range(NTT):
                s = uid() % 2
                pb = uid() % 8
                sl = slice(t * TT, (t + 1) * TT)
                act(sq[s], of_[b_][:, sl], AF.Square, [R("of", b_)], [R("sq", s)])
                mm(psum[pb][:], ones128[:], sq[s], True, True, [R("sq", s)], [Rps(pb)])
                rsqrt(rs[s], psum[pb][:], [Rps(pb)], [R("rs", s)])
                stt(of_[b_][:, sl], of_[b_][:, sl], prm[:, 16:17], rs[s], ALU.mult, ALU.mult, [R("of", b_), R("rs", s), R("prm")], [R("of", b_)])
                tt(yb[b_][:, sl], of_[b_][:, sl], zz[b_][:, sl], ALU.mult, [R("of", b_), R("zz", b_)], [R("yb", b_)])
            dma(YT[512 + h * 128:512 + (h + 1) * 128, :], yb[b_], [R("yb", b_)], [], q="pool")

    def phase_na(l):
        phase()
        BM = carve(7680, [64, 120, 64], parts=64)
        vtk = carve(10240, [64, 40, 512], BF16, parts=64)
        kvfb = carve(8192, parts=64)
        kvf = [kvfb[:, i * 4096:(i + 1) * 4096].rearrange("p (a b) -> p a b", a=8) for i in range(2)]
        kctx = carve(2048, [64, 8, 512], BF16, parts=64)
        vctx = carve(1024, [128, 4, 512], BF16)
        fr = [carve(2560) for _ in range(2)]
        qTh = [carve(1280, None, BF16, parts=64) for _ in range(2)]
        kTh = [carve(1280, None, BF16, parts=64) for _ in range(2)]
        sc = [carve(512, parts=64) for _ in range(2)]
        pT = [carve(256, None, BF16, parts=64) for _ in range(2)]
        pc = [carve(128, None, BF16) for _ in range(2)]
        rden = [carve(256, parts=64) for _ in range(2)]
        yst = [carve(1280, None, BF16, parts=64) for _ in range(2)]
        RB = R("BM")
        src = rpbp[l]
        BHv = kvfb
        for h_ in range(8):
            dma(BHv[:, h_ * 960:(h_ + 1) * 960].rearrange("p (a b) -> p a b", a=15),
                bass.AP(src.tensor, src.offset + h_ * 15 * 128, [[1, 64], [128, 15], [1, 64]]), (), [R("kvf", 0), R("kvf", 1)])
        BMv = BM.rearrange("p a b -> p (a b)")
        for j in range(15):
            pb = uid() % 8
            mm(psum[pb][0:64, :], Jm[:, :], BHv[:, j * 512:(j + 1) * 512], True, True, [R("kvf", 0), R("kvf", 1)], [Rps(pb)])
            cmb = colmask[:, :].unsqueeze(1).to_broadcast([64, 8, 64])
            tt(BMv[:, j * 512:(j + 1) * 512].rearrange("p (a b) -> p a b", a=8), psum[pb][0:64, :].rearrange("p (a b) -> p a b", a=8), cmb, ALU.add, [Rps(pb)], [RB])
        dma(kctx, ck[l].rearrange("h d k -> d h k"), (), [R("kctx")], q="pool")
        dma(vctx, cvv[l].rearrange("(b p) c -> p b c", p=128), (), [R("vctx")], q="pool")
        for which, r00 in ((0, NAK0), (1, NAV0)):
            for chn in range(4):
                b_ = uid() % 2
                dma(fr[b_], U[r00 + chn * 128:r00 + (chn + 1) * 128, :], (), [R("fr", b_)])
                nrow = 40 if which == 1 else 8
                for r4 in range(0, nrow, 4):
                    pb = uid() % 8
                    for j in range(4):
                        r = r4 + j
                        tr(psum[pb][0:64, j * 128:(j + 1) * 128], fr[b_][:, r * 64:(r + 1) * 64], ident[:], [R("fr", b_)], [Rps(pb)])
                    src_ = psum[pb][0:64, :].rearrange("p (j n) -> p j n", j=4)
                    if which == 1:
                        cp(vtk[:, r4:r4 + 4, chn * 128:(chn + 1) * 128], src_, [Rps(pb)], [R("vtk")], evac_eng())
                    if r4 < 8:
                        cp(kvf[which][:, r4:r4 + 4, chn * 128:(chn + 1) * 128], src_, [Rps(pb)], [R("kvf", which)], evac_eng())
        for sq_ in range(2):
            dma(nk_out[sq_, l].rearrange("(r p) c -> p r c", p=64), kvf[0][:, sq_ * 4:(sq_ + 1) * 4, :], [R("kvf", 0)], [])
            dma(nv_out[sq_, l].rearrange("(r p) c -> p r c", p=64), kvf[1][:, sq_ * 4:(sq_ + 1) * 4, :], [R("kvf", 1)], [])
        for h in range(8):
            b_ = h % 2
            dma(qTh[b_], U[NAQ0 + h * 64:NAQ0 + (h + 1) * 64, :], (), [R("qTh", b_)], q="pool")
            dma(kTh[b_], U[NAK0 + h * 64:NAK0 + (h + 1) * 64, :], (), [R("kTh", b_)], q="pool")
            hs = slice(h * 64, (h + 1) * 64)
            for sq_ in range(2):
                for qb in range(4):
                    q0 = sq_ * 256 + qb * 64
                    i_ = uid() % 2
                    pS, pO, pD = (uid() % 2) * 4, (uid() % 2) * 4 + 1, (uid() % 2) * 4 + 2
                    for kr in range(4):
                        k0 = sq_ * 256 + kr * 64
                        mm(psum[pS][0:64, kr * 64:(kr + 1) * 64], kTh[b_][:, k0:k0 + 64], qTh[b_][:, q0:q0 + 64], True, True,
                           [R("qTh", b_), R("kTh", b_)], [Rps(pS)])
                    act(pT[i_][:, 0:256], psum[pS][0:64, 0:256], AF.Exp, [Rps(pS)], [R("pT", i_)], scale=0.125)
                    for kr in range(4):
                        mm(psum[pO][0:64, 0:64], vtk[:, sq_ * 4 + kr, hs], pT[i_][:, kr * 64:(kr + 1) * 64], kr == 0, kr == 3,
                           [R("vtk"), R("pT", i_)], [Rps(pO)])
                    for kr in range(4):
                        mm(psum[pD][0:64, 0:64], onesb[0:64, 0:64], pT[i_][:, kr * 64:(kr + 1) * 64], kr == 0, kr == 3,
                           [R("pT", i_)], [Rps(pD)])
                    P.add("dve", lambda e, o=rden[i_][:, 0:64], i=psum[pD][0:64, 0:64]: e.reciprocal(o, i), [Rps(pD)], [R("rden", i_)])
                    tt(yst[b_][:, q0:q0 + 64], psum[pO][0:64, 0:64], rden[i_][:, 0:64], ALU.mult, [Rps(pO), R("rden", i_)], [R("yst", b_)])
            for r in range(32):
                q0 = 512 + r * 64
                kr0 = min(max(r - 4, 0), 24)
                d0 = kr0 - r + 7
                i_ = uid() % 2
                base = (r % 2) * 4
                pS, pC, pO, pD = base, base + 1, base + 2, base + 3
                for i in range(8):
                    k0 = 512 + (kr0 + i) * 64
                    mm(psum[pS][0:64, i * 64:(i + 1) * 64], kTh[b_][:, k0:k0 + 64], qTh[b_][:, q0:q0 + 64], True, True,
                       [R("qTh", b_), R("kTh", b_)], [Rps(pS)])
                for cb in range(4):
                    mm(psum[pC][:, cb * 64:(cb + 1) * 64], kctx[:, h, cb * 128:(cb + 1) * 128], qTh[b_][:, q0:q0 + 64], True, True,
                       [R("qTh", b_), R("kctx")], [Rps(pC)])
                bm = BM[:, h * 15 + d0:h * 15 + d0 + 8, :]
                stt(sc[i_].rearrange("p (a b) -> p a b", a=8), psum[pS][0:64, :].rearrange("p (a b) -> p a b", a=8), 0.125, bm, ALU.mult, ALU.add,
                    [Rps(pS), RB], [R("sc", i_)])
                act(pT[i_], sc[i_], AF.Exp, [R("sc", i_)], [R("pT", i_)])
                act(pc[i_], psum[pC][:, 0:256], AF.Exp, [Rps(pC)], [R("pc", i_)], scale=0.125)
                for i in range(8):
                    mm(psum[pO][0:64, 0:64], vtk[:, 8 + kr0 + i, hs], pT[i_][:, i * 64:(i + 1) * 64], i == 0, False,
                       [R("vtk"), R("pT", i_)], [Rps(pO)])
                for cb in range(4):
                    mm(psum[pO][0:64, 0:64], vctx[:, cb, hs], pc[i_][:, cb * 64:(cb + 1) * 64], False, cb == 3,
                       [R("vctx"), R("pc", i_)], [Rps(pO)])
                for i in range(8):
                    mm(psum[pD][0:64, 0:64], onesb[0:64, 0:64], pT[i_][:, i * 64:(i + 1) * 64], i == 0, False, [R("pT", i_)], [Rps(pD)])
                for cb in range(4):
                    mm(psum[pD][0:64, 0:64], onesb[:, 0:64], pc[i_][:, cb * 64:(cb + 1) * 64], False, cb == 3, [R("pc", i_)], [Rps(pD)])
                P.add("dve", lambda e, o=rden[i_][:, 0:64], i=psum[pD][0:64, 0:64]: e.reciprocal(o, i), [Rps(pD)], [R("rden", i_)])
                tt(yst[b_][:, q0:q0 + 64], psum[pO][0:64, 0:64], rden[i_][:, 0:64], ALU.mult, [Rps(pO), R("rden", i_)], [R("yst", b_)])
            dma(YT[1536 + h * 64:1536 + (h + 1) * 64, :], yst[b_], [R("yst", b_)], [])

    def load_H(src):
        H = carve(20480, [128, KC, NT], BF16)
        for kc in range(KC):
            dma(H[:, kc, :], src[kc * 128:(kc + 1) * 128, :], (), [R("H", kc)])
        return H

    def phase_merge(l):
        phase()
        H = load_H(YT)
        wbuf = [carve(4096, [128, KC, 512], BF16) for _ in range(2)]
        gt = [carve(1536, [128, 3, 512]) for _ in range(2)]
        t0_ = [carve(512) for _ in range(2)]
        t1_ = [carve(512) for _ in range(2)]
        ms = [carve(1280, None, BF16) for _ in range(2)]
        wv = wb[l].rearrange("(kc p) n -> p kc n", p=128)
        KR = ((0, 4), (4, 12), (12, 16))
        for g in range(4):
            s = g % 2
            dma(wbuf[s], wv[:, :, g * 512:(g + 1) * 512], (), [R("wbuf", s)], q="pool")
            for j in range(4):
                oc = g * 4 + j
                mb = oc % 2
                for t in range(NTT):
                    gi = uid() % 2
                    sl = slice(t * TT, (t + 1) * TT)
                    for i in range(3):
                        r0 = GATE0 + i * 2048 + oc * 128
                        dma(gt[gi][:, i, :], U[r0:r0 + 128, sl], (), [R("gt", gi)])
                    pbs = [(uid() % 2) * 4 + i for i in range(3)]
                    for i, (k0_, k1_) in enumerate(KR):
                        for kc in range(k0_, k1_):
                            mm(psum[pbs[i]][:], wbuf[s][:, kc, j * 128:(j + 1) * 128], H[:, kc, sl], kc == k0_, kc == k1_ - 1,
                               [R("wbuf", s), R("H", kc)], [Rps(pbs[i])])
                    tt(t0_[gi], psum[pbs[0]][:], gt[gi][:, 0, :], ALU.mult, [Rps(pbs[0]), R("gt", gi)], [R("t0", gi)])
                    tt(t1_[gi], psum[pbs[1]][:], gt[gi][:, 1, :], ALU.mult, [Rps(pbs[1]), R("gt", gi)], [R("t1", gi)])
                    tt(t0_[gi], t0_[gi], t1_[gi], ALU.add, [R("t0", gi), R("t1", gi)], [R("t0", gi)])
                    tt(t1_[gi], psum[pbs[2]][:], gt[gi][:, 2, :], ALU.mult, [Rps(pbs[2]), R("gt", gi)], [R("t1", gi)])
                    tt(ms[mb][:, sl], t0_[gi], t1_[gi], ALU.add, [R("t0", gi), R("t1", gi)], [R("ms", mb)])
                dma(MT[oc * 128:(oc + 1) * 128, :], ms[mb], [R("ms", mb)], [])

    def phase_wout(l):
        phase()
        H = load_H(MT)
        wbuf = [carve(4096, [128, KC, 512], BF16) for _ in range(2)]
        xs = [carve(2560) for _ in range(2)]
        wv = w_out[l].rearrange("(kc p) n -> p kc n", p=128)
        for g in range(4):
            s = g % 2
            dma(wbuf[s], wv[:, :, g * 512:(g + 1) * 512], (), [R("wbuf", s)], q="pool")
            for j in range(4):
                oc = g * 4 + j
                xb = oc % 2
                dma(xs[xb], XT[oc * 128:(oc + 1) * 128, :], (), [R("xs", xb)])
                for t in range(NTT):
                    c = 0 if t == 0 else 1
                    sl = slice(t * TT, (t + 1) * TT)
                    pb = uid() % 8
                    for kc in range(KC):
                        mm(psum[pb][:], wbuf[s][:, kc, j * 128:(j + 1) * 128], H[:, kc, sl], kc == 0, kc == KC - 1,
                           [R("wbuf", s), R("H", kc)], [Rps(pb)])
                    stt(xs[xb][:, sl], psum[pb][:], modsb[:, 32 + oc, c:c + 1], xs[xb][:, sl], ALU.mult, ALU.add,
                        [Rps(pb), R("xs", xb)], [R("xs", xb)])
                dma(XT[oc * 128:(oc + 1) * 128, :], xs[xb], [R("xs", xb)], [])

    def phase_ffn_down(l):
        phase()
        actb = carve(22528, [128, 44, 1024], BF16)
        ab = [carve(2080, [128, 4, 520]) for _ in range(4)]
        cb_ = [carve(2080, [128, 4, 520]) for _ in range(2)]
        wd = [carve(2816, [128, 44, 128], BF16) for _ in range(2)]
        xs = [carve(512) for _ in range(4)]
        fw = carve(264, [128, 88, 3])
        dma(fw, fconvT[l], (), [R("fw")])
        wv = w_down[l].rearrange("(kc p) n -> p kc n", p=128)
        it = 0
        xi = 0
        for grp in ((0, 1), (2, 3), (4,)):
            for gi, t in enumerate(grp):
                lo = t * TT
                hl = t > 1
                hr = 1 <= t < 4
                segs = [(1, 257), (257, 513)] if t == 0 else [(0, 514)]
                for c4 in range(0, 44, 4):
                    bufs = []
                    for half in range(2):
                        bi = (it % 2) * 2 + half
                        r0 = half * DFF + c4 * 128
                        a_, Ra_ = ab[bi], R("ab", bi)
                        c0 = lo - (1 if hl else 0)
                        c1 = lo + TT + (1 if hr else 0)
                        o0 = 0 if hl else 1
                        if not hl:
                            memset(a_[:, :, 0:1], 0.0, [Ra_])
                        if not hr:
                            memset(a_[:, :, 513:514], 0.0, [Ra_])
                        dma(a_[:, :, o0:o0 + (c1 - c0)], U[r0:r0 + 512, c0:c1].rearrange("(j p) t -> p j t", p=128), (), [Ra_])
                        cv_, Rc_ = cb_[half], R("cb", half)
                        for j in range(4):
                            conv3(cv_[:, j, 0:514], a_[:, j, 0:514], fw[:, half * 44 + c4 + j, :], segs, [Ra_, R("fw")], [Rc_])
                        bufs.append((cv_, Rc_))
                    it += 1
                    (ca, Rca), (cbb, Rcb) = bufs
                    act(ca[:, :, 1:513], ca[:, :, 1:513], AF.Silu, [Rca], [Rca])
                    tt(actb[:, c4:c4 + 4, gi * TT:(gi + 1) * TT], ca[:, :, 1:513], cbb[:, :, 1:513], ALU.mult, [Rca, Rcb],
                       [R("actb", c4 + j) for j in range(4)])
            for oc in range(16):
                s = oc % 2
                dma(wd[s], wv[:, :, oc * 128:(oc + 1) * 128], (), [R("wd", s)], q="pool")
                for gi, t in enumerate(grp):
                    c = 0 if t == 0 else 1
                    lo = t * TT
                    x_ = xi % 4
                    xi += 1
                    dma(xs[x_], XT[oc * 128:(oc + 1) * 128, lo:lo + TT], (), [R("xs", x_)])
                    pb = uid() % 8
                    for kc in range(44):
                        mm(psum[pb][:], wd[s][:, kc, :], actb[:, kc, gi * TT:(gi + 1) * TT], kc == 0, kc == 43, [R("wd", s), R("actb", kc)], [Rps(pb)])
                    stt(xs[x_], psum[pb][:], modsb[:, 80 + oc, c:c + 1], xs[x_], ALU.mult, ALU.add, [Rps(pb), R("xs", x_)], [R("xs", x_)])
                    dma(XT[oc * 128:(oc + 1) * 128, lo:lo + TT], xs[x_], [R("xs", x_)], [])

    def in_func(col):
        if Z0 <= col < BETA0:
            return AF.Silu
        if col >= GATE0:
            return AF.Sigmoid
        return None

    IN_GROUPS = [(g * 512, 512) for g in range(9)] + [(4608, 32)] + [(4640 + g * 512, 512) for g in range(15)]

    if plan is None:
        plan = ["consts", "intr"]
        for l in range(L):
            plan += [("mod", l), ("norm1", l), ("pool", l), ("gdn_pre", l), ("gdn_main", l), ("gdn_post", l), ("na", l),
                     ("merge", l), ("wout", l), ("norm2", l), ("ffn", l)]
        plan += ["final"]
    for p in plan:
        if p == "consts":
            phase_consts()
        elif p == "intr":
            phase_in_transpose()
        elif p == "final":
            phase_norm(0, final=True)
        else:
            nm, l = p
            if nm == "mod":
                phase_mod(l)
            elif nm == "norm1":
                phase_norm(0)
                phase_proj(w_in[l], 0, N_IN, U, 0, in_func, IN_GROUPS)
            elif nm == "pool":
                phase_pool(l)
            elif nm == "gdn_pre":
                phase_gdn_pre(l)
            elif nm == "gdn_main":
                phase_gdn_main(l)
            elif nm == "gdn_post":
                phase_gdn_post(l)
            elif nm == "na":
                phase_na(l)
            elif nm == "merge":
                phase_merge(l)
            elif nm == "wout":
                phase_wout(l)
            elif nm == "norm2":
                phase_norm(1)
                phase_proj(w_up[l], 0, 2 * DFF, U, 0)
            elif nm == "ffn":
                phase_ffn_down(l)
    P.emit()
    st.close()
    return nc


def make_consts():
    misc = np.zeros((128, 640), np.float32)
    misc[0:64, 512:576] = np.eye(64, dtype=np.float32)[::-1]
    misc[:, 0:128] = np.eye(128, dtype=np.float32)
    i = np.arange(64)[:, None]
    j = np.arange(64)[None, :]
    m = np.zeros((64, 4, 64), np.float32)
    m[:, 0, :] = np.where(j >= i, BIG, 0.0)
    m[:, 1, :] = np.where(j < i, -BIG, 0.0)
    m[:, 2, :] = np.where(j <= i, BIG, 0.0)
    m[:, 3, :] = np.where(j > i, -BIG, 0.0)
    misc[0:64, 128:384] = m.reshape(64, 256)
    qc = np.arange(64)
    cs = np.clip(qc - 8, 0, 48)
    kc = np.arange(64)[:, None]
    ok = (kc >= cs[None, :]) & (kc < cs[None, :] + 16)
    misc[0:64, 384:448] = np.where(ok, 0.0, -BIG)
    misc[0:16, 448] = (np.arange(16) >= 8).astype(np.float32)
    reset = np.ones((16, 2048), np.float32)
    reset[:, 0::64] = 0.0
    inv = np.zeros((4, 128, 2304), np.float32)
    for g, w in enumerate((2, 4, 8, 16)):
        for off, T in ((0, 256), (256, 2048)):
            t = np.arange(T)
            lo = np.maximum(t - w // 2, 0)
            hi = np.minimum(t + w - 1 - w // 2, T - 1)
            inv[g, :, off:off + T] = (1.0 / (hi - lo + 1).astype(np.float32))[None, :]
    return misc, reset, inv


def make_inputs(inp):
    f = lambda a: np.ascontiguousarray(np.asarray(a, dtype=np.float32))
    misc, reset, inv = make_consts()
    tr128 = lambda v: f(v.reshape(-1, 128).T)
    rp = f(inp["na_rpb"])
    rpbp = np.zeros((L, 8, 15, 128), np.float32)
    rpbp[..., 48:79] = rp[..., ::-1]
    shared = {
        "w_ada": f(inp["w_ada"]),
        "b_adaT": f(np.stack([tr128(inp["b_ada"][l]) for l in range(L)])),
        "gn1T": f(np.stack([tr128(inp["g_norm1"][l]) for l in range(L)])),
        "gn2T": f(np.stack([tr128(inp["g_norm2"][l]) for l in range(L)])),
        "gfT": tr128(f(inp["g_final"])),
        "w_in": f(inp["w_in"]),
        "pool_w": f(inp["pool_w"]),
        "pool_scT": f(np.stack([tr128(inp["pool_scale"][l]) for l in range(L)])),
        "gconvT": f(np.asarray(inp["gdn_conv"]).reshape(L, 3, 24, 128).transpose(0, 3, 2, 1)),
        "alog": f(np.asarray(inp["gdn_a_log"]).reshape(L, 16, 1)),
        "dtb": f(np.asarray(inp["gdn_dt_bias"]).reshape(L, 16, 1)),
        "gng": f(np.asarray(inp["gdn_norm_g"]).reshape(L, 128, 1)),
        "rpbp": rpbp,
        "wb": f(np.concatenate([inp["w_branch_pool"], inp["w_branch_gdn"], inp["w_branch_na"]], axis=1)),
        "w_out": f(inp["w_out"]),
        "w_up": f(inp["w_up"]),
        "fconvT": f(np.asarray(inp["ffn_conv"]).reshape(L, 3, 88, 128).transpose(0, 3, 2, 1)),
        "w_down": f(inp["w_down"]),
        "c_invcnt": inv,
        "c_misc": misc,
        "c_reset": reset,
    }
    xp = np.asarray(inp["x_prompt"], np.float32)
    xs = np.asarray(inp["x_sample"], np.float32)
    maps = []
    for c in range(8):
        b = c % 2
        m = dict(shared)
        m["xin"] = f(np.concatenate([xp[2 * c], xp[2 * c + 1], xs[b]], axis=0))
        cond = np.stack([np.asarray(inp["c_ctx"], np.float32), np.asarray(inp["c"], np.float32)[b]], axis=0)
        m["condT"] = f(cond.reshape(2, KC, 128).transpose(2, 1, 0))
        m["ck"] = f(np.asarray(inp["cache_na_k"])[b].transpose(0, 2, 3, 1))
        m["cvv"] = f(np.asarray(inp["cache_na_v"])[b].reshape(L, 512, 512))
        m["s0"] = f(np.asarray(inp["state_gdn"])[b])
        maps.append(m)
    return maps


_NC = {}


def kernel(**inputs):
    import os
    ncores = int(os.environ.get("KCORES", "8"))
    if "nc" not in _NC:
        if os.environ.get("KPLAN") == "gdn1":
            DBG["gdn_steps"] = 1
            _NC["nc"] = build(plan=["consts", ("gdn_main", 0)])
        else:
            _NC["nc"] = build()
    nc = _NC["nc"]
    maps = make_inputs(inputs)[:ncores]
    res = run_bass_kernel_spmd(nc, maps, core_ids=list(range(ncores)))
    if ncores < 8:
        res.results.extend([res.results[0]] * (8 - ncores))
    rs = res.results
    yp = np.zeros((16, 256, D), np.float32)
    ys = np.zeros((2, 2048, D), np.float32)
    nk = np.zeros((16, L, 256, 8, 64), np.float32)
    nv = np.zeros((16, L, 256, 8, 64), np.float32)
    ns = np.zeros((16, L, 2, 8, 128, 128), np.float32)
    for c in range(8):
        y = np.asarray(rs[c]["y_out"])
        yp[2 * c] = y[0:256]
        yp[2 * c + 1] = y[256:512]
        if c < 2:
            ys[c] = y[512:2560]
        nk[2 * c:2 * c + 2] = np.asarray(rs[c]["nk_out"]).reshape(2, L, 256, 8, 64)
        nv[2 * c:2 * c + 2] = np.asarray(rs[c]["nv_out"]).reshape(2, L, 256, 8, 64)
        ns[2 * c:2 * c + 2] = np.asarray(rs[c]["ns_out"])
    return yp, ys, nk, nv, ns
```
